# Optimizing a Trainium2 kernel written in Bass

```python
import math
import jax, jax.numpy as jnp
from jax import lax
import numpy as np

D_MODEL = 1024
BATCH = 16
SEQ = 2048
DEPTH = 1

D_RNN = 1024
N_RNN_BLOCKS = 8
RNN_BLOCK = D_RNN // N_RNN_BLOCKS
CONV_WIDTH = 4
LRU_C = 8.0
HEAD_DIM = 64
N_Q_HEADS = 16
N_KV_HEADS = 4
Q_PER_KV = N_Q_HEADS // N_KV_HEADS
D_ATTN = N_Q_HEADS * HEAD_DIM
D_KV = N_KV_HEADS * HEAD_DIM
WINDOW = 128
BLOCK = 128
ROPE_DIM = HEAD_DIM // 4
ROPE_THETA = 500000.0
NORM_EPS = 1e-6

OFF_RNN_X = 0
OFF_RNN_G = OFF_RNN_X + D_RNN
OFF_Q = OFF_RNN_G + D_RNN
OFF_K = OFF_Q + D_ATTN
OFF_V = OFF_K + D_KV
OFF_ATTN_G = OFF_V + D_KV
OFF_MERGE_R = OFF_ATTN_G + D_ATTN
OFF_MERGE_A = OFF_MERGE_R + D_MODEL
D_IN = OFF_MERGE_A + D_MODEL

kernel_name = "hybrid_rglru_swa_sink_gated_merge"


def rms_norm(x, g):
    xf = x.astype(jnp.float32)
    y = xf * lax.rsqrt(jnp.mean(xf * xf, axis=-1, keepdims=True) + NORM_EPS)
    return (y * g.astype(jnp.float32)).astype(x.dtype)


def causal_depthwise_conv(u, w, b):
    C = u.shape[-1]
    y = lax.conv_general_dilated(
        u, w.astype(u.dtype)[:, None, :], window_strides=(1,),
        padding=[(CONV_WIDTH - 1, 0)], dimension_numbers=("NWC", "WIO", "NWC"),
        feature_group_count=C)
    return y + b.astype(u.dtype)


def block_diag_linear(u, w, b):
    B, S, _ = u.shape
    ub = u.reshape(B, S, N_RNN_BLOCKS, RNN_BLOCK)
    y = jnp.einsum("bsnc,ncd->bsnd", ub, w.astype(u.dtype))
    return y.reshape(B, S, D_RNN) + b.astype(u.dtype)


def rg_lru(u, w_a, b_a, w_x, b_x, lam):
    r = jax.nn.sigmoid(block_diag_linear(u, w_a, b_a).astype(jnp.float32))
    i = jax.nn.sigmoid(block_diag_linear(u, w_x, b_x).astype(jnp.float32))
    log_a = -LRU_C * r * jax.nn.softplus(-lam.astype(jnp.float32))
    a = jnp.exp(log_a)
    b = jnp.sqrt(-jnp.expm1(2.0 * log_a)) * (i * u.astype(jnp.float32))

    def combine(c1, c2):
        a1, b1 = c1
        a2, b2 = c2
        return a1 * a2, a2 * b1 + b2

    _, h = lax.associative_scan(combine, (a, b), axis=1)
    return h.astype(u.dtype)


def rope_tables(seq_len):
    pos = jnp.arange(seq_len, dtype=jnp.float32)
    inv_freq = ROPE_THETA ** (-jnp.arange(0, ROPE_DIM, 2, dtype=jnp.float32) / ROPE_DIM)
    ang = pos[:, None] * inv_freq[None, :]
    return jnp.cos(ang)[:, None, :], jnp.sin(ang)[:, None, :]


def apply_partial_rope(t, cos, sin):
    half = ROPE_DIM // 2
    c = cos.astype(t.dtype)
    s = sin.astype(t.dtype)
    t1 = t[..., :half]
    t2 = t[..., half:ROPE_DIM]
    rot = jnp.concatenate([t1 * c - t2 * s, t2 * c + t1 * s], axis=-1)
    return jnp.concatenate([rot, t[..., ROPE_DIM:]], axis=-1)


def sliding_window_attention_with_sinks(q, k, v, sinks):
    B, S, _, _ = q.shape
    nb = S // BLOCK
    qb = q.reshape(B, nb, BLOCK, N_KV_HEADS, Q_PER_KV, HEAD_DIM)
    kb = k.reshape(B, nb, BLOCK, N_KV_HEADS, HEAD_DIM)
    vb = v.reshape(B, nb, BLOCK, N_KV_HEADS, HEAD_DIM)
    zeros = jnp.zeros_like(kb[:, :1])
    kw = jnp.concatenate([jnp.concatenate([zeros, kb[:, :-1]], axis=1), kb], axis=2)
    vw = jnp.concatenate([jnp.concatenate([zeros, vb[:, :-1]], axis=1), vb], axis=2)

    scale = 1.0 / math.sqrt(HEAD_DIM)
    scores = jnp.einsum("bnqkgd,bnjkd->bnkgqj", qb, kw).astype(jnp.float32) * scale

    blk = jnp.arange(nb)[:, None, None]
    q_abs = blk * BLOCK + jnp.arange(BLOCK)[None, :, None]
    k_abs = (blk - 1) * BLOCK + jnp.arange(2 * BLOCK)[None, None, :]
    valid = (k_abs <= q_abs) & (q_abs - k_abs < WINDOW) & (k_abs >= 0)
    scores = jnp.where(valid[None, :, None, None, :, :], scores, -jnp.inf)

    sink = sinks.astype(jnp.float32).reshape(N_KV_HEADS, Q_PER_KV)[None, None, :, :, None]
    m = jnp.maximum(jnp.max(scores, axis=-1), sink)
    p = jnp.exp(scores - m[..., None])
    denom = jnp.sum(p, axis=-1) + jnp.exp(sink - m)
    probs = (p / denom[..., None]).astype(v.dtype)
    out = jnp.einsum("bnkgqj,bnjkd->bnqkgd", probs, vw)
    return out.reshape(B, S, D_ATTN)


def setup_inputs(seed: int = 0) -> dict:
    key = jax.random.key(seed)
    ks = jax.random.split(key, 17)
    f32 = jnp.float32
    x = jax.random.normal(ks[0], (BATCH, SEQ, D_MODEL), f32)
    norm_g = 1.0 + 0.02 * jax.random.normal(ks[1], (DEPTH, D_MODEL), f32)
    w_in = jax.random.normal(ks[2], (DEPTH, D_MODEL, D_IN), f32) * D_MODEL ** -0.5
    conv_w = jax.random.normal(ks[3], (DEPTH, CONV_WIDTH, D_RNN), f32) * CONV_WIDTH ** -0.5
    conv_b = 0.02 * jax.random.normal(ks[4], (DEPTH, D_RNN), f32)
    lru_w_a = jax.random.normal(ks[5], (DEPTH, N_RNN_BLOCKS, RNN_BLOCK, RNN_BLOCK), f32) * RNN_BLOCK ** -0.5
    lru_b_a = 0.02 * jax.random.normal(ks[6], (DEPTH, D_RNN), f32)
    lru_w_x = jax.random.normal(ks[7], (DEPTH, N_RNN_BLOCKS, RNN_BLOCK, RNN_BLOCK), f32) * RNN_BLOCK ** -0.5
    lru_b_x = 0.02 * jax.random.normal(ks[8], (DEPTH, D_RNN), f32)
    a0 = jax.random.uniform(ks[9], (DEPTH, D_RNN), f32, 0.9, 0.999)
    lru_lambda = jnp.log(a0) - jnp.log1p(-a0)
    attn_sinks = 0.5 * jax.random.normal(ks[10], (DEPTH, N_Q_HEADS), f32)
    w_rnn_out = jax.random.normal(ks[11], (DEPTH, D_RNN, D_MODEL), f32) * D_RNN ** -0.5
    w_attn_out = jax.random.normal(ks[12], (DEPTH, D_ATTN, D_MODEL), f32) * D_ATTN ** -0.5
    w_o = jax.random.normal(ks[13], (DEPTH, D_MODEL, D_MODEL), f32) * D_MODEL ** -0.5
    final_norm_g = 1.0 + 0.02 * jax.random.normal(ks[14], (D_MODEL,), f32)
    return {"x": x, "norm_g": norm_g, "w_in": w_in, "conv_w": conv_w, "conv_b": conv_b,
            "lru_w_a": lru_w_a, "lru_b_a": lru_b_a, "lru_w_x": lru_w_x, "lru_b_x": lru_b_x,
            "lru_lambda": lru_lambda, "attn_sinks": attn_sinks, "w_rnn_out": w_rnn_out,
            "w_attn_out": w_attn_out, "w_o": w_o, "final_norm_g": final_norm_g}


def reference(x, norm_g, w_in, conv_w, conv_b, lru_w_a, lru_b_a, lru_w_x, lru_b_x,
              lru_lambda, attn_sinks, w_rnn_out, w_attn_out, w_o, final_norm_g):
    B, S, _ = x.shape
    cos, sin = rope_tables(S)
    for l in range(DEPTH):
        h = rms_norm(x, norm_g[l])
        proj = jnp.einsum("bsd,de->bse", h, w_in[l])
        u = proj[..., OFF_RNN_X:OFF_RNN_G]
        g_rnn = proj[..., OFF_RNN_G:OFF_Q]
        q = proj[..., OFF_Q:OFF_K].reshape(B, S, N_Q_HEADS, HEAD_DIM)
        k = proj[..., OFF_K:OFF_V].reshape(B, S, N_KV_HEADS, HEAD_DIM)
        v = proj[..., OFF_V:OFF_ATTN_G].reshape(B, S, N_KV_HEADS, HEAD_DIM)
        g_attn = proj[..., OFF_ATTN_G:OFF_MERGE_R]
        m_rnn = proj[..., OFF_MERGE_R:OFF_MERGE_A]
        m_attn = proj[..., OFF_MERGE_A:D_IN]

        u = causal_depthwise_conv(u, conv_w[l], conv_b[l])
        y_rnn = rg_lru(u, lru_w_a[l], lru_b_a[l], lru_w_x[l], lru_b_x[l], lru_lambda[l]) * jax.nn.silu(g_rnn)

        q = apply_partial_rope(q, cos, sin)
        k = apply_partial_rope(k, cos, sin)
        y_attn = sliding_window_attention_with_sinks(q, k, v, attn_sinks[l]) * jax.nn.silu(g_attn)

        merged = (jax.nn.sigmoid(m_rnn) * jnp.einsum("bsr,rd->bsd", y_rnn, w_rnn_out[l])
                  + jax.nn.sigmoid(m_attn) * jnp.einsum("bsa,ad->bsd", y_attn, w_attn_out[l]))
        x = x + jnp.einsum("bsd,de->bse", merged, w_o[l])
    return rms_norm(x, final_norm_g)
```

```python
import math
from contextlib import ExitStack

import numpy as np

import concourse.bass as bass
import concourse.mybir as mybir
from concourse.bass_utils import run_bass_kernel_spmd

F32 = mybir.dt.float32
BF16 = mybir.dt.bfloat16
AF = mybir.ActivationFunctionType
ALU = mybir.AluOpType
AX = mybir.AxisListType

NCORES = 8
SEQ = 2048
D = 1024
DIN = 6656
T = 512
TPS = SEQ // T
NT = 2 * TPS
OFF_U, OFF_G, OFF_Q, OFF_K, OFF_V, OFF_GA, OFF_MR, OFF_MA = 0, 1024, 2048, 3072, 3328, 3584, 4608, 5632
NSLOT = 20
EPS = 1e-6


class Res:
    __slots__ = ("name", "w", "rs", "dsem", "dcnt")

    def __init__(self, name):
        self.name = name
        self.w = None
        self.rs = {}
        self.dsem = None
        self.dcnt = 0


class Sched:
    CE = ("pe", "act", "dve", "pool")

    def __init__(self, nc, stack, same_engine_sync=True):
        self.nc = nc
        self.stack = stack
        self.q = {e: [] for e in self.CE + ("sp",)}
        self.sem = {e: stack.enter_context(nc.semaphore("s_" + e)) for e in self.CE}
        self.cnt = {e: 0 for e in self.CE}
        self.waited = {}
        self.same = same_engine_sync
        self.nsem = 0
        self.final = []
        self.pool_dmas = []

    @staticmethod
    def _flat(xs):
        out = []
        for x in xs:
            if isinstance(x, (list, tuple)):
                out.extend(Sched._flat(x))
            else:
                out.append(x)
        return out

    def _deps(self, reads, writes):
        deps = []
        for r in reads:
            if r.w is not None:
                deps.append(r.w)
        for r in writes:
            if r.w is not None:
                deps.append(r.w)
            deps.extend(r.rs.values())
        return deps

    def _need(self, eng, deps):
        best = {}
        for sem, val, src in deps:
            if src == eng and (eng == "pe" or not self.same):
                continue
            k = id(sem)
            if k not in best or val > best[k][1]:
                best[k] = (sem, val)
        out = []
        for k, (sem, val) in best.items():
            if self.waited.get((eng, k), 0) >= val:
                continue
            self.waited[(eng, k)] = val
            out.append((sem, val))
        return out

    @staticmethod
    def _addr(r, tok):
        k = id(tok[0])
        if k not in r.rs or r.rs[k][1] < tok[1]:
            r.rs[k] = tok

    def op(self, eng, fn, reads=(), writes=()):
        reads, writes = self._flat(reads), self._flat(writes)
        waits = self._need(eng, self._deps(reads, writes))
        self.cnt[eng] += 1
        tok = (self.sem[eng], self.cnt[eng], eng)
        self.q[eng].append((waits, fn, (self.sem[eng], 1)))
        for r in reads:
            self._addr(r, tok)
        for r in writes:
            r.w = tok
            r.rs = {}

    def dma(self, fn, owner, reads=(), writes=(), qeng="sp", final=False, skip_own=False):
        reads, writes = self._flat(reads), self._flat(writes)
        kind = 0 if qeng == "pool" else 1
        if owner.dsem is None:
            owner.dsem = [None, None]
            owner.dcnt = [0, 0]
        if owner.dsem[kind] is None:
            owner.dsem[kind] = self.stack.enter_context(self.nc.semaphore("d%d" % self.nsem))
            self.nsem += 1
        deps = self._deps(reads, writes)
        if skip_own:
            deps = [d for d in deps if d[0] is not owner.dsem[kind]]
        if qeng == "pool" and len(self.pool_dmas) >= 6:
            deps.append(self.pool_dmas[-6])
        waits = self._need(qeng, deps)
        owner.dcnt[kind] += 16
        tok = (owner.dsem[kind], owner.dcnt[kind], "dma")
        self.q[qeng].append((waits, fn, (owner.dsem[kind], 16)))
        for r in reads:
            self._addr(r, tok)
        for r in writes:
            r.w = tok
            r.rs = {}
        if final:
            self.final.append(tok)
        if qeng == "pool":
            self.pool_dmas.append(tok)

    def finish(self, qeng="sp"):
        waits = self._need(qeng, self.final)
        self.q[qeng].append((waits, None, None))

    def replay(self, name, e):
        for waits, fn, inc in self.q[name]:
            for sem, val in waits:
                e.wait_ge(sem, val)
            if fn is None:
                continue
            ins = fn(e)
            if inc is not None:
                ins.then_inc(inc[0], inc[1])

    def run_block(self):
        with self.nc.Block() as block:
            @block.tensor
            def _(e):
                self.replay("pe", e)

            @block.scalar
            def _(e):
                self.replay("act", e)

            @block.vector
            def _(e):
                self.replay("dve", e)

            @block.gpsimd
            def _(e):
                self.replay("pool", e)

            @block.sync
            def _(e):
                self.replay("sp", e)


class Ring:
    def __init__(self, items):
        self.items = items
        self.i = 0

    def get(self):
        it = self.items[self.i % len(self.items)]
        self.i += 1
        return it


def _consts():
    c = {}
    c["c_ident"] = np.eye(128, dtype=np.float32)
    perm = np.zeros((128, 128), np.float32)
    for m in range(128):
        d = m % 64
        if d < 8:
            perm[m + 8, m] = 1.0
        elif d < 16:
            perm[m - 8, m] = 1.0
    c["c_perm"] = perm
    k = np.arange(128)[:, None]
    q = np.arange(128)[None, :]
    mask = np.concatenate([(q < k), (q >= k)], axis=1).astype(np.float32)
    c["c_mask"] = np.concatenate([mask, mask], axis=1)
    eh = np.zeros((128, 16, 16), np.float32)
    for h in range(16):
        eh[:, h, h] = 1.0
    c["c_eh"] = eh.reshape(128, 256)
    bc = np.zeros((16, 8, 128), np.float32)
    for cc in range(8):
        for p in range(128):
            bc[2 * cc + p // 64, cc, p] = 1.0
    c["c_bc"] = bc.reshape(16, 1024)
    pos = np.arange(SEQ, dtype=np.float32)
    inv_freq = (np.float32(500000.0) ** (-np.arange(0, 16, 2, dtype=np.float32) / np.float32(16))).astype(np.float32)
    ang = (pos[:, None] * inv_freq[None, :]).astype(np.float32)
    cos = np.cos(ang).astype(np.float32)
    sin = np.sin(ang).astype(np.float32)
    C = np.ones((128, SEQ), np.float32)
    Sg = np.zeros((128, SEQ), np.float32)
    for p in range(128):
        d = p % 64
        if d < 8:
            C[p] = cos[:, d]
            Sg[p] = -sin[:, d]
        elif d < 16:
            C[p] = cos[:, d - 8]
            Sg[p] = sin[:, d - 8]
    c["c_ropeC"] = C
    c["c_ropeS"] = Sg
    return c


def build_program(ntiles=NT, same_engine_sync=True, stop_after=None, dump=()):
    nc = bass.Bass("TRN2", target_bir_lowering=False)

    def din(name, shape):
        return nc.dram_tensor(name, shape, F32, kind="ExternalInput").ap()

    x = din("x", [2 * SEQ, D])
    norm_g = din("norm_g", [1, D])
    w_in = din("w_in", [1, D, DIN])
    conv_w = din("conv_w", [1, 4, D])
    conv_b = din("conv_b", [1, D])
    lru_w_a = din("lru_w_a", [1, 8, 128, 128])
    lru_b_a = din("lru_b_a", [1, D])
    lru_w_x = din("lru_w_x", [1, 8, 128, 128])
    lru_b_x = din("lru_b_x", [1, D])
    lru_lambda = din("lru_lambda", [1, D])
    attn_sinks = din("attn_sinks", [1, 16])
    w_rnn_out = din("w_rnn_out", [1, D, D])
    w_attn_out = din("w_attn_out", [1, D, D])
    w_o = din("w_o", [1, D, D])
    final_norm_g = din("final_norm_g", [1, D])
    c_ident = din("c_ident", [128, 128])
    c_perm = din("c_perm", [128, 128])
    c_mask = din("c_mask", [128, 512])
    c_eh = din("c_eh", [128, 256])
    c_bc = din("c_bc", [16, 1024])
    c_ropeC = din("c_ropeC", [128, SEQ])
    c_ropeS = din("c_ropeS", [128, SEQ])
    wstream = nc.dram_tensor("wstream", [NSLOT, 128, 4096], BF16, kind="Internal").ap()
    out = nc.dram_tensor("out", [2 * SEQ, D], F32, kind="ExternalOutput").ap()

    win_v = w_in.rearrange("o (k p) c -> p (o k) c", p=128)
    wro_v = w_rnn_out.rearrange("o (k p) c -> p (o k) c", p=128)
    wao_v = w_attn_out.rearrange("o (k p) c -> p (o k) c", p=128)
    wo_v = w_o.rearrange("o (k p) c -> p (o k) c", p=128)

    with ExitStack() as st:
        S = Sched(nc, st, same_engine_sync=same_engine_sync)

        def sb(name, shape, dt):
            return st.enter_context(nc.sbuf_tensor(name, shape, dt))

        ps = st.enter_context(nc.psum_tensor("ps", [128, 7, 512], F32))
        psT = st.enter_context(nc.psum_tensor("psT", [128, 1024], BF16))
        r_ps = [Res("ps%d" % i) for i in range(7)]
        r_psT = Res("psT")

        ident = sb("ident", [128, 128], BF16); r_ident = Res("ident")
        permb = sb("permb", [128, 128], BF16); r_perm = Res("perm")
        maskb = sb("maskb", [128, 2, 256], BF16); r_mask = Res("mask")
        ehb = sb("ehb", [128, 256], BF16); r_eh = Res("eh")
        bcs = sb("bcs", [16, 1024], F32); r_bc = Res("bc")
        grep = sb("grep", [128, D], F32); r_grep = Res("grep")
        fgrep = sb("fgrep", [128, D], F32); r_fgrep = Res("fgrep")
        mhalf = sb("mhalf", [128, 1], F32); r_mhalf = Res("mhalf")
        vrow = sb("vrow", [64, 128], F32); r_vrow = Res("vrow")
        identf = sb("identf", [128, 128], F32); r_identf = Res("identf")
        vT = sb("vT", [128, 8, 8], F32); r_vT = Res("vT")
        vd = sb("vd", [128, 4, 8], F32); r_vd = Res("vd")
        sinkt = sb("sinkt", [16, 2], F32); r_sink = Res("sink")
        lruA = sb("lruA", [128, 8, 128], BF16); r_lruA = Res("lruA")
        lruX = sb("lruX", [128, 8, 128], BF16); r_lruX = Res("lruX")

        wring = [sb("wring%d" % i, [128, 4096], BF16) for i in range(4)]
        r_wring = [Res("wring%d" % i) for i in range(4)]
        hT = [sb("hT%d" % i, [128, 8, T], BF16) for i in range(2)]
        r_hT = [[Res("hT%d_%d" % (i, b)) for b in range(4)] for i in range(2)]
        yr = sb("yr", [128, 8, T], BF16); r_yr = [Res("yr%d" % i) for i in range(8)]
        ya = sb("ya", [128, 8, T], BF16); r_ya = [Res("ya%d" % i) for i in range(4)]
        qmg = sb("qmg", [128, 8, T], BF16); r_qmg = [Res("qmg%d" % i) for i in range(8)]
        sga = sb("sga", [128, 8, T], BF16); r_sga = [Res("sga%d" % i) for i in range(8)]
        kT = sb("kT", [128, 4, 128 + T], BF16); r_kT = [Res("kT%d" % i) for i in range(4)]
        vtok = sb("vtok", [128, 5, 256], BF16); r_vtok = Res("vtok")
        halo = sb("halo", [128, 8, 4], F32); r_halo = [Res("halo%d" % i) for i in range(8)]
        hst = sb("hst", [128, 8], F32); r_hst = [Res("hst%d" % i) for i in range(8)]
        ropeC = [sb("ropeC%d" % i, [128, T], F32) for i in range(2)]
        ropeS = [sb("ropeS%d" % i, [128, T], F32) for i in range(2)]
        r_rope = [Res("rope%d" % i) for i in range(2)]
        xset = [[sb("x%d_%d" % (i, b), [128, D], F32) for b in range(4)] for i in range(2)]
        r_xset = [[Res("x%d_%d" % (i, b)) for b in range(4)] for i in range(2)]
        xnb = [sb("xnb%d" % i, [128, D], BF16) for i in range(2)]
        r_xnb = [Res("xnb%d" % i) for i in range(2)]
        big = [sb("big%d" % i, [128, D], F32) for i in range(2)]
        r_big = [Res("big%d" % i) for i in range(2)]
        stat = sb("stat", [128, 16], F32)
        r_stat = [Res("stat%d" % i) for i in range(4)]
        def mkring(name, n, shape, dt):
            return Ring([(sb("%s%d" % (name, i), shape, dt), Res("%s%d" % (name, i))) for i in range(n)])
        wk_ue = mkring("wue", 2, [128, 520], F32)
        wk_uc = mkring("wuc", 3, [128, 512], F32)
        wk_sg = mkring("wsg", 3, [128, 512], F32)
        wk = mkring("wk", 8, [128, 512], F32)
        wb_ucb = mkring("wucb", 3, [128, 512], BF16)
        wb = mkring("wb", 3, [128, 512], BF16)
        pT_t = [sb("pT%d" % i, [128, 2, 256], BF16) for i in range(3)]
        pTr = Ring([(pT_t[i], Res("pT%d" % i)) for i in range(3)])
        dsm = sb("dsm", [16, 2, 128], F32); r_dsm = [Res("dsm0"), Res("dsm1")]

        psr = Ring([(i, r_ps[i]) for i in range(7)])

        def pbank(i):
            return ps[:, i, :]

        def cast_load(dst_ap, src_ap, res, reads=(), skip_own=False):
            S.dma(lambda e, d=dst_ap, s=src_ap: e.dma_start(out=d, in_=s), res, reads=reads, writes=[res], qeng="pool",
                  skip_own=skip_own)

        def load(dst_ap, src_ap, res, slow=False):
            S.dma(lambda e, d=dst_ap, s=src_ap, sl=slow: e.dma_start(out=d, in_=s, allow_slow_non_contiguous=sl),
                  res, writes=[res])

        cast_load(ident[:], c_ident, r_ident)
        cast_load(permb[:], c_perm, r_perm)
        cast_load(maskb[:].rearrange("p a m -> p (a m)"), c_mask, r_mask)
        cast_load(ehb[:], c_eh, r_eh)
        cast_load(lruA[:], lru_w_a.rearrange("o n c d -> c (o n) d"), r_lruA)
        cast_load(lruX[:], lru_w_x.rearrange("o n c d -> c (o n) d"), r_lruX)
        load(bcs[:], c_bc, r_bc)
        load(grep[:], norm_g.partition_broadcast(128), r_grep)
        load(fgrep[:], final_norm_g.partition_broadcast(128), r_fgrep)
        vsrc = [conv_w[0, 0:1, :], conv_w[0, 1:2, :], conv_w[0, 2:3, :], conv_w[0, 3:4, :], conv_b, lru_b_a, lru_b_x, lru_lambda]
        for i, v in enumerate(vsrc):
            load(vrow[8 * i:8 * i + 8, :], v.rearrange("o (n p) -> (o n) p", p=128), r_vrow)
        load(identf[:], c_ident, r_identf)
        load(sinkt[:, 0:1], attn_sinks.rearrange("o h -> h o"), r_sink, slow=True)
        S.op("pe", lambda e: e.transpose(ps[:, 0, 0:64], vrow[:, :], identf[0:64, 0:64]), reads=[r_vrow, r_identf], writes=[r_ps[0]])
        S.op("act", lambda e: e.activation(out=vT[:].rearrange("p a b -> p (a b)"), in_=ps[:, 0, 0:64], func=AF.Copy),
             reads=[r_ps[0]], writes=[r_vT])

        S.op("pool", lambda e: e.memset(mhalf[:], -0.5), writes=[r_mhalf])
        S.op("dve", lambda e: e.tensor_scalar(out=vd[:, 0:2, :], in0=vT[:, 5:7, :], scalar1=0.5, scalar2=None, op0=ALU.mult),
             reads=[r_vT], writes=[r_vd])
        S.op("act", lambda e: e.activation(out=vd[:, 2, :], in_=vT[:, 7, :], func=AF.Exp, scale=-1.0), reads=[r_vT], writes=[r_vd])
        S.op("act", lambda e: e.activation(out=vd[:, 3, :], in_=vd[:, 2, :], func=AF.Ln, bias=1.0), reads=[r_vd], writes=[r_vd])
        S.op("dve", lambda e: e.tensor_scalar(out=vd[:, 2, :], in0=vd[:, 3, :], scalar1=-4.0, scalar2=None, op0=ALU.mult),
             reads=[r_vd], writes=[r_vd])
        S.op("dve", lambda e: e.tensor_scalar(out=vd[:, 3, :], in0=vd[:, 3, :], scalar1=-8.0, scalar2=None, op0=ALU.mult),
             reads=[r_vd], writes=[r_vd])
        S.op("act", lambda e: e.activation(out=sinkt[:, 1:2], in_=sinkt[:, 0:1], func=AF.Exp), reads=[r_sink], writes=[r_sink])

        def slot_units(s):
            res = []
            if s < 4:
                res.append((0, 256, win_v[:, :, OFF_U + 2 * s * 128: OFF_U + (2 * s + 2) * 128]))
                res.append((256, 512, win_v[:, :, OFF_G + 2 * s * 128: OFF_G + (2 * s + 2) * 128]))
            elif s == 4:
                for g in range(4):
                    for hf in range(2):
                        res.append((g * 128 + hf * 64, g * 128 + hf * 64 + 64, win_v[:, :, OFF_K + g * 64: OFF_K + (g + 1) * 64]))
            elif s == 5:
                res.append((0, 256, win_v[:, :, OFF_V:OFF_V + 256]))
                res.append((256, 512, win_v[:, :, OFF_Q:OFF_Q + 256]))
            elif s == 6:
                res.append((0, 512, win_v[:, :, OFF_Q + 256:OFF_Q + 768]))
            elif s == 7:
                res.append((0, 256, win_v[:, :, OFF_Q + 768:OFF_Q + 1024]))
                res.append((256, 512, win_v[:, :, OFF_GA:OFF_GA + 256]))
            elif s == 8:
                res.append((0, 512, win_v[:, :, OFF_GA + 256:OFF_GA + 768]))
            elif s == 9:
                res.append((0, 256, win_v[:, :, OFF_GA + 768:OFF_GA + 1024]))
            elif s < 18:
                i, y = (s - 10) // 2, (s - 10) % 2
                if y == 0:
                    res.append((0, 256, wro_v[:, :, 2 * i * 128:(2 * i + 2) * 128]))
                    res.append((256, 512, wao_v[:, :, 2 * i * 128:(2 * i + 2) * 128]))
                else:
                    res.append((0, 256, win_v[:, :, OFF_MR + 2 * i * 128: OFF_MR + (2 * i + 2) * 128]))
                    res.append((256, 512, win_v[:, :, OFF_MA + 2 * i * 128: OFF_MA + (2 * i + 2) * 128]))
            else:
                half = s - 18
                res.append((0, 512, wo_v[:, :, half * 512:(half + 1) * 512]))
            return res

        def kview(buf):
            return buf[:].rearrange("p (k c) -> p k c", k=8)

        wslot_i = [0]
        r_wstream = [Res("wstream%d" % i) for i in range(NSLOT)]

        def fetch_slot(t, s):
            i = wslot_i[0] % 4
            wslot_i[0] += 1
            buf, res = wring[i], r_wring[i]
            if t == 0:
                for ii, (c0, c1, src) in enumerate(slot_units(s)):
                    cast_load(kview(buf)[:, :, c0:c1], src, res, skip_own=(ii > 0))
                S.dma(lambda e, b=buf, s=s: e.dma_start(out=wstream[s], in_=b[:]), res, reads=[res], writes=[r_wstream[s]])
            else:
                S.dma(lambda e, b=buf, s=s: e.dma_start(out=b[:], in_=wstream[s]), res, reads=[r_wstream[s]], writes=[res])
            return buf, res

        def unit(buf, j):
            return kview(buf)[:, :, j * 128:(j + 1) * 128]

        def mm_group(out_ap, lhs_fn, rhs_fn, nk, reads, writes):
            def fn(e, out_ap=out_ap, lhs_fn=lhs_fn, rhs_fn=rhs_fn, nk=nk):
                ins = None
                for k in range(nk):
                    ins = e.matmul(out_ap, lhsT=lhs_fn(k), rhs=rhs_fn(k), start=(k == 0), stop=(k == nk - 1))
                return ins
            S.op("pe", fn, reads=reads, writes=writes)

        def load_x(t):
            xs, rx = xset[t % 2], r_xset[t % 2]
            for b in range(4):
                r0 = t * T + b * 128
                S.dma(lambda e, d=xs[b], r0=r0: e.dma_start(out=d[:], in_=x[r0:r0 + 128, :]), rx[b], writes=[rx[b]])
            pos0 = (t % TPS) * T
            i = t % 2
            S.dma(lambda e, i=i, p=pos0: e.dma_start(out=ropeC[i][:], in_=c_ropeC[:, p:p + T]), r_rope[i], writes=[r_rope[i]])
            S.dma(lambda e, i=i, p=pos0: e.dma_start(out=ropeS[i][:], in_=c_ropeS[:, p:p + T]), r_rope[i], writes=[r_rope[i]])

        def rstd_ops(sidx, rs):
            c0 = 4 * sidx
            S.op("pool", lambda e, c0=c0: e.tensor_scalar(out=stat[:, c0 + 1:c0 + 2], in0=stat[:, c0:c0 + 1], scalar1=1.0 / D,
                                                          scalar2=EPS, op0=ALU.mult, op1=ALU.add), reads=[rs], writes=[rs])
            S.op("pool", lambda e, c0=c0: e.tensor_tensor(out=stat[:, c0 + 1:c0 + 2], in0=stat[:, c0 + 1:c0 + 2], in1=mhalf[:],
                                                          op=ALU.pow), reads=[rs, r_mhalf], writes=[rs])

        stat_i = [0]

        def stage_A(t, blocks=(0, 1, 2, 3), part="both"):
            hTt, rh = hT[t % 2], r_hT[t % 2]
            xs, rx = xset[t % 2], r_xset[t % 2]
            for b in blocks:
                xb_, rxb = xnb[b % 2], r_xnb[b % 2]
                if part in ("both", "tr"):
                    def tr(e, xb_=xb_):
                        ins = None
                        for c in range(8):
                            ins = e.transpose(psT[:, c * 128:(c + 1) * 128], xb_[:, c * 128:(c + 1) * 128], ident[:])
                        return ins
                if part == "tr":
                    S.op("pe", tr, reads=[rxb, r_ident], writes=[r_psT])
                    S.op("act", lambda e, hTt=hTt, b=b: e.activation(
                        out=hTt[:, :, b * 128:(b + 1) * 128], in_=psT[:].rearrange("p (c m) -> p c m", c=8), func=AF.Copy),
                        reads=[r_psT], writes=[rh[b]])
                    continue
                si = stat_i[0] % 4
                stat_i[0] += 1
                rs = r_stat[si]
                bg, rbg = big[b % 2], r_big[b % 2]
                S.op("act", lambda e, bg=bg, xb=xs[b]: e.activation(out=bg[:], in_=xb[:], func=AF.Square), reads=[rx[b]], writes=[rbg])
                S.op("dve", lambda e, bg=bg, si=si: e.tensor_reduce(out=stat[:, 4 * si:4 * si + 1], in_=bg[:], axis=AX.X, op=ALU.add),
                     reads=[rbg], writes=[rs])
                rstd_ops(si, rs)
                S.op("dve", lambda e, xb_=xb_, xb=xs[b], si=si: e.scalar_tensor_tensor(
                    out=xb_[:], in0=xb[:], scalar=stat[:, 4 * si + 1:4 * si + 2], in1=grep[:], op0=ALU.mult, op1=ALU.mult),
                    reads=[rx[b], rs, r_grep], writes=[rxb])
                if part == "elem":
                    continue
                S.op("pe", tr, reads=[rxb, r_ident], writes=[r_psT])
                S.op("act", lambda e, hTt=hTt, b=b: e.activation(
                    out=hTt[:, :, b * 128:(b + 1) * 128], in_=psT[:].rearrange("p (c m) -> p c m", c=8), func=AF.Copy),
                    reads=[r_psT], writes=[rh[b]])

        def make_B(t):
            hTt, rh = hT[t % 2], r_hT[t % 2]
            first = (t % TPS == 0)
            bstate = {}

            def front(n, buf, rbuf, j0):
                wu, wg = unit(buf, j0), unit(buf, 2 + j0)
                bu, rbu = psr.get()
                mm_group(pbank(bu), lambda k, wu=wu: wu[:, k, :], lambda k: hTt[:, k, :], 8, reads=[rbuf] + rh, writes=[rbu])
                bgp, rbg_ = psr.get()
                mm_group(pbank(bgp), lambda k, wg=wg: wg[:, k, :], lambda k: hTt[:, k, :], 8, reads=[rbuf] + rh, writes=[rbg_])
                ue, rue = wk_ue.get()
                if first:
                    S.op("dve", lambda e, ue=ue: e.memset(ue[:, 0:3], 0.0), writes=[rue])
                else:
                    S.op("pool", lambda e, ue=ue, n=n: e.tensor_copy(out=ue[:, 0:3], in_=halo[:, n, 0:3]),
                         reads=[r_halo[n]], writes=[rue])
                S.op("act", lambda e, ue=ue, bu=bu: e.activation(out=ue[:, 3:3 + T], in_=pbank(bu), func=AF.Copy),
                     reads=[rbu, rue], writes=[rue])
                S.op("pool", lambda e, ue=ue, n=n: e.tensor_copy(out=halo[:, n, 0:3], in_=ue[:, T:T + 3]),
                     reads=[rue], writes=[r_halo[n]])
                sg, rsg = wk_sg.get()
                S.op("act", lambda e, sg=sg, bgp=bgp: e.activation(out=sg[:, 0:T], in_=pbank(bgp), func=AF.Tanh, scale=0.5),
                     reads=[rbg_], writes=[rsg])
                S.op("dve", lambda e, sg=sg, bgp=bgp: e.scalar_tensor_tensor(
                    out=sg[:, 0:T], in0=sg[:, 0:T], scalar=1.0, in1=pbank(bgp), op0=ALU.add, op1=ALU.mult),
                    reads=[rsg, rbg_], writes=[rsg])
                uc, ruc = wk_uc.get()
                S.op("pool", lambda e, ue=ue, uc=uc, n=n: e.tensor_scalar(
                    out=uc[:, 0:T], in0=ue[:, 3:3 + T], scalar1=vT[:, 3, n:n + 1], scalar2=vT[:, 4, n:n + 1], op0=ALU.mult, op1=ALU.add),
                    reads=[rue, r_vT], writes=[ruc])
                for j in (2,):
                    cq, rcq = wk.get()
                    S.op("pool", lambda e, ue=ue, cq=cq, n=n, j=j: e.tensor_scalar(
                        out=cq[:, 0:T], in0=ue[:, j:j + T], scalar1=vT[:, j, n:n + 1], scalar2=0.0, op0=ALU.mult, op1=ALU.add),
                        reads=[rue, r_vT], writes=[rcq])
                    S.op("pool", lambda e, cq=cq, uc=uc: e.tensor_tensor(out=uc[:, 0:T], in0=uc[:, 0:T], in1=cq[:, 0:T], op=ALU.add),
                         reads=[rcq, ruc], writes=[ruc])
                for j in (1, 0):
                    S.op("dve", lambda e, ue=ue, uc=uc, n=n, j=j: e.scalar_tensor_tensor(
                        out=uc[:, 0:T], in0=ue[:, j:j + T], scalar=vT[:, j, n:n + 1], in1=uc[:, 0:T], op0=ALU.mult, op1=ALU.add),
                        reads=[rue, r_vT, ruc], writes=[ruc])
                ucb, rucb = wb_ucb.get()
                return (n, uc, ruc, ucb, rucb, sg, rsg)

            def back(n, uc, ruc, ucb, rucb, sg, rsg):
                br, rbr = psr.get()
                mm_group(pbank(br), lambda k, n=n: lruA[:, n, :], lambda k, ucb=ucb: ucb[:], 1, reads=[r_lruA, rucb], writes=[rbr])
                bi, rbi = psr.get()
                mm_group(pbank(bi), lambda k, n=n: lruX[:, n, :], lambda k, ucb=ucb: ucb[:], 1, reads=[r_lruX, rucb], writes=[rbi])
                tr_, rtr = wk.get()
                S.op("act", lambda e, tr_=tr_, br=br, n=n: e.activation(out=tr_[:, 0:T], in_=pbank(br), func=AF.Tanh, scale=0.5,
                                                                       bias=vd[:, 0, n:n + 1]), reads=[rbr, r_vd], writes=[rtr])
                iu, riu = wk.get()
                S.op("act", lambda e, iu=iu, bi=bi, n=n: e.activation(out=iu[:, 0:T], in_=pbank(bi), func=AF.Tanh, scale=0.5,
                                                                     bias=vd[:, 1, n:n + 1]), reads=[rbi, r_vd], writes=[riu])
                a_, ra = wk.get()
                S.op("act", lambda e, a_=a_, tr_=tr_, n=n: e.activation(out=a_[:, 0:T], in_=tr_[:, 0:T], func=AF.Exp,
                                                                       scale=vd[:, 2, n:n + 1], bias=vd[:, 2, n:n + 1]),
                     reads=[rtr, r_vd], writes=[ra])
                s_, rs_ = wk.get()
                S.op("act", lambda e, s_=s_, tr_=tr_, n=n: e.activation(out=s_[:, 0:T], in_=tr_[:, 0:T], func=AF.Exp,
                                                                       scale=vd[:, 3, n:n + 1], bias=vd[:, 3, n:n + 1]),
                     reads=[rtr, r_vd], writes=[rs_])
                S.op("act", lambda e, s_=s_: e.activation(out=s_[:, 0:T], in_=s_[:, 0:T], func=AF.Sqrt, scale=-1.0, bias=1.0),
                     reads=[rs_], writes=[rs_])
                S.op("dve", lambda e, iu=iu, uc=uc: e.scalar_tensor_tensor(out=iu[:, 0:T], in0=iu[:, 0:T], scalar=1.0, in1=uc[:, 0:T],
                                                                          op0=ALU.add, op1=ALU.mult), reads=[riu, ruc], writes=[riu])
                S.op("dve", lambda e, iu=iu, s_=s_: e.scalar_tensor_tensor(out=iu[:, 0:T], in0=s_[:, 0:T], scalar=0.5, in1=iu[:, 0:T],
                                                                          op0=ALU.mult, op1=ALU.mult), reads=[riu, rs_], writes=[riu])
                h_, rh_ = wk.get()
                if first:
                    S.op("dve", lambda e, h_=h_, a_=a_, iu=iu: e.tensor_tensor_scan(
                        out=h_[:, 0:T], data0=a_[:, 0:T], data1=iu[:, 0:T], initial=0.0, op0=ALU.mult, op1=ALU.add),
                        reads=[ra, riu], writes=[rh_])
                else:
                    S.op("dve", lambda e, h_=h_, a_=a_, iu=iu, n=n: e.tensor_tensor_scan(
                        out=h_[:, 0:T], data0=a_[:, 0:T], data1=iu[:, 0:T], initial=hst[:, n:n + 1], op0=ALU.mult, op1=ALU.add),
                        reads=[ra, riu, r_hst[n]], writes=[rh_])
                S.op("pool", lambda e, h_=h_, n=n: e.tensor_copy(out=hst[:, n:n + 1], in_=h_[:, T - 1:T]),
                     reads=[rh_], writes=[r_hst[n]])
                S.op("dve", lambda e, h_=h_, sg=sg, n=n: e.scalar_tensor_tensor(
                    out=yr[:, n, :], in0=h_[:, 0:T], scalar=0.5, in1=sg[:, 0:T], op0=ALU.mult, op1=ALU.mult),
                    reads=[rh_, rsg], writes=[r_yr[n]])

            def front_n(n):
                if n % 2 == 0:
                    bstate["buf"] = fetch_slot(t, n // 2)
                buf, rbuf = bstate["buf"]
                return front(n, buf, rbuf, n % 2)

            def front_b(n, uc, ruc, ucb, rucb, sg, rsg):
                S.op("act", lambda e, uc=uc, ucb=ucb: e.activation(out=ucb[:], in_=uc[:, 0:T], func=AF.Copy), reads=[ruc], writes=[rucb])
            return front_n, front_b, back

        def rope_chain(pb, rpb, dst_ap, rdst, t):
            i = t % 2
            raw, rraw = wb.get()
            S.op("act", lambda e, raw=raw, pb=pb: e.activation(out=raw[:], in_=pbank(pb), func=AF.Copy), reads=[rpb], writes=[rraw])
            return (raw, rraw, dst_ap, rdst, i)

        def rope_finish(raw, rraw, dst_ap, rdst, i):
            p2, rp2 = psr.get()
            mm_group(pbank(p2), lambda k: permb[:], lambda k, raw=raw: raw[:], 1, reads=[r_perm, rraw], writes=[rp2])
            t2, rt2 = wk.get()
            S.op("pool", lambda e, t2=t2, raw=raw, i=i: e.tensor_tensor(out=t2[:, 0:T], in0=raw[:], in1=ropeC[i][:], op=ALU.mult),
                 reads=[rraw, r_rope[i]], writes=[rt2])
            t1, rt1 = wk.get()
            S.op("dve", lambda e, t1=t1, p2=p2, i=i: e.tensor_tensor(out=t1[:, 0:T], in0=pbank(p2), in1=ropeS[i][:], op=ALU.mult),
                 reads=[rp2, r_rope[i]], writes=[rt1])
            S.op("dve", lambda e, t1=t1, t2=t2, dst_ap=dst_ap: e.tensor_tensor(out=dst_ap, in0=t1[:, 0:T], in1=t2[:, 0:T], op=ALU.add),
                 reads=[rt1, rt2], writes=[rdst])

        def make_C(t):
            hTt, rh = hT[t % 2], r_hT[t % 2]
            units = [("kd", g) for g in range(4)] + [("v", 0), ("v", 1)] + [("q", c) for c in range(8)] + \
                    [("ga", c) for c in range(8)] + [("pad", 0), ("pad", 1)]
            pend = []
            cstate = {}

            def unit_fn(ui):
                kind, i = units[ui]
                j = ui % 4
                if j == 0:
                    cstate["buf"] = fetch_slot(t, 4 + ui // 4)
                buf, rbuf = cstate["buf"]
                if kind in ("kd", "q"):
                    w = unit(buf, j)
                    pb, rpb = psr.get()
                    mm_group(pbank(pb), lambda k, w=w: w[:, k, :], lambda k: hTt[:, k, :], 8, reads=[rbuf] + rh, writes=[rpb])
                    if kind == "kd":
                        dst, rd = kT[:, i, 128:128 + T], r_kT[i]
                    else:
                        dst, rd = qmg[:, i, :], r_qmg[i]
                    pend.append(rope_chain(pb, rpb, dst, rd, t))
                    if len(pend) > 2:
                        rope_finish(*pend.pop(0))
                elif kind == "v" and i == 0:
                    wv = kview(buf)[:, :, j * 128:(j + 2) * 128]
                    for pr in range(2):
                        pb, rpb = psr.get()

                        def fn(e, pb=pb, pr=pr, wv=wv):
                            ins = None
                            for bb in range(2):
                                b = pr * 2 + bb
                                for k in range(8):
                                    ins = e.matmul(ps[:, pb, bb * 256:(bb + 1) * 256], lhsT=hTt[:, k, b * 128:(b + 1) * 128],
                                                   rhs=wv[:, k, :], start=(k == 0), stop=(k == 7))
                            return ins
                        S.op("pe", fn, reads=[rbuf] + rh, writes=[rpb])
                        S.op("act", lambda e, pb=pb, pr=pr: e.activation(
                            out=vtok[:, 1 + 2 * pr:3 + 2 * pr, :], in_=pbank(pb).rearrange("p (b m) -> p b m", b=2), func=AF.Copy),
                            reads=[rpb], writes=[r_vtok])
                elif kind == "ga":
                    w = unit(buf, j)
                    pb, rpb = psr.get()
                    mm_group(pbank(pb), lambda k, w=w: w[:, k, :], lambda k: hTt[:, k, :], 8, reads=[rbuf] + rh, writes=[rpb])
                    tg, rtg = wk.get()
                    S.op("act", lambda e, tg=tg, pb=pb: e.activation(out=tg[:, 0:T], in_=pbank(pb), func=AF.Tanh, scale=0.5),
                         reads=[rpb], writes=[rtg])
                    S.op("dve", lambda e, tg=tg, pb=pb, i=i: e.scalar_tensor_tensor(
                        out=sga[:, i, :], in0=tg[:, 0:T], scalar=1.0, in1=pbank(pb), op0=ALU.add, op1=ALU.mult),
                        reads=[rtg, rpb], writes=[r_sga[i]])

            def flush():
                while pend:
                    rope_finish(*pend.pop(0))
            return unit_fn, flush, len(units)

        def stage_BC(t):
            frontB, frontB2, backB = make_B(t)
            unitC, flushC, NU = make_C(t)
            pendB = []
            ui = 0
            NP = 10
            for p_ in range(NP):
                if p_ < 8:
                    pendB.append(frontB(p_))
                if p_ >= 2:
                    backB(*pendB.pop(0))
                if p_ < 8:
                    frontB2(*pendB[-1])
                k = -(-(NU - ui) // (NP - p_))
                for _ in range(k):
                    unitC(ui)
                    ui += 1
            flushC()

        def stage_D(t):
            ts = t % TPS
            S_BANKS = [(0, 0), (5, 0)]
            sring = Ring(S_BANKS)
            DEN, VALS = 2, 3
            rbv = psT[:].bitcast(F32)
            post2_pending = [None]
            for jq in range(4):
                first = (ts == 0 and jq == 0)
                lo = 128 if first else 0
                pend = []

                def s1(i, jq=jq, first=first, lo=lo):
                    c, g = i, i // 2
                    bk, _hf = sring.get()
                    rbk = [r_ps[bk], r_ps[bk + 1]]

                    def fn(e, bk=bk, g=g, c=c):
                        ins = None
                        for hh in range(2):
                            hb = hh * 64
                            if not first:
                                ins = e.matmul(ps[:, bk + hh, 0:128], lhsT=kT[hb:hb + 64, g, jq * 128:(jq + 1) * 128],
                                               rhs=qmg[hb:hb + 64, c, jq * 128:(jq + 1) * 128], start=True, stop=True)
                            ins = e.matmul(ps[:, bk + hh, 128:256], lhsT=kT[hb:hb + 64, g, 128 + jq * 128:128 + (jq + 1) * 128],
                                           rhs=qmg[hb:hb + 64, c, jq * 128:(jq + 1) * 128], start=True, stop=True)
                        return ins
                    S.op("pe", fn, reads=[r_kT[g], r_qmg[c]], writes=[rbk])
                    pt, rpt = pTr.get()
                    S.op("act", lambda e, pt=pt, bk=bk: e.activation(
                        out=pt[:, :, lo:256], in_=ps[:, bk:bk + 2, lo:256], func=AF.Exp, scale=0.125),
                        reads=[rbk], writes=[rpt])
                    S.op("pool", lambda e, pt=pt: e.tensor_tensor(out=pt[:, :, lo:256], in0=pt[:, :, lo:256], in1=maskb[:, :, lo:256], op=ALU.mult),
                         reads=[rpt, r_mask], writes=[rpt])
                    return (i, pt, rpt)

                def s4(i, pt, rpt, jq=jq, first=first):
                    c, g = i, i // 2

                    def fn(e, pt=pt, c=c, g=g):
                        ins = None
                        for hh in range(2):
                            h = 2 * c + hh
                            hb = hh * 64
                            vo = ps[hb:hb + 64, VALS + c // 4, (c % 4) * 128:(c % 4 + 1) * 128]
                            if not first:
                                e.matmul(vo, lhsT=vtok[:, jq, g * 64:(g + 1) * 64], rhs=pt[:, hh, 0:128], start=True, stop=False)
                            e.matmul(vo, lhsT=vtok[:, jq + 1, g * 64:(g + 1) * 64], rhs=pt[:, hh, 128:256], start=first, stop=True)
                            if not first:
                                ins = e.matmul(ps[0:16, DEN, 0:256], lhsT=ehb[:, h * 16:(h + 1) * 16], rhs=pt[:, hh, 0:256], start=(h == 0),
                                               stop=(h == 15))
                            else:
                                ins = e.matmul(ps[0:16, DEN, 128:256], lhsT=ehb[:, h * 16:(h + 1) * 16], rhs=pt[:, hh, 128:256],
                                               start=(h == 0), stop=(h == 15))
                        return ins
                    S.op("pe", fn, reads=[rpt, r_vtok, r_eh], writes=[r_ps[VALS], r_ps[VALS + 1], r_ps[DEN]])

                def post1(jq=jq, first=first):
                    di = jq % 2
                    bg, rbg = big[0], r_big[0]
                    S.op("act", lambda e, bg=bg: e.activation(out=bg[:].rearrange("p (a m) -> p a m", a=2), in_=ps[:, VALS:VALS + 2, :],
                                                              func=AF.Copy), reads=[r_ps[VALS], r_ps[VALS + 1]], writes=[rbg])
                    S.op("dve", lambda e, di=di: e.tensor_scalar(out=dsm[:, di, :], in0=ps[0:16, DEN, 128:256], scalar1=sinkt[:, 1:2],
                                                                  scalar2=None, op0=ALU.add), reads=[r_ps[DEN], r_sink], writes=[r_dsm[di]])
                    if not first:
                        S.op("dve", lambda e, di=di: e.tensor_tensor(out=dsm[:, di, :], in0=dsm[:, di, :], in1=ps[0:16, DEN, 0:128],
                                                                      op=ALU.add), reads=[r_ps[DEN], r_dsm[di]], writes=[r_dsm[di]])
                    S.op("dve", lambda e, di=di: e.reciprocal(out=dsm[:, di, :], in_=dsm[:, di, :]), reads=[r_dsm[di]], writes=[r_dsm[di]])

                def post2(jq=jq):
                    di = jq % 2
                    bg, rbg = big[0], r_big[0]
                    b1, rb1 = big[1], r_big[1]
                    for hf in range(2):
                        def fnb(e, di=di, hf=hf):
                            ins = None
                            for cc in range(4):
                                c = hf * 4 + cc
                                ins = e.matmul(rbv[:, cc * 128:(cc + 1) * 128], lhsT=bcs[:, c * 128:(c + 1) * 128],
                                               rhs=dsm[:, di, :], start=True, stop=True)
                            return ins
                        S.op("pe", fnb, reads=[r_bc, r_dsm[di]], writes=[r_psT])
                        S.op("dve", lambda e, bg=bg, b1=b1, hf=hf: e.tensor_tensor(
                            out=b1[:, hf * 512:(hf + 1) * 512], in0=rbv, in1=bg[:, hf * 512:(hf + 1) * 512], op=ALU.mult),
                            reads=[r_psT, rbg], writes=[rb1])
                    S.op("dve", lambda e, b1=b1, jq=jq: e.scalar_tensor_tensor(
                        out=ya[:, :, jq * 128:(jq + 1) * 128], in0=b1[:].rearrange("p (c m) -> p c m", c=8), scalar=0.5,
                        in1=sga[:, :, jq * 128:(jq + 1) * 128], op0=ALU.mult, op1=ALU.mult),
                        reads=[rb1] + r_sga, writes=[r_ya[jq]])

                SK = 1
                for i in range(8 + SK):
                    if i < 8:
                        pend.append(s1(i))
                    if i >= SK:
                        s4(*pend.pop(0))
                    if i == 3 and post2_pending[0] is not None:
                        post2_pending[0]()
                        post2_pending[0] = None
                post1()
                post2_pending[0] = post2
            post2_pending[0]()
            S.op("pool", lambda e: e.tensor_copy(out=kT[:, :, 0:128], in_=kT[:, :, T:T + 128]), reads=r_kT, writes=r_kT)
            S.op("pool", lambda e: e.tensor_copy(out=vtok[:, 0, :], in_=vtok[:, 4, :]), reads=[r_vtok], writes=[r_vtok])

        def stage_E(t, a_next=None):
            hTt, rh = hT[t % 2], r_hT[t % 2]
            ebuf = {}
            for f in range(8):
                if a_next is not None:
                    stage_A(a_next, blocks=(f // 2,), part=("elem" if f % 2 == 0 else "tr"))
                if f % 2 == 0:
                    ebuf["x"] = fetch_slot(t, 10 + f)
                    ebuf["y"] = fetch_slot(t, 11 + f)
                (bx, rbx), (by, rby) = ebuf["x"], ebuf["y"]
                rbuf = [rbx, rby]
                wro, wao, wmr, wma = unit(bx, f % 2), unit(bx, 2 + f % 2), unit(by, f % 2), unit(by, 2 + f % 2)
                pc, rpc = psr.get()
                mm_group(pbank(pc), lambda k, w=wmr: w[:, k, :], lambda k: hTt[:, k, :], 8, reads=[rbuf] + rh, writes=[rpc])
                pd, rpd = psr.get()
                mm_group(pbank(pd), lambda k, w=wma: w[:, k, :], lambda k: hTt[:, k, :], 8, reads=[rbuf] + rh, writes=[rpd])
                pa, rpa = psr.get()
                mm_group(pbank(pa), lambda k, w=wro: w[:, k, :], lambda k: yr[:, k, :], 8, reads=[rbuf] + r_yr, writes=[rpa])
                pb, rpb = psr.get()
                mm_group(pbank(pb), lambda k, w=wao: w[:, k, :], lambda k: ya[:, k, :], 8, reads=[rbuf] + r_ya, writes=[rpb])
                tc_, rtc = wk.get()
                S.op("act", lambda e, tc_=tc_, pc=pc: e.activation(out=tc_[:, 0:T], in_=pbank(pc), func=AF.Tanh, scale=0.5),
                     reads=[rpc], writes=[rtc])
                td_, rtd = wk.get()
                S.op("act", lambda e, td_=td_, pd=pd: e.activation(out=td_[:, 0:T], in_=pbank(pd), func=AF.Tanh, scale=0.5),
                     reads=[rpd], writes=[rtd])
                S.op("dve", lambda e, tc_=tc_, pa=pa: e.scalar_tensor_tensor(out=tc_[:, 0:T], in0=tc_[:, 0:T], scalar=1.0, in1=pbank(pa),
                                                                            op0=ALU.add, op1=ALU.mult), reads=[rtc, rpa], writes=[rtc])
                S.op("dve", lambda e, td_=td_, pb=pb: e.scalar_tensor_tensor(out=td_[:, 0:T], in0=td_[:, 0:T], scalar=1.0, in1=pbank(pb),
                                                                            op0=ALU.add, op1=ALU.mult), reads=[rtd, rpb], writes=[rtd])
                S.op("pool", lambda e, tc_=tc_, td_=td_, f=f: e.tensor_tensor(out=qmg[:, f, :], in0=tc_[:, 0:T], in1=td_[:, 0:T], op=ALU.add),
                     reads=[rtc, rtd], writes=[r_qmg[f]])

        def stage_F(t):
            xs, rx = xset[t % 2], r_xset[t % 2]
            bufs = [fetch_slot(t, 18), fetch_slot(t, 19)]
            pairs = [(0, 1), (2, 3), (4, 5)]
            for b in range(4):
                p0, p1 = pairs[b % 3]

                def fn(e, b=b, p0=p0):
                    ins = None
                    for half in range(2):
                        w = bufs[half][0][:].rearrange("p (k m) -> p k m", k=8)
                        for k in range(8):
                            ins = e.matmul(ps[:, p0 + half, :], lhsT=qmg[:, k, b * 128:(b + 1) * 128], rhs=w[:, k, :],
                                           start=(k == 0), stop=(k == 7))
                    return ins
                S.op("pe", fn, reads=[bufs[0][1], bufs[1][1]] + r_qmg, writes=[r_ps[p0], r_ps[p1]])
                xb = xs[b]
                xv = xb[:].rearrange("p (a m) -> p a m", a=2)
                S.op("dve", lambda e, xv=xv, p0=p0: e.scalar_tensor_tensor(out=xv, in0=ps[:, p0:p0 + 2, :], scalar=0.5, in1=xv,
                                                                          op0=ALU.mult, op1=ALU.add),
                     reads=[r_ps[p0], r_ps[p1], rx[b]], writes=[rx[b]])
                si = stat_i[0] % 4
                stat_i[0] += 1
                rs = r_stat[si]
                bg, rbg = big[b % 2], r_big[b % 2]
                S.op("act", lambda e, bg=bg, xb=xb: e.activation(out=bg[:], in_=xb[:], func=AF.Square), reads=[rx[b]], writes=[rbg])
                S.op("dve", lambda e, bg=bg, si=si: e.tensor_reduce(out=stat[:, 4 * si:4 * si + 1], in_=bg[:], axis=AX.X, op=ALU.add),
                     reads=[rbg], writes=[rs])
                rstd_ops(si, rs)
                S.op("pool", lambda e, xb=xb, si=si: e.tensor_scalar(
                    out=xb[:], in0=xb[:], scalar1=stat[:, 4 * si + 1:4 * si + 2], scalar2=0.0, op0=ALU.mult, op1=ALU.add),
                    reads=[rx[b], rs], writes=[rx[b]])
                S.op("pool", lambda e, xb=xb: e.tensor_tensor(out=xb[:], in0=xb[:], in1=fgrep[:], op=ALU.mult),
                     reads=[rx[b], r_fgrep], writes=[rx[b]])
                r0 = t * T + b * 128
                S.dma(lambda e, xb=xb, r0=r0: e.dma_start(out=out[r0:r0 + 128, :], in_=xb[:]), rx[b], reads=[rx[b]], qeng="pool", final=True)

        load_x(0)
        stage_A(0)
        stop = tuple(stop_after) if stop_after is not None else None
        for t in range(ntiles):
            more = (t + 1 < ntiles)
            stage_BC(t)
            if stop == ("B", t) or stop == ("C", t):
                break
            if more:
                load_x(t + 1)
            stage_D(t)
            if stop == ("D", t):
                break
            stage_E(t, a_next=(t + 1 if more else None))
            if stop == ("E", t):
                break
            stage_F(t)
            if stop == ("F", t):
                break
        dumpable = {
            "hT0": (hT[0], r_hT[0], [128, 8, T], BF16), "yr": (yr, r_yr, [128, 8, T], BF16), "ya": (ya, r_ya, [128, 8, T], BF16),
            "qmg": (qmg, r_qmg, [128, 8, T], BF16), "sga": (sga, r_sga, [128, 8, T], BF16), "kT": (kT, r_kT, [128, 4, 128 + T], BF16),
            "vtok": (vtok, [r_vtok], [128, 5, 256], BF16), "vT": (vT, [r_vT], [128, 8, 8], F32), "vd": (vd, [r_vd], [128, 4, 8], F32),
            "x00": (xset[0][0], [r_xset[0][0]], [128, D], F32),
        }
        for name in dump:
            tens, rl, shape, dt = dumpable[name]
            dd = nc.dram_tensor("dbg_" + name, shape, dt, kind="ExternalOutput").ap()
            S.dma(lambda e, dd=dd, tens=tens: e.dma_start(out=dd, in_=tens[:]), rl[0], reads=rl, qeng="pool", final=True)
        S.finish("pool")
        S.run_block()
    return nc


def make_in_maps(inputs):
    consts = _consts()
    maps = []
    xs = np.ascontiguousarray(inputs["x"], dtype=np.float32)
    for c in range(NCORES):
        m = {"x": xs[2 * c:2 * c + 2].reshape(2 * SEQ, D)}
        for k in ("norm_g", "w_in", "conv_w", "conv_b", "lru_w_a", "lru_b_a", "lru_w_x", "lru_b_x", "lru_lambda",
                  "attn_sinks", "w_rnn_out", "w_attn_out", "w_o"):
            m[k] = np.ascontiguousarray(inputs[k], dtype=np.float32)
        m["final_norm_g"] = np.ascontiguousarray(inputs["final_norm_g"], dtype=np.float32).reshape(1, D)
        m.update(consts)
        maps.append(m)
    return maps


def kernel(**inputs):
    nc = build_program()
    in_maps = make_in_maps(inputs)
    res = run_bass_kernel_spmd(nc, in_maps, core_ids=list(range(NCORES)))
    outs = [np.asarray(r["out"], dtype=np.float32).reshape(2, SEQ, D) for r in res.results]
    return np.concatenate(outs, axis=0)
```

```python
import math
from contextlib import ExitStack

import numpy as np

import concourse.bass as bass
import concourse.mybir as mybir
from concourse.bass_utils import run_bass_kernel_spmd

F32 = mybir.dt.float32
BF16 = mybir.dt.bfloat16
AF = mybir.ActivationFunctionType
ALU = mybir.AluOpType
AX = mybir.AxisListType

NCORES = 8
SEQ = 2048
D = 1024
DIN = 6656
T = 512
TPS = SEQ // T
NT = 2 * TPS
OFF_U, OFF_G, OFF_Q, OFF_K, OFF_V, OFF_GA, OFF_MR, OFF_MA = 0, 1024, 2048, 3072, 3328, 3584, 4608, 5632
NSLOT = 20
EPS = 1e-6


class Res:
    __slots__ = ("name", "w", "rs", "dsem", "dcnt")

    def __init__(self, name):
        self.name = name
        self.w = None
        self.rs = {}
        self.dsem = None
        self.dcnt = 0


class Sched:
    CE = ("pe", "act", "dve", "pool")

    def __init__(self, nc, stack, same_engine_sync=True):
        self.nc = nc
        self.stack = stack
        self.q = {e: [] for e in self.CE + ("sp",)}
        self.sem = {e: stack.enter_context(nc.semaphore("s_" + e)) for e in self.CE}
        self.cnt = {e: 0 for e in self.CE}
        self.waited = {}
        self.same = same_engine_sync
        self.nsem = 0
        self.final = []
        self.pool_dmas = []

    @staticmethod
    def _flat(xs):
        out = []
        for x in xs:
            if isinstance(x, (list, tuple)):
                out.extend(Sched._flat(x))
            else:
                out.append(x)
        return out

    def _deps(self, reads, writes):
        deps = []
        for r in reads:
            if r.w is not None:
                deps.append(r.w)
        for r in writes:
            if r.w is not None:
                deps.append(r.w)
            deps.extend(r.rs.values())
        return deps

    def _need(self, eng, deps):
        best = {}
        for sem, val, src in deps:
            if src == eng and (eng == "pe" or not self.same):
                continue
            k = id(sem)
            if k not in best or val > best[k][1]:
                best[k] = (sem, val)
        out = []
        for k, (sem, val) in best.items():
            if self.waited.get((eng, k), 0) >= val:
                continue
            self.waited[(eng, k)] = val
            out.append((sem, val))
        return out

    @staticmethod
    def _addr(r, tok):
        k = id(tok[0])
        if k not in r.rs or r.rs[k][1] < tok[1]:
            r.rs[k] = tok

    def op(self, eng, fn, reads=(), writes=()):
        reads, writes = self._flat(reads), self._flat(writes)
        waits = self._need(eng, self._deps(reads, writes))
        self.cnt[eng] += 1
        tok = (self.sem[eng], self.cnt[eng], eng)
        self.q[eng].append((waits, fn, (self.sem[eng], 1)))
        for r in reads:
            self._addr(r, tok)
        for r in writes:
            r.w = tok
            r.rs = {}

    def dma(self, fn, owner, reads=(), writes=(), qeng="sp", final=False, skip_own=False):
        reads, writes = self._flat(reads), self._flat(writes)
        kind = 0 if qeng == "pool" else 1
        if owner.dsem is None:
            owner.dsem = [None, None]
            owner.dcnt = [0, 0]
        if owner.dsem[kind] is None:
            owner.dsem[kind] = self.stack.enter_context(self.nc.semaphore("d%d" % self.nsem))
            self.nsem += 1
        deps = self._deps(reads, writes)
        if skip_own:
            deps = [d for d in deps if d[0] is not owner.dsem[kind]]
        if qeng == "pool" and not skip_own and len(self.pool_dmas) >= 3:
            deps.append(self.pool_dmas[-3])
        waits = self._need(qeng, deps)
        owner.dcnt[kind] += 16
        tok = (owner.dsem[kind], owner.dcnt[kind], "dma")
        self.q[qeng].append((waits, fn, (owner.dsem[kind], 16)))
        for r in reads:
            self._addr(r, tok)
        for r in writes:
            r.w = tok
            r.rs = {}
        if final:
            self.final.append(tok)
        if qeng == "pool":
            if skip_own and self.pool_dmas and self.pool_dmas[-1][0] is tok[0]:
                self.pool_dmas[-1] = tok
            else:
                self.pool_dmas.append(tok)

    def finish(self, qeng="sp"):
        waits = self._need(qeng, self.final)
        self.q[qeng].append((waits, None, None))

    def replay(self, name, e):
        for waits, fn, inc in self.q[name]:
            for sem, val in waits:
                e.wait_ge(sem, val)
            if fn is None:
                continue
            ins = fn(e)
            if inc is not None:
                ins.then_inc(inc[0], inc[1])

    def run_block(self):
        with self.nc.Block() as block:
            @block.tensor
            def _(e):
                self.replay("pe", e)

            @block.scalar
            def _(e):
                self.replay("act", e)

            @block.vector
            def _(e):
                self.replay("dve", e)

            @block.gpsimd
            def _(e):
                self.replay("pool", e)

            @block.sync
            def _(e):
                self.replay("sp", e)


class Ring:
    def __init__(self, items):
        self.items = items
        self.i = 0

    def get(self):
        it = self.items[self.i % len(self.items)]
        self.i += 1
        return it


def _consts():
    c = {}
    c["c_ident"] = np.eye(128, dtype=np.float32)
    perm = np.zeros((128, 128), np.float32)
    for m in range(128):
        d = m % 64
        if d < 8:
            perm[m + 8, m] = 1.0
        elif d < 16:
            perm[m - 8, m] = 1.0
    c["c_perm"] = perm
    k = np.arange(128)[:, None]
    q = np.arange(128)[None, :]
    mask = np.concatenate([(q < k), (q >= k)], axis=1).astype(np.float32)
    c["c_mask"] = np.concatenate([mask, mask], axis=1)
    eh = np.zeros((128, 16, 16), np.float32)
    for h in range(16):
        eh[:, h, h] = 1.0
    c["c_eh"] = eh.reshape(128, 256)
    bc = np.zeros((16, 8, 128), np.float32)
    for cc in range(8):
        for p in range(128):
            bc[2 * cc + p // 64, cc, p] = 1.0
    c["c_bc"] = bc.reshape(16, 1024)
    pos = np.arange(SEQ, dtype=np.float32)
    inv_freq = (np.float32(500000.0) ** (-np.arange(0, 16, 2, dtype=np.float32) / np.float32(16))).astype(np.float32)
    ang = (pos[:, None] * inv_freq[None, :]).astype(np.float32)
    cos = np.cos(ang).astype(np.float32)
    sin = np.sin(ang).astype(np.float32)
    C = np.ones((128, SEQ), np.float32)
    Sg = np.zeros((128, SEQ), np.float32)
    for p in range(128):
        d = p % 64
        if d < 8:
            C[p] = cos[:, d]
            Sg[p] = -sin[:, d]
        elif d < 16:
            C[p] = cos[:, d - 8]
            Sg[p] = sin[:, d - 8]
    c["c_ropeC"] = C
    c["c_ropeS"] = Sg
    return c


def build_program(ntiles=NT, same_engine_sync=True, stop_after=None, dump=()):
    nc = bass.Bass("TRN2", target_bir_lowering=False)

    def din(name, shape):
        return nc.dram_tensor(name, shape, F32, kind="ExternalInput").ap()

    x = din("x", [2 * SEQ, D])
    norm_g = din("norm_g", [1, D])
    w_in = din("w_in", [1, D, DIN])
    conv_w = din("conv_w", [1, 4, D])
    conv_b = din("conv_b", [1, D])
    lru_w_a = din("lru_w_a", [1, 8, 128, 128])
    lru_b_a = din("lru_b_a", [1, D])
    lru_w_x = din("lru_w_x", [1, 8, 128, 128])
    lru_b_x = din("lru_b_x", [1, D])
    lru_lambda = din("lru_lambda", [1, D])
    attn_sinks = din("attn_sinks", [1, 16])
    w_rnn_out = din("w_rnn_out", [1, D, D])
    w_attn_out = din("w_attn_out", [1, D, D])
    w_o = din("w_o", [1, D, D])
    final_norm_g = din("final_norm_g", [1, D])
    c_ident = din("c_ident", [128, 128])
    c_perm = din("c_perm", [128, 128])
    c_mask = din("c_mask", [128, 512])
    c_eh = din("c_eh", [128, 256])
    c_bc = din("c_bc", [16, 1024])
    c_ropeC = din("c_ropeC", [128, SEQ])
    c_ropeS = din("c_ropeS", [128, SEQ])
    wstream = nc.dram_tensor("wstream", [NSLOT, 128, 4096], BF16, kind="Internal").ap()
    out = nc.dram_tensor("out", [2 * SEQ, D], F32, kind="ExternalOutput").ap()

    win_v = w_in.rearrange("o (k p) c -> p (o k) c", p=128)
    wro_v = w_rnn_out.rearrange("o (k p) c -> p (o k) c", p=128)
    wao_v = w_attn_out.rearrange("o (k p) c -> p (o k) c", p=128)
    wo_v = w_o.rearrange("o (k p) c -> p (o k) c", p=128)

    with ExitStack() as st:
        S = Sched(nc, st, same_engine_sync=same_engine_sync)

        def sb(name, shape, dt):
            return st.enter_context(nc.sbuf_tensor(name, shape, dt))

        ps = st.enter_context(nc.psum_tensor("ps", [128, 7, 512], F32))
        psT = st.enter_context(nc.psum_tensor("psT", [128, 1024], BF16))
        r_ps = [Res("ps%d" % i) for i in range(7)]
        r_psT = Res("psT")

        ident = sb("ident", [128, 128], BF16); r_ident = Res("ident")
        permb = sb("permb", [128, 128], BF16); r_perm = Res("perm")
        maskb = sb("maskb", [128, 2, 256], BF16); r_mask = Res("mask")
        ehb = sb("ehb", [128, 256], BF16); r_eh = Res("eh")
        bcs = sb("bcs", [16, 1024], F32); r_bc = Res("bc")
        grep = sb("grep", [128, D], F32); r_grep = Res("grep")
        fgrep = sb("fgrep", [128, D], F32); r_fgrep = Res("fgrep")
        mhalf = sb("mhalf", [128, 1], F32); r_mhalf = Res("mhalf")
        vrow = sb("vrow", [64, 128], F32); r_vrow = Res("vrow")
        identf = sb("identf", [128, 128], F32); r_identf = Res("identf")
        vT = sb("vT", [128, 8, 8], F32); r_vT = Res("vT")
        vd = sb("vd", [128, 4, 8], F32); r_vd = Res("vd")
        sinkt = sb("sinkt", [16, 2], F32); r_sink = Res("sink")
        lruA = sb("lruA", [128, 8, 128], BF16); r_lruA = Res("lruA")
        lruX = sb("lruX", [128, 8, 128], BF16); r_lruX = Res("lruX")

        wring = [sb("wring%d" % i, [128, 4096], BF16) for i in range(4)]
        r_wring = [Res("wring%d" % i) for i in range(4)]
        hT = [sb("hT%d" % i, [128, 8, T], BF16) for i in range(2)]
        r_hT = [[Res("hT%d_%d" % (i, b)) for b in range(4)] for i in range(2)]
        yr = sb("yr", [128, 8, T], BF16); r_yr = [Res("yr%d" % i) for i in range(8)]
        ya = sb("ya", [128, 8, T], BF16); r_ya = [Res("ya%d" % i) for i in range(4)]
        qmg = sb("qmg", [128, 8, T], BF16); r_qmg = [Res("qmg%d" % i) for i in range(8)]
        sga = sb("sga", [128, 8, T], BF16); r_sga = [Res("sga%d" % i) for i in range(8)]
        kT = sb("kT", [128, 4, 128 + T], BF16); r_kT = [Res("kT%d" % i) for i in range(4)]
        vtok = sb("vtok", [128, 5, 256], BF16); r_vtok = Res("vtok")
        halo = sb("halo", [128, 8, 4], F32); r_halo = [Res("halo%d" % i) for i in range(8)]
        hst = sb("hst", [128, 8], F32); r_hst = [Res("hst%d" % i) for i in range(8)]
        ropeC = [sb("ropeC%d" % i, [128, T], F32) for i in range(2)]
        ropeS = [sb("ropeS%d" % i, [128, T], F32) for i in range(2)]
        r_rope = [Res("rope%d" % i) for i in range(2)]
        xset = [[sb("x%d_%d" % (i, b), [128, D], F32) for b in range(4)] for i in range(2)]
        r_xset = [[Res("x%d_%d" % (i, b)) for b in range(4)] for i in range(2)]
        xnb = [sb("xnb%d" % i, [128, D], BF16) for i in range(2)]
        r_xnb = [Res("xnb%d" % i) for i in range(2)]
        big = [sb("big%d" % i, [128, D], F32) for i in range(2)]
        r_big = [Res("big%d" % i) for i in range(2)]
        stat = sb("stat", [128, 16], F32)
        r_stat = [Res("stat%d" % i) for i in range(4)]
        def mkring(name, n, shape, dt):
            return Ring([(sb("%s%d" % (name, i), shape, dt), Res("%s%d" % (name, i))) for i in range(n)])
        wk_ue = mkring("wue", 2, [128, 520], F32)
        wk_uc = mkring("wuc", 3, [128, 512], F32)
        wk_sg = mkring("wsg", 3, [128, 512], F32)
        wk = mkring("wk", 8, [128, 512], F32)
        wb_ucb = mkring("wucb", 3, [128, 512], BF16)
        wb = mkring("wb", 3, [128, 512], BF16)
        pT_t = [sb("pT%d" % i, [128, 2, 256], BF16) for i in range(3)]
        pTr = Ring([(pT_t[i], Res("pT%d" % i)) for i in range(3)])
        dsm = sb("dsm", [16, 2, 128], F32); r_dsm = [Res("dsm0"), Res("dsm1")]

        psr = Ring([(i, r_ps[i]) for i in range(7)])

        def pbank(i):
            return ps[:, i, :]

        def cast_load(dst_ap, src_ap, res, reads=(), skip_own=False):
            S.dma(lambda e, d=dst_ap, s=src_ap: e.dma_start(out=d, in_=s), res, reads=reads, writes=[res], qeng="pool",
                  skip_own=skip_own)

        def load(dst_ap, src_ap, res, slow=False):
            S.dma(lambda e, d=dst_ap, s=src_ap, sl=slow: e.dma_start(out=d, in_=s, allow_slow_non_contiguous=sl),
                  res, writes=[res])

        cast_load(ident[:], c_ident, r_ident)
        cast_load(permb[:], c_perm, r_perm)
        cast_load(maskb[:].rearrange("p a m -> p (a m)"), c_mask, r_mask)
        cast_load(ehb[:], c_eh, r_eh)
        cast_load(lruA[:], lru_w_a.rearrange("o n c d -> c (o n) d"), r_lruA)
        cast_load(lruX[:], lru_w_x.rearrange("o n c d -> c (o n) d"), r_lruX)
        load(bcs[:], c_bc, r_bc)
        load(grep[:], norm_g.partition_broadcast(128), r_grep)
        load(fgrep[:], final_norm_g.partition_broadcast(128), r_fgrep)
        vsrc = [conv_w[0, 0:1, :], conv_w[0, 1:2, :], conv_w[0, 2:3, :], conv_w[0, 3:4, :], conv_b, lru_b_a, lru_b_x, lru_lambda]
        for i, v in enumerate(vsrc):
            load(vrow[8 * i:8 * i + 8, :], v.rearrange("o (n p) -> (o n) p", p=128), r_vrow)
        load(identf[:], c_ident, r_identf)
        load(sinkt[:, 0:1], attn_sinks.rearrange("o h -> h o"), r_sink, slow=True)
        S.op("pe", lambda e: e.transpose(ps[:, 0, 0:64], vrow[:, :], identf[0:64, 0:64]), reads=[r_vrow, r_identf], writes=[r_ps[0]])
        S.op("act", lambda e: e.activation(out=vT[:].rearrange("p a b -> p (a b)"), in_=ps[:, 0, 0:64], func=AF.Copy),
             reads=[r_ps[0]], writes=[r_vT])

        S.op("pool", lambda e: e.memset(mhalf[:], -0.5), writes=[r_mhalf])
        S.op("dve", lambda e: e.tensor_scalar(out=vd[:, 0:2, :], in0=vT[:, 5:7, :], scalar1=0.5, scalar2=None, op0=ALU.mult),
             reads=[r_vT], writes=[r_vd])
        S.op("act", lambda e: e.activation(out=vd[:, 2, :], in_=vT[:, 7, :], func=AF.Exp, scale=-1.0), reads=[r_vT], writes=[r_vd])
        S.op("act", lambda e: e.activation(out=vd[:, 3, :], in_=vd[:, 2, :], func=AF.Ln, bias=1.0), reads=[r_vd], writes=[r_vd])
        S.op("dve", lambda e: e.tensor_scalar(out=vd[:, 2, :], in0=vd[:, 3, :], scalar1=-4.0, scalar2=None, op0=ALU.mult),
             reads=[r_vd], writes=[r_vd])
        S.op("dve", lambda e: e.tensor_scalar(out=vd[:, 3, :], in0=vd[:, 3, :], scalar1=-8.0, scalar2=None, op0=ALU.mult),
             reads=[r_vd], writes=[r_vd])
        S.op("act", lambda e: e.activation(out=sinkt[:, 1:2], in_=sinkt[:, 0:1], func=AF.Exp), reads=[r_sink], writes=[r_sink])

        def slot_units(s):
            res = []
            if s < 4:
                res.append((0, 256, win_v[:, :, OFF_U + 2 * s * 128: OFF_U + (2 * s + 2) * 128]))
                res.append((256, 512, win_v[:, :, OFF_G + 2 * s * 128: OFF_G + (2 * s + 2) * 128]))
            elif s == 4:
                for g in range(4):
                    for hf in range(2):
                        res.append((g * 128 + hf * 64, g * 128 + hf * 64 + 64, win_v[:, :, OFF_K + g * 64: OFF_K + (g + 1) * 64]))
            elif s == 5:
                res.append((0, 256, win_v[:, :, OFF_V:OFF_V + 256]))
                res.append((256, 512, win_v[:, :, OFF_Q:OFF_Q + 256]))
            elif s == 6:
                res.append((0, 512, win_v[:, :, OFF_Q + 256:OFF_Q + 768]))
            elif s == 7:
                res.append((0, 256, win_v[:, :, OFF_Q + 768:OFF_Q + 1024]))
                res.append((256, 512, win_v[:, :, OFF_GA:OFF_GA + 256]))
            elif s == 8:
                res.append((0, 512, win_v[:, :, OFF_GA + 256:OFF_GA + 768]))
            elif s == 9:
                res.append((0, 256, win_v[:, :, OFF_GA + 768:OFF_GA + 1024]))
            elif s < 18:
                i, y = (s - 10) // 2, (s - 10) % 2
                if y == 0:
                    res.append((0, 256, wro_v[:, :, 2 * i * 128:(2 * i + 2) * 128]))
                    res.append((256, 512, wao_v[:, :, 2 * i * 128:(2 * i + 2) * 128]))
                else:
                    res.append((0, 256, win_v[:, :, OFF_MR + 2 * i * 128: OFF_MR + (2 * i + 2) * 128]))
                    res.append((256, 512, win_v[:, :, OFF_MA + 2 * i * 128: OFF_MA + (2 * i + 2) * 128]))
            else:
                half = s - 18
                res.append((0, 512, wo_v[:, :, half * 512:(half + 1) * 512]))
            return res

        def kview(buf):
            return buf[:].rearrange("p (k c) -> p k c", k=8)

        wslot_i = [0]
        r_wstream = [Res("wstream%d" % i) for i in range(NSLOT)]

        def fetch_slot(t, s):
            i = wslot_i[0] % 4
            wslot_i[0] += 1
            buf, res = wring[i], r_wring[i]
            if t == 0:
                for ii, (c0, c1, src) in enumerate(slot_units(s)):
                    cast_load(kview(buf)[:, :, c0:c1], src, res, skip_own=(ii > 0))
                S.dma(lambda e, b=buf, s=s: e.dma_start(out=wstream[s], in_=b[:]), res, reads=[res], writes=[r_wstream[s]])
            else:
                S.dma(lambda e, b=buf, s=s: e.dma_start(out=b[:], in_=wstream[s]), res, reads=[r_wstream[s]], writes=[res])
            return buf, res

        def unit(buf, j):
            return kview(buf)[:, :, j * 128:(j + 1) * 128]

        def mm_group(out_ap, lhs_fn, rhs_fn, nk, reads, writes):
            def fn(e, out_ap=out_ap, lhs_fn=lhs_fn, rhs_fn=rhs_fn, nk=nk):
                ins = None
                for k in range(nk):
                    ins = e.matmul(out_ap, lhsT=lhs_fn(k), rhs=rhs_fn(k), start=(k == 0), stop=(k == nk - 1))
                return ins
            S.op("pe", fn, reads=reads, writes=writes)

        def load_x(t):
            xs, rx = xset[t % 2], r_xset[t % 2]
            for b in range(4):
                r0 = t * T + b * 128
                S.dma(lambda e, d=xs[b], r0=r0: e.dma_start(out=d[:], in_=x[r0:r0 + 128, :]), rx[b], writes=[rx[b]])
            pos0 = (t % TPS) * T
            i = t % 2
            S.dma(lambda e, i=i, p=pos0: e.dma_start(out=ropeC[i][:], in_=c_ropeC[:, p:p + T]), r_rope[i], writes=[r_rope[i]])
            S.dma(lambda e, i=i, p=pos0: e.dma_start(out=ropeS[i][:], in_=c_ropeS[:, p:p + T]), r_rope[i], writes=[r_rope[i]])

        def rstd_ops(sidx, rs):
            c0 = 4 * sidx
            S.op("pool", lambda e, c0=c0: e.tensor_scalar(out=stat[:, c0 + 1:c0 + 2], in0=stat[:, c0:c0 + 1], scalar1=1.0 / D,
                                                          scalar2=EPS, op0=ALU.mult, op1=ALU.add), reads=[rs], writes=[rs])
            S.op("pool", lambda e, c0=c0: e.tensor_tensor(out=stat[:, c0 + 1:c0 + 2], in0=stat[:, c0 + 1:c0 + 2], in1=mhalf[:],
                                                          op=ALU.pow), reads=[rs, r_mhalf], writes=[rs])

        stat_i = [0]

        def stage_A(t, blocks=(0, 1, 2, 3), part="both"):
            hTt, rh = hT[t % 2], r_hT[t % 2]
            xs, rx = xset[t % 2], r_xset[t % 2]
            for b in blocks:
                xb_, rxb = xnb[b % 2], r_xnb[b % 2]
                if part in ("both", "tr"):
                    def tr(e, xb_=xb_):
                        ins = None
                        for c in range(8):
                            ins = e.transpose(psT[:, c * 128:(c + 1) * 128], xb_[:, c * 128:(c + 1) * 128], ident[:])
                        return ins
                if part == "tr":
                    S.op("pe", tr, reads=[rxb, r_ident], writes=[r_psT])
                    S.op("act", lambda e, hTt=hTt, b=b: e.activation(
                        out=hTt[:, :, b * 128:(b + 1) * 128], in_=psT[:].rearrange("p (c m) -> p c m", c=8), func=AF.Copy),
                        reads=[r_psT], writes=[rh[b]])
                    continue
                si = stat_i[0] % 4
                stat_i[0] += 1
                rs = r_stat[si]
                bg, rbg = big[b % 2], r_big[b % 2]
                S.op("act", lambda e, bg=bg, xb=xs[b]: e.activation(out=bg[:], in_=xb[:], func=AF.Square), reads=[rx[b]], writes=[rbg])
                S.op("dve", lambda e, bg=bg, si=si: e.tensor_reduce(out=stat[:, 4 * si:4 * si + 1], in_=bg[:], axis=AX.X, op=ALU.add),
                     reads=[rbg], writes=[rs])
                rstd_ops(si, rs)
                S.op("dve", lambda e, xb_=xb_, xb=xs[b], si=si: e.scalar_tensor_tensor(
                    out=xb_[:], in0=xb[:], scalar=stat[:, 4 * si + 1:4 * si + 2], in1=grep[:], op0=ALU.mult, op1=ALU.mult),
                    reads=[rx[b], rs, r_grep], writes=[rxb])
                if part == "elem":
                    continue
                S.op("pe", tr, reads=[rxb, r_ident], writes=[r_psT])
                S.op("act", lambda e, hTt=hTt, b=b: e.activation(
                    out=hTt[:, :, b * 128:(b + 1) * 128], in_=psT[:].rearrange("p (c m) -> p c m", c=8), func=AF.Copy),
                    reads=[r_psT], writes=[rh[b]])

        def make_B(t):
            hTt, rh = hT[t % 2], r_hT[t % 2]
            first = (t % TPS == 0)
            bstate = {}

            def front(n, buf, rbuf, j0):
                wu, wg = unit(buf, j0), unit(buf, 2 + j0)
                bu, rbu = psr.get()
                mm_group(pbank(bu), lambda k, wu=wu: wu[:, k, :], lambda k: hTt[:, k, :], 8, reads=[rbuf] + rh, writes=[rbu])
                bgp, rbg_ = psr.get()
                mm_group(pbank(bgp), lambda k, wg=wg: wg[:, k, :], lambda k: hTt[:, k, :], 8, reads=[rbuf] + rh, writes=[rbg_])
                ue, rue = wk_ue.get()
                if first:
                    S.op("dve", lambda e, ue=ue: e.memset(ue[:, 0:3], 0.0), writes=[rue])
                else:
                    S.op("pool", lambda e, ue=ue, n=n: e.tensor_copy(out=ue[:, 0:3], in_=halo[:, n, 0:3]),
                         reads=[r_halo[n]], writes=[rue])
                S.op("act", lambda e, ue=ue, bu=bu: e.activation(out=ue[:, 3:3 + T], in_=pbank(bu), func=AF.Copy),
                     reads=[rbu, rue], writes=[rue])
                S.op("pool", lambda e, ue=ue, n=n: e.tensor_copy(out=halo[:, n, 0:3], in_=ue[:, T:T + 3]),
                     reads=[rue], writes=[r_halo[n]])
                sg, rsg = wk_sg.get()
                S.op("act", lambda e, sg=sg, bgp=bgp: e.activation(out=sg[:, 0:T], in_=pbank(bgp), func=AF.Tanh, scale=0.5),
                     reads=[rbg_], writes=[rsg])
                S.op("dve", lambda e, sg=sg, bgp=bgp: e.scalar_tensor_tensor(
                    out=sg[:, 0:T], in0=sg[:, 0:T], scalar=1.0, in1=pbank(bgp), op0=ALU.add, op1=ALU.mult),
                    reads=[rsg, rbg_], writes=[rsg])
                uc, ruc = wk_uc.get()
                S.op("pool", lambda e, ue=ue, uc=uc, n=n: e.tensor_scalar(
                    out=uc[:, 0:T], in0=ue[:, 3:3 + T], scalar1=vT[:, 3, n:n + 1], scalar2=vT[:, 4, n:n + 1], op0=ALU.mult, op1=ALU.add),
                    reads=[rue, r_vT], writes=[ruc])
                for j in (2,):
                    cq, rcq = wk.get()
                    S.op("pool", lambda e, ue=ue, cq=cq, n=n, j=j: e.tensor_scalar(
                        out=cq[:, 0:T], in0=ue[:, j:j + T], scalar1=vT[:, j, n:n + 1], scalar2=0.0, op0=ALU.mult, op1=ALU.add),
                        reads=[rue, r_vT], writes=[rcq])
                    S.op("pool", lambda e, cq=cq, uc=uc: e.tensor_tensor(out=uc[:, 0:T], in0=uc[:, 0:T], in1=cq[:, 0:T], op=ALU.add),
                         reads=[rcq, ruc], writes=[ruc])
                for j in (1, 0):
                    S.op("dve", lambda e, ue=ue, uc=uc, n=n, j=j: e.scalar_tensor_tensor(
                        out=uc[:, 0:T], in0=ue[:, j:j + T], scalar=vT[:, j, n:n + 1], in1=uc[:, 0:T], op0=ALU.mult, op1=ALU.add),
                        reads=[rue, r_vT, ruc], writes=[ruc])
                ucb, rucb = wb_ucb.get()
                return (n, uc, ruc, ucb, rucb, sg, rsg)

            def back(n, uc, ruc, ucb, rucb, sg, rsg):
                br, rbr = psr.get()
                mm_group(pbank(br), lambda k, n=n: lruA[:, n, :], lambda k, ucb=ucb: ucb[:], 1, reads=[r_lruA, rucb], writes=[rbr])
                bi, rbi = psr.get()
                mm_group(pbank(bi), lambda k, n=n: lruX[:, n, :], lambda k, ucb=ucb: ucb[:], 1, reads=[r_lruX, rucb], writes=[rbi])
                tr_, rtr = wk.get()
                S.op("act", lambda e, tr_=tr_, br=br, n=n: e.activation(out=tr_[:, 0:T], in_=pbank(br), func=AF.Tanh, scale=0.5,
                                                                       bias=vd[:, 0, n:n + 1]), reads=[rbr, r_vd], writes=[rtr])
                iu, riu = wk.get()
                S.op("act", lambda e, iu=iu, bi=bi, n=n: e.activation(out=iu[:, 0:T], in_=pbank(bi), func=AF.Tanh, scale=0.5,
                                                                     bias=vd[:, 1, n:n + 1]), reads=[rbi, r_vd], writes=[riu])
                a_, ra = wk.get()
                S.op("act", lambda e, a_=a_, tr_=tr_, n=n: e.activation(out=a_[:, 0:T], in_=tr_[:, 0:T], func=AF.Exp,
                                                                       scale=vd[:, 2, n:n + 1], bias=vd[:, 2, n:n + 1]),
                     reads=[rtr, r_vd], writes=[ra])
                s_, rs_ = wk.get()
                S.op("act", lambda e, s_=s_, tr_=tr_, n=n: e.activation(out=s_[:, 0:T], in_=tr_[:, 0:T], func=AF.Exp,
                                                                       scale=vd[:, 3, n:n + 1], bias=vd[:, 3, n:n + 1]),
                     reads=[rtr, r_vd], writes=[rs_])
                S.op("act", lambda e, s_=s_: e.activation(out=s_[:, 0:T], in_=s_[:, 0:T], func=AF.Sqrt, scale=-1.0, bias=1.0),
                     reads=[rs_], writes=[rs_])
                S.op("dve", lambda e, iu=iu, uc=uc: e.scalar_tensor_tensor(out=iu[:, 0:T], in0=iu[:, 0:T], scalar=1.0, in1=uc[:, 0:T],
                                                                          op0=ALU.add, op1=ALU.mult), reads=[riu, ruc], writes=[riu])
                S.op("dve", lambda e, iu=iu, s_=s_: e.scalar_tensor_tensor(out=iu[:, 0:T], in0=s_[:, 0:T], scalar=0.5, in1=iu[:, 0:T],
                                                                          op0=ALU.mult, op1=ALU.mult), reads=[riu, rs_], writes=[riu])
                h_, rh_ = wk.get()
                if first:
                    S.op("dve", lambda e, h_=h_, a_=a_, iu=iu: e.tensor_tensor_scan(
                        out=h_[:, 0:T], data0=a_[:, 0:T], data1=iu[:, 0:T], initial=0.0, op0=ALU.mult, op1=ALU.add),
                        reads=[ra, riu], writes=[rh_])
                else:
                    S.op("dve", lambda e, h_=h_, a_=a_, iu=iu, n=n: e.tensor_tensor_scan(
                        out=h_[:, 0:T], data0=a_[:, 0:T], data1=iu[:, 0:T], initial=hst[:, n:n + 1], op0=ALU.mult, op1=ALU.add),
                        reads=[ra, riu, r_hst[n]], writes=[rh_])
                S.op("pool", lambda e, h_=h_, n=n: e.tensor_copy(out=hst[:, n:n + 1], in_=h_[:, T - 1:T]),
                     reads=[rh_], writes=[r_hst[n]])
                S.op("dve", lambda e, h_=h_, sg=sg, n=n: e.scalar_tensor_tensor(
                    out=yr[:, n, :], in0=h_[:, 0:T], scalar=0.5, in1=sg[:, 0:T], op0=ALU.mult, op1=ALU.mult),
                    reads=[rh_, rsg], writes=[r_yr[n]])

            def front_n(n):
                if n % 2 == 0:
                    bstate["buf"] = fetch_slot(t, n // 2)
                buf, rbuf = bstate["buf"]
                return front(n, buf, rbuf, n % 2)

            def front_b(n, uc, ruc, ucb, rucb, sg, rsg):
                S.op("act", lambda e, uc=uc, ucb=ucb: e.activation(out=ucb[:], in_=uc[:, 0:T], func=AF.Copy), reads=[ruc], writes=[rucb])
            return front_n, front_b, back

        def rope_chain(pb, rpb, dst_ap, rdst, t):
            i = t % 2
            raw, rraw = wb.get()
            S.op("act", lambda e, raw=raw, pb=pb: e.activation(out=raw[:], in_=pbank(pb), func=AF.Copy), reads=[rpb], writes=[rraw])
            return (raw, rraw, dst_ap, rdst, i)

        def rope_finish(raw, rraw, dst_ap, rdst, i):
            p2, rp2 = psr.get()
            mm_group(pbank(p2), lambda k: permb[:], lambda k, raw=raw: raw[:], 1, reads=[r_perm, rraw], writes=[rp2])
            t2, rt2 = wk.get()
            S.op("pool", lambda e, t2=t2, raw=raw, i=i: e.tensor_tensor(out=t2[:, 0:T], in0=raw[:], in1=ropeC[i][:], op=ALU.mult),
                 reads=[rraw, r_rope[i]], writes=[rt2])
            t1, rt1 = wk.get()
            S.op("dve", lambda e, t1=t1, p2=p2, i=i: e.tensor_tensor(out=t1[:, 0:T], in0=pbank(p2), in1=ropeS[i][:], op=ALU.mult),
                 reads=[rp2, r_rope[i]], writes=[rt1])
            S.op("dve", lambda e, t1=t1, t2=t2, dst_ap=dst_ap: e.tensor_tensor(out=dst_ap, in0=t1[:, 0:T], in1=t2[:, 0:T], op=ALU.add),
                 reads=[rt1, rt2], writes=[rdst])

        def make_C(t):
            hTt, rh = hT[t % 2], r_hT[t % 2]
            units = [("kd", g) for g in range(4)] + [("v", 0), ("v", 1)] + [("q", c) for c in range(8)] + \
                    [("ga", c) for c in range(8)] + [("pad", 0), ("pad", 1)]
            pend = []
            cstate = {}

            def unit_fn(ui):
                kind, i = units[ui]
                j = ui % 4
                if j == 0:
                    cstate["buf"] = fetch_slot(t, 4 + ui // 4)
                buf, rbuf = cstate["buf"]
                if kind in ("kd", "q"):
                    w = unit(buf, j)
                    pb, rpb = psr.get()
                    mm_group(pbank(pb), lambda k, w=w: w[:, k, :], lambda k: hTt[:, k, :], 8, reads=[rbuf] + rh, writes=[rpb])
                    if kind == "kd":
                        dst, rd = kT[:, i, 128:128 + T], r_kT[i]
                    else:
                        dst, rd = qmg[:, i, :], r_qmg[i]
                    pend.append(rope_chain(pb, rpb, dst, rd, t))
                    if len(pend) > 2:
                        rope_finish(*pend.pop(0))
                elif kind == "v" and i == 0:
                    wv = kview(buf)[:, :, j * 128:(j + 2) * 128]
                    for pr in range(2):
                        pb, rpb = psr.get()

                        def fn(e, pb=pb, pr=pr, wv=wv):
                            ins = None
                            for bb in range(2):
                                b = pr * 2 + bb
                                for k in range(8):
                                    ins = e.matmul(ps[:, pb, bb * 256:(bb + 1) * 256], lhsT=hTt[:, k, b * 128:(b + 1) * 128],
                                                   rhs=wv[:, k, :], start=(k == 0), stop=(k == 7))
                            return ins
                        S.op("pe", fn, reads=[rbuf] + rh, writes=[rpb])
                        S.op("act", lambda e, pb=pb, pr=pr: e.activation(
                            out=vtok[:, 1 + 2 * pr:3 + 2 * pr, :], in_=pbank(pb).rearrange("p (b m) -> p b m", b=2), func=AF.Copy),
                            reads=[rpb], writes=[r_vtok])
                elif kind == "ga":
                    w = unit(buf, j)
                    pb, rpb = psr.get()
                    mm_group(pbank(pb), lambda k, w=w: w[:, k, :], lambda k: hTt[:, k, :], 8, reads=[rbuf] + rh, writes=[rpb])
                    tg, rtg = wk.get()
                    S.op("act", lambda e, tg=tg, pb=pb: e.activation(out=tg[:, 0:T], in_=pbank(pb), func=AF.Tanh, scale=0.5),
                         reads=[rpb], writes=[rtg])
                    S.op("dve", lambda e, tg=tg, pb=pb, i=i: e.scalar_tensor_tensor(
                        out=sga[:, i, :], in0=tg[:, 0:T], scalar=1.0, in1=pbank(pb), op0=ALU.add, op1=ALU.mult),
                        reads=[rtg, rpb], writes=[r_sga[i]])

            def flush():
                while pend:
                    rope_finish(*pend.pop(0))
            return unit_fn, flush, len(units)

        def stage_BC(t):
            frontB, frontB2, backB = make_B(t)
            unitC, flushC, NU = make_C(t)
            pendB = []
            ui = 0
            NP = 10
            for p_ in range(NP):
                if p_ < 8:
                    pendB.append(frontB(p_))
                if p_ >= 2:
                    backB(*pendB.pop(0))
                if p_ < 8:
                    frontB2(*pendB[-1])
                k = -(-(NU - ui) // (NP - p_))
                for _ in range(k):
                    unitC(ui)
                    ui += 1
            flushC()

        def stage_D(t):
            ts = t % TPS
            S_BANKS = [(0, 0), (5, 0)]
            sring = Ring(S_BANKS)
            DEN, VALS = 2, 3
            rbv = psT[:].bitcast(F32)
            post2_pending = [None]
            for jq in range(4):
                first = (ts == 0 and jq == 0)
                lo = 128 if first else 0
                pend = []

                def s1(i, jq=jq, first=first, lo=lo):
                    c, g = i, i // 2
                    bk, _hf = sring.get()
                    rbk = [r_ps[bk], r_ps[bk + 1]]

                    def fn(e, bk=bk, g=g, c=c):
                        ins = None
                        for hh in range(2):
                            hb = hh * 64
                            if not first:
                                ins = e.matmul(ps[:, bk + hh, 0:128], lhsT=kT[hb:hb + 64, g, jq * 128:(jq + 1) * 128],
                                               rhs=qmg[hb:hb + 64, c, jq * 128:(jq + 1) * 128], start=True, stop=True)
                            ins = e.matmul(ps[:, bk + hh, 128:256], lhsT=kT[hb:hb + 64, g, 128 + jq * 128:128 + (jq + 1) * 128],
                                           rhs=qmg[hb:hb + 64, c, jq * 128:(jq + 1) * 128], start=True, stop=True)
                        return ins
                    S.op("pe", fn, reads=[r_kT[g], r_qmg[c]], writes=[rbk])
                    pt, rpt = pTr.get()
                    S.op("act", lambda e, pt=pt, bk=bk: e.activation(
                        out=pt[:, :, lo:256], in_=ps[:, bk:bk + 2, lo:256], func=AF.Exp, scale=0.125),
                        reads=[rbk], writes=[rpt])
                    S.op("pool", lambda e, pt=pt: e.tensor_tensor(out=pt[:, :, lo:256], in0=pt[:, :, lo:256], in1=maskb[:, :, lo:256], op=ALU.mult),
                         reads=[rpt, r_mask], writes=[rpt])
                    return (i, pt, rpt)

                def s4(i, pt, rpt, jq=jq, first=first):
                    c, g = i, i // 2

                    def fn(e, pt=pt, c=c, g=g):
                        ins = None
                        for hh in range(2):
                            h = 2 * c + hh
                            hb = hh * 64
                            vo = ps[hb:hb + 64, VALS + c // 4, (c % 4) * 128:(c % 4 + 1) * 128]
                            if not first:
                                e.matmul(vo, lhsT=vtok[:, jq, g * 64:(g + 1) * 64], rhs=pt[:, hh, 0:128], start=True, stop=False)
                            e.matmul(vo, lhsT=vtok[:, jq + 1, g * 64:(g + 1) * 64], rhs=pt[:, hh, 128:256], start=first, stop=True)
                            if not first:
                                ins = e.matmul(ps[0:16, DEN, 0:256], lhsT=ehb[:, h * 16:(h + 1) * 16], rhs=pt[:, hh, 0:256], start=(h == 0),
                                               stop=(h == 15))
                            else:
                                ins = e.matmul(ps[0:16, DEN, 128:256], lhsT=ehb[:, h * 16:(h + 1) * 16], rhs=pt[:, hh, 128:256],
                                               start=(h == 0), stop=(h == 15))
                        return ins
                    S.op("pe", fn, reads=[rpt, r_vtok, r_eh], writes=[r_ps[VALS], r_ps[VALS + 1], r_ps[DEN]])

                def post1(jq=jq, first=first):
                    di = jq % 2
                    bg, rbg = big[0], r_big[0]
                    S.op("act", lambda e, bg=bg: e.activation(out=bg[:].rearrange("p (a m) -> p a m", a=2), in_=ps[:, VALS:VALS + 2, :],
                                                              func=AF.Copy), reads=[r_ps[VALS], r_ps[VALS + 1]], writes=[rbg])
                    S.op("dve", lambda e, di=di: e.tensor_scalar(out=dsm[:, di, :], in0=ps[0:16, DEN, 128:256], scalar1=sinkt[:, 1:2],
                                                                  scalar2=None, op0=ALU.add), reads=[r_ps[DEN], r_sink], writes=[r_dsm[di]])
                    if not first:
                        S.op("dve", lambda e, di=di: e.tensor_tensor(out=dsm[:, di, :], in0=dsm[:, di, :], in1=ps[0:16, DEN, 0:128],
                                                                      op=ALU.add), reads=[r_ps[DEN], r_dsm[di]], writes=[r_dsm[di]])
                    S.op("dve", lambda e, di=di: e.reciprocal(out=dsm[:, di, :], in_=dsm[:, di, :]), reads=[r_dsm[di]], writes=[r_dsm[di]])

                def post2(jq=jq):
                    di = jq % 2
                    bg, rbg = big[0], r_big[0]
                    b1, rb1 = big[1], r_big[1]
                    for hf in range(2):
                        def fnb(e, di=di, hf=hf):
                            ins = None
                            for cc in range(4):
                                c = hf * 4 + cc
                                ins = e.matmul(rbv[:, cc * 128:(cc + 1) * 128], lhsT=bcs[:, c * 128:(c + 1) * 128],
                                               rhs=dsm[:, di, :], start=True, stop=True)
                            return ins
                        S.op("pe", fnb, reads=[r_bc, r_dsm[di]], writes=[r_psT])
                        S.op("dve", lambda e, bg=bg, b1=b1, hf=hf: e.tensor_tensor(
                            out=b1[:, hf * 512:(hf + 1) * 512], in0=rbv, in1=bg[:, hf * 512:(hf + 1) * 512], op=ALU.mult),
                            reads=[r_psT, rbg], writes=[rb1])
                    S.op("dve", lambda e, b1=b1, jq=jq: e.scalar_tensor_tensor(
                        out=ya[:, :, jq * 128:(jq + 1) * 128], in0=b1[:].rearrange("p (c m) -> p c m", c=8), scalar=0.5,
                        in1=sga[:, :, jq * 128:(jq + 1) * 128], op0=ALU.mult, op1=ALU.mult),
                        reads=[rb1] + r_sga, writes=[r_ya[jq]])

                SK = 1
                for i in range(8 + SK):
                    if i < 8:
                        pend.append(s1(i))
                    if i >= SK:
                        s4(*pend.pop(0))
                    if i == 3 and post2_pending[0] is not None:
                        post2_pending[0]()
                        post2_pending[0] = None
                post1()
                post2_pending[0] = post2
            post2_pending[0]()
            S.op("pool", lambda e: e.tensor_copy(out=kT[:, :, 0:128], in_=kT[:, :, T:T + 128]), reads=r_kT, writes=r_kT)
            S.op("pool", lambda e: e.tensor_copy(out=vtok[:, 0, :], in_=vtok[:, 4, :]), reads=[r_vtok], writes=[r_vtok])

        def stage_E(t, a_next=None):
            hTt, rh = hT[t % 2], r_hT[t % 2]
            ebuf = {}
            for f in range(8):
                if a_next is not None:
                    stage_A(a_next, blocks=(f // 2,), part=("elem" if f % 2 == 0 else "tr"))
                if f % 2 == 0:
                    ebuf["x"] = fetch_slot(t, 10 + f)
                    ebuf["y"] = fetch_slot(t, 11 + f)
                (bx, rbx), (by, rby) = ebuf["x"], ebuf["y"]
                rbuf = [rbx, rby]
                wro, wao, wmr, wma = unit(bx, f % 2), unit(bx, 2 + f % 2), unit(by, f % 2), unit(by, 2 + f % 2)
                pc, rpc = psr.get()
                mm_group(pbank(pc), lambda k, w=wmr: w[:, k, :], lambda k: hTt[:, k, :], 8, reads=[rbuf] + rh, writes=[rpc])
                pd, rpd = psr.get()
                mm_group(pbank(pd), lambda k, w=wma: w[:, k, :], lambda k: hTt[:, k, :], 8, reads=[rbuf] + rh, writes=[rpd])
                pa, rpa = psr.get()
                mm_group(pbank(pa), lambda k, w=wro: w[:, k, :], lambda k: yr[:, k, :], 8, reads=[rbuf] + r_yr, writes=[rpa])
                pb, rpb = psr.get()
                mm_group(pbank(pb), lambda k, w=wao: w[:, k, :], lambda k: ya[:, k, :], 8, reads=[rbuf] + r_ya, writes=[rpb])
                tc_, rtc = wk.get()
                S.op("act", lambda e, tc_=tc_, pc=pc: e.activation(out=tc_[:, 0:T], in_=pbank(pc), func=AF.Tanh, scale=0.5),
                     reads=[rpc], writes=[rtc])
                td_, rtd = wk.get()
                S.op("act", lambda e, td_=td_, pd=pd: e.activation(out=td_[:, 0:T], in_=pbank(pd), func=AF.Tanh, scale=0.5),
                     reads=[rpd], writes=[rtd])
                S.op("dve", lambda e, tc_=tc_, pa=pa: e.scalar_tensor_tensor(out=tc_[:, 0:T], in0=tc_[:, 0:T], scalar=1.0, in1=pbank(pa),
                                                                            op0=ALU.add, op1=ALU.mult), reads=[rtc, rpa], writes=[rtc])
                S.op("dve", lambda e, td_=td_, pb=pb: e.scalar_tensor_tensor(out=td_[:, 0:T], in0=td_[:, 0:T], scalar=1.0, in1=pbank(pb),
                                                                            op0=ALU.add, op1=ALU.mult), reads=[rtd, rpb], writes=[rtd])
                S.op("pool", lambda e, tc_=tc_, td_=td_, f=f: e.tensor_tensor(out=qmg[:, f, :], in0=tc_[:, 0:T], in1=td_[:, 0:T], op=ALU.add),
                     reads=[rtc, rtd], writes=[r_qmg[f]])

        def stage_F(t):
            xs, rx = xset[t % 2], r_xset[t % 2]
            bufs = [fetch_slot(t, 18), fetch_slot(t, 19)]
            pairs = [(0, 1), (2, 3), (4, 5)]
            for b in range(4):
                p0, p1 = pairs[b % 3]

                def fn(e, b=b, p0=p0):
                    ins = None
                    for half in range(2):
                        w = bufs[half][0][:].rearrange("p (k m) -> p k m", k=8)
                        for k in range(8):
                            ins = e.matmul(ps[:, p0 + half, :], lhsT=qmg[:, k, b * 128:(b + 1) * 128], rhs=w[:, k, :],
                                           start=(k == 0), stop=(k == 7))
                    return ins
                S.op("pe", fn, reads=[bufs[0][1], bufs[1][1]] + r_qmg, writes=[r_ps[p0], r_ps[p1]])
                xb = xs[b]
                xv = xb[:].rearrange("p (a m) -> p a m", a=2)
                S.op("dve", lambda e, xv=xv, p0=p0: e.scalar_tensor_tensor(out=xv, in0=ps[:, p0:p0 + 2, :], scalar=0.5, in1=xv,
                                                                          op0=ALU.mult, op1=ALU.add),
                     reads=[r_ps[p0], r_ps[p1], rx[b]], writes=[rx[b]])
                si = stat_i[0] % 4
                stat_i[0] += 1
                rs = r_stat[si]
                bg, rbg = big[b % 2], r_big[b % 2]
                S.op("act", lambda e, bg=bg, xb=xb: e.activation(out=bg[:], in_=xb[:], func=AF.Square), reads=[rx[b]], writes=[rbg])
                S.op("dve", lambda e, bg=bg, si=si: e.tensor_reduce(out=stat[:, 4 * si:4 * si + 1], in_=bg[:], axis=AX.X, op=ALU.add),
                     reads=[rbg], writes=[rs])
                rstd_ops(si, rs)
                S.op("pool", lambda e, xb=xb, si=si: e.tensor_scalar(
                    out=xb[:], in0=xb[:], scalar1=stat[:, 4 * si + 1:4 * si + 2], scalar2=0.0, op0=ALU.mult, op1=ALU.add),
                    reads=[rx[b], rs], writes=[rx[b]])
                S.op("pool", lambda e, xb=xb: e.tensor_tensor(out=xb[:], in0=xb[:], in1=fgrep[:], op=ALU.mult),
                     reads=[rx[b], r_fgrep], writes=[rx[b]])
                r0 = t * T + b * 128
                S.dma(lambda e, xb=xb, r0=r0: e.dma_start(out=out[r0:r0 + 128, :], in_=xb[:]), rx[b], reads=[rx[b]], qeng="pool", final=True)

        load_x(0)
        stage_A(0)
        stop = tuple(stop_after) if stop_after is not None else None
        for t in range(ntiles):
            more = (t + 1 < ntiles)
            stage_BC(t)
            if stop == ("B", t) or stop == ("C", t):
                break
            if more:
                load_x(t + 1)
            stage_D(t)
            if stop == ("D", t):
                break
            stage_E(t, a_next=(t + 1 if more else None))
            if stop == ("E", t):
                break
            stage_F(t)
            if stop == ("F", t):
                break
        dumpable = {
            "hT0": (hT[0], r_hT[0], [128, 8, T], BF16), "yr": (yr, r_yr, [128, 8, T], BF16), "ya": (ya, r_ya, [128, 8, T], BF16),
            "qmg": (qmg, r_qmg, [128, 8, T], BF16), "sga": (sga, r_sga, [128, 8, T], BF16), "kT": (kT, r_kT, [128, 4, 128 + T], BF16),
            "vtok": (vtok, [r_vtok], [128, 5, 256], BF16), "vT": (vT, [r_vT], [128, 8, 8], F32), "vd": (vd, [r_vd], [128, 4, 8], F32),
            "x00": (xset[0][0], [r_xset[0][0]], [128, D], F32),
        }
        for name in dump:
            tens, rl, shape, dt = dumpable[name]
            dd = nc.dram_tensor("dbg_" + name, shape, dt, kind="ExternalOutput").ap()
            S.dma(lambda e, dd=dd, tens=tens: e.dma_start(out=dd, in_=tens[:]), rl[0], reads=rl, qeng="pool", final=True)
        S.finish("pool")
        S.run_block()
    return nc


def make_in_maps(inputs):
    consts = _consts()
    maps = []
    xs = np.ascontiguousarray(inputs["x"], dtype=np.float32)
    for c in range(NCORES):
        m = {"x": xs[2 * c:2 * c + 2].reshape(2 * SEQ, D)}
        for k in ("norm_g", "w_in", "conv_w", "conv_b", "lru_w_a", "lru_b_a", "lru_w_x", "lru_b_x", "lru_lambda",
                  "attn_sinks", "w_rnn_out", "w_attn_out", "w_o"):
            m[k] = np.ascontiguousarray(inputs[k], dtype=np.float32)
        m["final_norm_g"] = np.ascontiguousarray(inputs["final_norm_g"], dtype=np.float32).reshape(1, D)
        m.update(consts)
        maps.append(m)
    return maps


def kernel(**inputs):
    nc = build_program()
    in_maps = make_in_maps(inputs)
    res = run_bass_kernel_spmd(nc, in_maps, core_ids=list(range(NCORES)))
    outs = [np.asarray(r["out"], dtype=np.float32).reshape(2, SEQ, D) for r in res.results]
    return np.concatenate(outs, axis=0)
```

```python
import math
from contextlib import ExitStack

import numpy as np

import concourse.bass as bass
import concourse.mybir as mybir
from concourse.bass_utils import run_bass_kernel_spmd

F32 = mybir.dt.float32
BF16 = mybir.dt.bfloat16
AF = mybir.ActivationFunctionType
ALU = mybir.AluOpType
AX = mybir.AxisListType

NCORES = 8
SEQ = 2048
D = 1024
DIN = 6656
T = 512
TPS = SEQ // T
NT = 2 * TPS
OFF_U, OFF_G, OFF_Q, OFF_K, OFF_V, OFF_GA, OFF_MR, OFF_MA = 0, 1024, 2048, 3072, 3328, 3584, 4608, 5632
NSLOT = 20
EPS = 1e-6


class Res:
    __slots__ = ("name", "w", "rs", "dsem", "dcnt")

    def __init__(self, name):
        self.name = name
        self.w = None
        self.rs = {}
        self.dsem = None
        self.dcnt = 0


class Sched:
    CE = ("pe", "act", "dve", "pool")

    def __init__(self, nc, stack, same_engine_sync=True):
        self.nc = nc
        self.stack = stack
        self.q = {e: [] for e in self.CE + ("sp",)}
        self.sem = {e: stack.enter_context(nc.semaphore("s_" + e)) for e in self.CE}
        self.cnt = {e: 0 for e in self.CE}
        self.waited = {}
        self.same = same_engine_sync
        self.nsem = 0
        self.final = []
        self.pool_dmas = []

    @staticmethod
    def _flat(xs):
        out = []
        for x in xs:
            if isinstance(x, (list, tuple)):
                out.extend(Sched._flat(x))
            else:
                out.append(x)
        return out

    def _deps(self, reads, writes):
        deps = []
        for r in reads:
            if r.w is not None:
                deps.append(r.w)
        for r in writes:
            if r.w is not None:
                deps.append(r.w)
            deps.extend(r.rs.values())
        return deps

    def _need(self, eng, deps):
        best = {}
        for sem, val, src in deps:
            if src == eng and (eng == "pe" or not self.same):
                continue
            k = id(sem)
            if k not in best or val > best[k][1]:
                best[k] = (sem, val)
        out = []
        for k, (sem, val) in best.items():
            if self.waited.get((eng, k), 0) >= val:
                continue
            self.waited[(eng, k)] = val
            out.append((sem, val))
        return out

    @staticmethod
    def _addr(r, tok):
        k = id(tok[0])
        if k not in r.rs or r.rs[k][1] < tok[1]:
            r.rs[k] = tok

    def op(self, eng, fn, reads=(), writes=()):
        reads, writes = self._flat(reads), self._flat(writes)
        waits = self._need(eng, self._deps(reads, writes))
        self.cnt[eng] += 1
        tok = (self.sem[eng], self.cnt[eng], eng)
        self.q[eng].append((waits, fn, (self.sem[eng], 1)))
        for r in reads:
            self._addr(r, tok)
        for r in writes:
            r.w = tok
            r.rs = {}

    def dma(self, fn, owner, reads=(), writes=(), qeng="sp", final=False, skip_own=False):
        reads, writes = self._flat(reads), self._flat(writes)
        kind = 0 if qeng == "pool" else 1
        if owner.dsem is None:
            owner.dsem = [None, None]
            owner.dcnt = [0, 0]
        if owner.dsem[kind] is None:
            owner.dsem[kind] = self.stack.enter_context(self.nc.semaphore("d%d" % self.nsem))
            self.nsem += 1
        deps = self._deps(reads, writes)
        if skip_own:
            deps = [d for d in deps if d[0] is not owner.dsem[kind]]
        if qeng == "pool" and not skip_own and len(self.pool_dmas) >= 3:
            deps.append(self.pool_dmas[-3])
        waits = self._need(qeng, deps)
        owner.dcnt[kind] += 16
        tok = (owner.dsem[kind], owner.dcnt[kind], "dma")
        self.q[qeng].append((waits, fn, (owner.dsem[kind], 16)))
        for r in reads:
            self._addr(r, tok)
        for r in writes:
            r.w = tok
            r.rs = {}
        if final:
            self.final.append(tok)
        if qeng == "pool":
            if skip_own and self.pool_dmas and self.pool_dmas[-1][0] is tok[0]:
                self.pool_dmas[-1] = tok
            else:
                self.pool_dmas.append(tok)

    def finish(self, qeng="sp"):
        waits = self._need(qeng, self.final)
        self.q[qeng].append((waits, None, None))

    def replay(self, name, e):
        for waits, fn, inc in self.q[name]:
            for sem, val in waits:
                e.wait_ge(sem, val)
            if fn is None:
                continue
            ins = fn(e)
            if inc is not None:
                ins.then_inc(inc[0], inc[1])

    def run_block(self):
        with self.nc.Block() as block:
            @block.tensor
            def _(e):
                self.replay("pe", e)

            @block.scalar
            def _(e):
                self.replay("act", e)

            @block.vector
            def _(e):
                self.replay("dve", e)

            @block.gpsimd
            def _(e):
                self.replay("pool", e)

            @block.sync
            def _(e):
                self.replay("sp", e)


class Ring:
    def __init__(self, items):
        self.items = items
        self.i = 0

    def get(self):
        it = self.items[self.i % len(self.items)]
        self.i += 1
        return it


def _consts():
    c = {}
    c["c_ident"] = np.eye(128, dtype=np.float32)
    perm = np.zeros((128, 128), np.float32)
    for m in range(128):
        d = m % 64
        if d < 8:
            perm[m + 8, m] = 1.0
        elif d < 16:
            perm[m - 8, m] = 1.0
    c["c_perm"] = perm
    k = np.arange(128)[:, None]
    q = np.arange(128)[None, :]
    mask = np.concatenate([(q < k), (q >= k)], axis=1).astype(np.float32)
    c["c_mask"] = np.concatenate([mask, mask], axis=1)
    eh = np.zeros((128, 16, 16), np.float32)
    for h in range(16):
        eh[:, h, h] = 1.0
    c["c_eh"] = eh.reshape(128, 256)
    bc = np.zeros((16, 8, 128), np.float32)
    for cc in range(8):
        for p in range(128):
            bc[2 * cc + p // 64, cc, p] = 1.0
    c["c_bc"] = bc.reshape(16, 1024)
    pos = np.arange(SEQ, dtype=np.float32)
    inv_freq = (np.float32(500000.0) ** (-np.arange(0, 16, 2, dtype=np.float32) / np.float32(16))).astype(np.float32)
    ang = (pos[:, None] * inv_freq[None, :]).astype(np.float32)
    cos = np.cos(ang).astype(np.float32)
    sin = np.sin(ang).astype(np.float32)
    C = np.ones((128, SEQ), np.float32)
    Sg = np.zeros((128, SEQ), np.float32)
    for p in range(128):
        d = p % 64
        if d < 8:
            C[p] = cos[:, d]
            Sg[p] = -sin[:, d]
        elif d < 16:
            C[p] = cos[:, d - 8]
            Sg[p] = sin[:, d - 8]
    c["c_ropeC"] = C
    c["c_ropeS"] = Sg
    return c


def build_program(ntiles=NT, same_engine_sync=True, stop_after=None, dump=()):
    nc = bass.Bass("TRN2", target_bir_lowering=False)

    def din(name, shape):
        return nc.dram_tensor(name, shape, F32, kind="ExternalInput").ap()

    x = din("x", [2 * SEQ, D])
    norm_g = din("norm_g", [1, D])
    w_in = din("w_in", [1, D, DIN])
    conv_w = din("conv_w", [1, 4, D])
    conv_b = din("conv_b", [1, D])
    lru_w_a = din("lru_w_a", [1, 8, 128, 128])
    lru_b_a = din("lru_b_a", [1, D])
    lru_w_x = din("lru_w_x", [1, 8, 128, 128])
    lru_b_x = din("lru_b_x", [1, D])
    lru_lambda = din("lru_lambda", [1, D])
    attn_sinks = din("attn_sinks", [1, 16])
    w_rnn_out = din("w_rnn_out", [1, D, D])
    w_attn_out = din("w_attn_out", [1, D, D])
    w_o = din("w_o", [1, D, D])
    final_norm_g = din("final_norm_g", [1, D])
    c_ident = din("c_ident", [128, 128])
    c_perm = din("c_perm", [128, 128])
    c_mask = din("c_mask", [128, 512])
    c_eh = din("c_eh", [128, 256])
    c_bc = din("c_bc", [16, 1024])
    c_ropeC = din("c_ropeC", [128, SEQ])
    c_ropeS = din("c_ropeS", [128, SEQ])
    wstream = nc.dram_tensor("wstream", [NSLOT, 128, 4096], BF16, kind="Internal").ap()
    out = nc.dram_tensor("out", [2 * SEQ, D], F32, kind="ExternalOutput").ap()

    win_v = w_in.rearrange("o (k p) c -> p (o k) c", p=128)
    wro_v = w_rnn_out.rearrange("o (k p) c -> p (o k) c", p=128)
    wao_v = w_attn_out.rearrange("o (k p) c -> p (o k) c", p=128)
    wo_v = w_o.rearrange("o (k p) c -> p (o k) c", p=128)

    with ExitStack() as st:
        S = Sched(nc, st, same_engine_sync=same_engine_sync)

        def sb(name, shape, dt):
            return st.enter_context(nc.sbuf_tensor(name, shape, dt))

        ps = st.enter_context(nc.psum_tensor("ps", [128, 7, 512], F32))
        psT = st.enter_context(nc.psum_tensor("psT", [128, 1024], BF16))
        r_ps = [Res("ps%d" % i) for i in range(7)]
        r_psT = Res("psT")

        ident = sb("ident", [128, 128], BF16); r_ident = Res("ident")
        permb = sb("permb", [128, 128], BF16); r_perm = Res("perm")
        maskb = sb("maskb", [128, 2, 256], BF16); r_mask = Res("mask")
        ehb = sb("ehb", [128, 256], BF16); r_eh = Res("eh")
        bcs = sb("bcs", [16, 1024], F32); r_bc = Res("bc")
        grep = sb("grep", [128, D], F32); r_grep = Res("grep")
        fgrep = sb("fgrep", [128, D], F32); r_fgrep = Res("fgrep")
        mhalf = sb("mhalf", [128, 1], F32); r_mhalf = Res("mhalf")
        vrow = sb("vrow", [64, 128], F32); r_vrow = Res("vrow")
        identf = sb("identf", [128, 128], F32); r_identf = Res("identf")
        vT = sb("vT", [128, 8, 8], F32); r_vT = Res("vT")
        vd = sb("vd", [128, 4, 8], F32); r_vd = Res("vd")
        sinkt = sb("sinkt", [16, 2], F32); r_sink = Res("sink")
        lruA = sb("lruA", [128, 8, 128], BF16); r_lruA = Res("lruA")
        lruX = sb("lruX", [128, 8, 128], BF16); r_lruX = Res("lruX")

        wring = [sb("wring%d" % i, [128, 4096], BF16) for i in range(4)]
        r_wring = [Res("wring%d" % i) for i in range(4)]
        hT = [sb("hT%d" % i, [128, 8, T], BF16) for i in range(2)]
        r_hT = [[Res("hT%d_%d" % (i, b)) for b in range(4)] for i in range(2)]
        yr = sb("yr", [128, 8, T], BF16); r_yr = [Res("yr%d" % i) for i in range(8)]
        ya = sb("ya", [128, 8, T], BF16); r_ya = [Res("ya%d" % i) for i in range(4)]
        qmg = sb("qmg", [128, 8, T], BF16); r_qmg = [Res("qmg%d" % i) for i in range(8)]
        sga = sb("sga", [128, 8, T], BF16); r_sga = [Res("sga%d" % i) for i in range(8)]
        kT = sb("kT", [128, 4, 128 + T], BF16); r_kT = [Res("kT%d" % i) for i in range(4)]
        vtok = sb("vtok", [128, 5, 256], BF16); r_vtok = Res("vtok")
        halo = sb("halo", [128, 8, 4], F32); r_halo = [Res("halo%d" % i) for i in range(8)]
        hst = sb("hst", [128, 8], F32); r_hst = [Res("hst%d" % i) for i in range(8)]
        ropeC = [sb("ropeC%d" % i, [128, T], F32) for i in range(2)]
        ropeS = [sb("ropeS%d" % i, [128, T], F32) for i in range(2)]
        r_rope = [Res("rope%d" % i) for i in range(2)]
        xset = [[sb("x%d_%d" % (i, b), [128, D], F32) for b in range(4)] for i in range(2)]
        r_xset = [[Res("x%d_%d" % (i, b)) for b in range(4)] for i in range(2)]
        xnb = [sb("xnb%d" % i, [128, D], BF16) for i in range(2)]
        r_xnb = [Res("xnb%d" % i) for i in range(2)]
        big = [sb("big%d" % i, [128, D], F32) for i in range(2)]
        r_big = [Res("big%d" % i) for i in range(2)]
        stat = sb("stat", [128, 16], F32)
        r_stat = [Res("stat%d" % i) for i in range(4)]
        def mkring(name, n, shape, dt):
            return Ring([(sb("%s%d" % (name, i), shape, dt), Res("%s%d" % (name, i))) for i in range(n)])
        wk_ue = mkring("wue", 2, [128, 520], F32)
        wk_uc = mkring("wuc", 3, [128, 512], F32)
        wk_sg = mkring("wsg", 3, [128, 512], F32)
        wk = mkring("wk", 8, [128, 512], F32)
        wb_ucb = mkring("wucb", 3, [128, 512], BF16)
        wb = mkring("wb", 3, [128, 512], BF16)
        pT_t = [sb("pT%d" % i, [128, 2, 256], BF16) for i in range(3)]
        pTr = Ring([(pT_t[i], Res("pT%d" % i)) for i in range(3)])
        dsm = sb("dsm", [16, 2, 128], F32); r_dsm = [Res("dsm0"), Res("dsm1")]

        psr = Ring([(i, r_ps[i]) for i in range(7)])

        def pbank(i):
            return ps[:, i, :]

        def cast_load(dst_ap, src_ap, res, reads=(), skip_own=False):
            S.dma(lambda e, d=dst_ap, s=src_ap: e.dma_start(out=d, in_=s), res, reads=reads, writes=[res], qeng="pool",
                  skip_own=skip_own)

        def load(dst_ap, src_ap, res, slow=False):
            S.dma(lambda e, d=dst_ap, s=src_ap, sl=slow: e.dma_start(out=d, in_=s, allow_slow_non_contiguous=sl),
                  res, writes=[res])

        cast_load(ident[:], c_ident, r_ident)
        cast_load(permb[:], c_perm, r_perm)
        cast_load(maskb[:].rearrange("p a m -> p (a m)"), c_mask, r_mask)
        cast_load(ehb[:], c_eh, r_eh)
        cast_load(lruA[:], lru_w_a.rearrange("o n c d -> c (o n) d"), r_lruA)
        cast_load(lruX[:], lru_w_x.rearrange("o n c d -> c (o n) d"), r_lruX)
        load(bcs[:], c_bc, r_bc)
        load(grep[:], norm_g.partition_broadcast(128), r_grep)
        load(fgrep[:], final_norm_g.partition_broadcast(128), r_fgrep)
        vsrc = [conv_w[0, 0:1, :], conv_w[0, 1:2, :], conv_w[0, 2:3, :], conv_w[0, 3:4, :], conv_b, lru_b_a, lru_b_x, lru_lambda]
        for i, v in enumerate(vsrc):
            load(vrow[8 * i:8 * i + 8, :], v.rearrange("o (n p) -> (o n) p", p=128), r_vrow)
        load(identf[:], c_ident, r_identf)
        load(sinkt[:, 0:1], attn_sinks.rearrange("o h -> h o"), r_sink, slow=True)
        S.op("pe", lambda e: e.transpose(ps[:, 0, 0:64], vrow[:, :], identf[0:64, 0:64]), reads=[r_vrow, r_identf], writes=[r_ps[0]])
        S.op("act", lambda e: e.activation(out=vT[:].rearrange("p a b -> p (a b)"), in_=ps[:, 0, 0:64], func=AF.Copy),
             reads=[r_ps[0]], writes=[r_vT])

        S.op("pool", lambda e: e.memset(mhalf[:], -0.5), writes=[r_mhalf])
        S.op("dve", lambda e: e.tensor_scalar(out=vd[:, 0:2, :], in0=vT[:, 5:7, :], scalar1=0.5, scalar2=None, op0=ALU.mult),
             reads=[r_vT], writes=[r_vd])
        S.op("act", lambda e: e.activation(out=vd[:, 2, :], in_=vT[:, 7, :], func=AF.Exp, scale=-1.0), reads=[r_vT], writes=[r_vd])
        S.op("act", lambda e: e.activation(out=vd[:, 3, :], in_=vd[:, 2, :], func=AF.Ln, bias=1.0), reads=[r_vd], writes=[r_vd])
        S.op("dve", lambda e: e.tensor_scalar(out=vd[:, 2, :], in0=vd[:, 3, :], scalar1=-4.0, scalar2=None, op0=ALU.mult),
             reads=[r_vd], writes=[r_vd])
        S.op("dve", lambda e: e.tensor_scalar(out=vd[:, 3, :], in0=vd[:, 3, :], scalar1=-8.0, scalar2=None, op0=ALU.mult),
             reads=[r_vd], writes=[r_vd])
        S.op("act", lambda e: e.activation(out=sinkt[:, 1:2], in_=sinkt[:, 0:1], func=AF.Exp), reads=[r_sink], writes=[r_sink])

        def slot_units(s):
            res = []
            if s < 4:
                res.append((0, 256, win_v[:, :, OFF_U + 2 * s * 128: OFF_U + (2 * s + 2) * 128]))
                res.append((256, 512, win_v[:, :, OFF_G + 2 * s * 128: OFF_G + (2 * s + 2) * 128]))
            elif s == 4:
                for g in range(4):
                    for hf in range(2):
                        res.append((g * 128 + hf * 64, g * 128 + hf * 64 + 64, win_v[:, :, OFF_K + g * 64: OFF_K + (g + 1) * 64]))
            elif s == 5:
                res.append((0, 256, win_v[:, :, OFF_V:OFF_V + 256]))
                res.append((256, 512, win_v[:, :, OFF_Q:OFF_Q + 256]))
            elif s == 6:
                res.append((0, 512, win_v[:, :, OFF_Q + 256:OFF_Q + 768]))
            elif s == 7:
                res.append((0, 256, win_v[:, :, OFF_Q + 768:OFF_Q + 1024]))
                res.append((256, 512, win_v[:, :, OFF_GA:OFF_GA + 256]))
            elif s == 8:
                res.append((0, 512, win_v[:, :, OFF_GA + 256:OFF_GA + 768]))
            elif s == 9:
                res.append((0, 256, win_v[:, :, OFF_GA + 768:OFF_GA + 1024]))
            elif s < 18:
                i, y = (s - 10) // 2, (s - 10) % 2
                if y == 0:
                    res.append((0, 256, wro_v[:, :, 2 * i * 128:(2 * i + 2) * 128]))
                    res.append((256, 512, wao_v[:, :, 2 * i * 128:(2 * i + 2) * 128]))
                else:
                    res.append((0, 256, win_v[:, :, OFF_MR + 2 * i * 128: OFF_MR + (2 * i + 2) * 128]))
                    res.append((256, 512, win_v[:, :, OFF_MA + 2 * i * 128: OFF_MA + (2 * i + 2) * 128]))
            else:
                half = s - 18
                res.append((0, 512, wo_v[:, :, half * 512:(half + 1) * 512]))
            return res

        def kview(buf):
            return buf[:].rearrange("p (k c) -> p k c", k=8)

        wslot_i = [0]
        r_wstream = [Res("wstream%d" % i) for i in range(NSLOT)]

        def fetch_slot(t, s):
            i = wslot_i[0] % 4
            wslot_i[0] += 1
            buf, res = wring[i], r_wring[i]
            if t == 0:
                for ii, (c0, c1, src) in enumerate(slot_units(s)):
                    cast_load(kview(buf)[:, :, c0:c1], src, res, skip_own=(ii > 0))
                S.dma(lambda e, b=buf, s=s: e.dma_start(out=wstream[s], in_=b[:]), res, reads=[res], writes=[r_wstream[s]])
            else:
                S.dma(lambda e, b=buf, s=s: e.dma_start(out=b[:], in_=wstream[s]), res, reads=[r_wstream[s]], writes=[res])
            return buf, res

        def unit(buf, j):
            return kview(buf)[:, :, j * 128:(j + 1) * 128]

        def mm_group(out_ap, lhs_fn, rhs_fn, nk, reads, writes):
            def fn(e, out_ap=out_ap, lhs_fn=lhs_fn, rhs_fn=rhs_fn, nk=nk):
                ins = None
                for k in range(nk):
                    ins = e.matmul(out_ap, lhsT=lhs_fn(k), rhs=rhs_fn(k), start=(k == 0), stop=(k == nk - 1))
                return ins
            S.op("pe", fn, reads=reads, writes=writes)

        def load_x(t):
            xs, rx = xset[t % 2], r_xset[t % 2]
            for b in range(4):
                r0 = t * T + b * 128
                S.dma(lambda e, d=xs[b], r0=r0: e.dma_start(out=d[:], in_=x[r0:r0 + 128, :]), rx[b], writes=[rx[b]])
            pos0 = (t % TPS) * T
            i = t % 2
            S.dma(lambda e, i=i, p=pos0: e.dma_start(out=ropeC[i][:], in_=c_ropeC[:, p:p + T]), r_rope[i], writes=[r_rope[i]])
            S.dma(lambda e, i=i, p=pos0: e.dma_start(out=ropeS[i][:], in_=c_ropeS[:, p:p + T]), r_rope[i], writes=[r_rope[i]])

        def rstd_ops(sidx, rs):
            c0 = 4 * sidx
            S.op("pool", lambda e, c0=c0: e.tensor_scalar(out=stat[:, c0 + 1:c0 + 2], in0=stat[:, c0:c0 + 1], scalar1=1.0 / D,
                                                          scalar2=EPS, op0=ALU.mult, op1=ALU.add), reads=[rs], writes=[rs])
            S.op("pool", lambda e, c0=c0: e.tensor_tensor(out=stat[:, c0 + 1:c0 + 2], in0=stat[:, c0 + 1:c0 + 2], in1=mhalf[:],
                                                          op=ALU.pow), reads=[rs, r_mhalf], writes=[rs])

        stat_i = [0]

        def stage_A(t, blocks=(0, 1, 2, 3), part="both"):
            hTt, rh = hT[t % 2], r_hT[t % 2]
            xs, rx = xset[t % 2], r_xset[t % 2]
            for b in blocks:
                xb_, rxb = xnb[b % 2], r_xnb[b % 2]
                if part in ("both", "tr"):
                    def tr(e, xb_=xb_):
                        ins = None
                        for c in range(8):
                            ins = e.transpose(psT[:, c * 128:(c + 1) * 128], xb_[:, c * 128:(c + 1) * 128], ident[:])
                        return ins
                if part == "tr":
                    S.op("pe", tr, reads=[rxb, r_ident], writes=[r_psT])
                    S.op("act", lambda e, hTt=hTt, b=b: e.activation(
                        out=hTt[:, :, b * 128:(b + 1) * 128], in_=psT[:].rearrange("p (c m) -> p c m", c=8), func=AF.Copy),
                        reads=[r_psT], writes=[rh[b]])
                    continue
                si = stat_i[0] % 4
                stat_i[0] += 1
                rs = r_stat[si]
                bg, rbg = big[b % 2], r_big[b % 2]
                S.op("act", lambda e, bg=bg, xb=xs[b]: e.activation(out=bg[:], in_=xb[:], func=AF.Square), reads=[rx[b]], writes=[rbg])
                S.op("dve", lambda e, bg=bg, si=si: e.tensor_reduce(out=stat[:, 4 * si:4 * si + 1], in_=bg[:], axis=AX.X, op=ALU.add),
                     reads=[rbg], writes=[rs])
                rstd_ops(si, rs)
                S.op("dve", lambda e, xb_=xb_, xb=xs[b], si=si: e.scalar_tensor_tensor(
                    out=xb_[:], in0=xb[:], scalar=stat[:, 4 * si + 1:4 * si + 2], in1=grep[:], op0=ALU.mult, op1=ALU.mult),
                    reads=[rx[b], rs, r_grep], writes=[rxb])
                if part == "elem":
                    continue
                S.op("pe", tr, reads=[rxb, r_ident], writes=[r_psT])
                S.op("act", lambda e, hTt=hTt, b=b: e.activation(
                    out=hTt[:, :, b * 128:(b + 1) * 128], in_=psT[:].rearrange("p (c m) -> p c m", c=8), func=AF.Copy),
                    reads=[r_psT], writes=[rh[b]])

        def make_B(t):
            hTt, rh = hT[t % 2], r_hT[t % 2]
            first = (t % TPS == 0)
            bstate = {}

            def front(n, buf, rbuf, j0):
                wu, wg = unit(buf, j0), unit(buf, 2 + j0)
                bu, rbu = psr.get()
                mm_group(pbank(bu), lambda k, wu=wu: wu[:, k, :], lambda k: hTt[:, k, :], 8, reads=[rbuf] + rh, writes=[rbu])
                bgp, rbg_ = psr.get()
                mm_group(pbank(bgp), lambda k, wg=wg: wg[:, k, :], lambda k: hTt[:, k, :], 8, reads=[rbuf] + rh, writes=[rbg_])
                ue, rue = wk_ue.get()
                if first:
                    S.op("dve", lambda e, ue=ue: e.memset(ue[:, 0:3], 0.0), writes=[rue])
                else:
                    S.op("pool", lambda e, ue=ue, n=n: e.tensor_copy(out=ue[:, 0:3], in_=halo[:, n, 0:3]),
                         reads=[r_halo[n]], writes=[rue])
                S.op("act", lambda e, ue=ue, bu=bu: e.activation(out=ue[:, 3:3 + T], in_=pbank(bu), func=AF.Copy),
                     reads=[rbu, rue], writes=[rue])
                S.op("pool", lambda e, ue=ue, n=n: e.tensor_copy(out=halo[:, n, 0:3], in_=ue[:, T:T + 3]),
                     reads=[rue], writes=[r_halo[n]])
                sg, rsg = wk_sg.get()
                S.op("act", lambda e, sg=sg, bgp=bgp: e.activation(out=sg[:, 0:T], in_=pbank(bgp), func=AF.Tanh, scale=0.5),
                     reads=[rbg_], writes=[rsg])
                S.op("dve", lambda e, sg=sg, bgp=bgp: e.scalar_tensor_tensor(
                    out=sg[:, 0:T], in0=sg[:, 0:T], scalar=1.0, in1=pbank(bgp), op0=ALU.add, op1=ALU.mult),
                    reads=[rsg, rbg_], writes=[rsg])
                uc, ruc = wk_uc.get()
                S.op("pool", lambda e, ue=ue, uc=uc, n=n: e.tensor_scalar(
                    out=uc[:, 0:T], in0=ue[:, 3:3 + T], scalar1=vT[:, 3, n:n + 1], scalar2=vT[:, 4, n:n + 1], op0=ALU.mult, op1=ALU.add),
                    reads=[rue, r_vT], writes=[ruc])
                for j in (2,):
                    cq, rcq = wk.get()
                    S.op("pool", lambda e, ue=ue, cq=cq, n=n, j=j: e.tensor_scalar(
                        out=cq[:, 0:T], in0=ue[:, j:j + T], scalar1=vT[:, j, n:n + 1], scalar2=0.0, op0=ALU.mult, op1=ALU.add),
                        reads=[rue, r_vT], writes=[rcq])
                    S.op("pool", lambda e, cq=cq, uc=uc: e.tensor_tensor(out=uc[:, 0:T], in0=uc[:, 0:T], in1=cq[:, 0:T], op=ALU.add),
                         reads=[rcq, ruc], writes=[ruc])
                for j in (1, 0):
                    S.op("dve", lambda e, ue=ue, uc=uc, n=n, j=j: e.scalar_tensor_tensor(
                        out=uc[:, 0:T], in0=ue[:, j:j + T], scalar=vT[:, j, n:n + 1], in1=uc[:, 0:T], op0=ALU.mult, op1=ALU.add),
                        reads=[rue, r_vT, ruc], writes=[ruc])
                ucb, rucb = wb_ucb.get()
                return (n, uc, ruc, ucb, rucb, sg, rsg)

            def back(n, uc, ruc, ucb, rucb, sg, rsg):
                br, rbr = psr.get()
                mm_group(pbank(br), lambda k, n=n: lruA[:, n, :], lambda k, ucb=ucb: ucb[:], 1, reads=[r_lruA, rucb], writes=[rbr])
                bi, rbi = psr.get()
                mm_group(pbank(bi), lambda k, n=n: lruX[:, n, :], lambda k, ucb=ucb: ucb[:], 1, reads=[r_lruX, rucb], writes=[rbi])
                tr_, rtr = wk.get()
                S.op("act", lambda e, tr_=tr_, br=br, n=n: e.activation(out=tr_[:, 0:T], in_=pbank(br), func=AF.Tanh, scale=0.5,
                                                                       bias=vd[:, 0, n:n + 1]), reads=[rbr, r_vd], writes=[rtr])
                iu, riu = wk.get()
                S.op("act", lambda e, iu=iu, bi=bi, n=n: e.activation(out=iu[:, 0:T], in_=pbank(bi), func=AF.Tanh, scale=0.5,
                                                                     bias=vd[:, 1, n:n + 1]), reads=[rbi, r_vd], writes=[riu])
                a_, ra = wk.get()
                S.op("act", lambda e, a_=a_, tr_=tr_, n=n: e.activation(out=a_[:, 0:T], in_=tr_[:, 0:T], func=AF.Exp,
                                                                       scale=vd[:, 2, n:n + 1], bias=vd[:, 2, n:n + 1]),
                     reads=[rtr, r_vd], writes=[ra])
                s_, rs_ = wk.get()
                S.op("act", lambda e, s_=s_, tr_=tr_, n=n: e.activation(out=s_[:, 0:T], in_=tr_[:, 0:T], func=AF.Exp,
                                                                       scale=vd[:, 3, n:n + 1], bias=vd[:, 3, n:n + 1]),
                     reads=[rtr, r_vd], writes=[rs_])
                S.op("act", lambda e, s_=s_: e.activation(out=s_[:, 0:T], in_=s_[:, 0:T], func=AF.Sqrt, scale=-1.0, bias=1.0),
                     reads=[rs_], writes=[rs_])
                S.op("dve", lambda e, iu=iu, uc=uc: e.scalar_tensor_tensor(out=iu[:, 0:T], in0=iu[:, 0:T], scalar=1.0, in1=uc[:, 0:T],
                                                                          op0=ALU.add, op1=ALU.mult), reads=[riu, ruc], writes=[riu])
                S.op("dve", lambda e, iu=iu, s_=s_: e.scalar_tensor_tensor(out=iu[:, 0:T], in0=s_[:, 0:T], scalar=0.5, in1=iu[:, 0:T],
                                                                          op0=ALU.mult, op1=ALU.mult), reads=[riu, rs_], writes=[riu])
                h_, rh_ = wk.get()
                if first:
                    S.op("dve", lambda e, h_=h_, a_=a_, iu=iu: e.tensor_tensor_scan(
                        out=h_[:, 0:T], data0=a_[:, 0:T], data1=iu[:, 0:T], initial=0.0, op0=ALU.mult, op1=ALU.add),
                        reads=[ra, riu], writes=[rh_])
                else:
                    S.op("dve", lambda e, h_=h_, a_=a_, iu=iu, n=n: e.tensor_tensor_scan(
                        out=h_[:, 0:T], data0=a_[:, 0:T], data1=iu[:, 0:T], initial=hst[:, n:n + 1], op0=ALU.mult, op1=ALU.add),
                        reads=[ra, riu, r_hst[n]], writes=[rh_])
                S.op("pool", lambda e, h_=h_, n=n: e.tensor_copy(out=hst[:, n:n + 1], in_=h_[:, T - 1:T]),
                     reads=[rh_], writes=[r_hst[n]])
                S.op("dve", lambda e, h_=h_, sg=sg, n=n: e.scalar_tensor_tensor(
                    out=yr[:, n, :], in0=h_[:, 0:T], scalar=0.5, in1=sg[:, 0:T], op0=ALU.mult, op1=ALU.mult),
                    reads=[rh_, rsg], writes=[r_yr[n]])

            def front_n(n):
                if n % 2 == 0:
                    bstate["buf"] = fetch_slot(t, n // 2)
                buf, rbuf = bstate["buf"]
                return front(n, buf, rbuf, n % 2)

            def front_b(n, uc, ruc, ucb, rucb, sg, rsg):
                S.op("act", lambda e, uc=uc, ucb=ucb: e.activation(out=ucb[:], in_=uc[:, 0:T], func=AF.Copy), reads=[ruc], writes=[rucb])
            return front_n, front_b, back

        def rope_chain(pb, rpb, dst_ap, rdst, t):
            i = t % 2
            raw, rraw = wb.get()
            S.op("act", lambda e, raw=raw, pb=pb: e.activation(out=raw[:], in_=pbank(pb), func=AF.Copy), reads=[rpb], writes=[rraw])
            return (raw, rraw, dst_ap, rdst, i)

        def rope_finish(raw, rraw, dst_ap, rdst, i):
            p2, rp2 = psr.get()
            mm_group(pbank(p2), lambda k: permb[:], lambda k, raw=raw: raw[:], 1, reads=[r_perm, rraw], writes=[rp2])
            t2, rt2 = wk.get()
            S.op("pool", lambda e, t2=t2, raw=raw, i=i: e.tensor_tensor(out=t2[:, 0:T], in0=raw[:], in1=ropeC[i][:], op=ALU.mult),
                 reads=[rraw, r_rope[i]], writes=[rt2])
            t1, rt1 = wk.get()
            S.op("dve", lambda e, t1=t1, p2=p2, i=i: e.tensor_tensor(out=t1[:, 0:T], in0=pbank(p2), in1=ropeS[i][:], op=ALU.mult),
                 reads=[rp2, r_rope[i]], writes=[rt1])
            S.op("dve", lambda e, t1=t1, t2=t2, dst_ap=dst_ap: e.tensor_tensor(out=dst_ap, in0=t1[:, 0:T], in1=t2[:, 0:T], op=ALU.add),
                 reads=[rt1, rt2], writes=[rdst])

        def make_C(t):
            hTt, rh = hT[t % 2], r_hT[t % 2]
            units = [("kd", g) for g in range(4)] + [("v", 0), ("v", 1)] + [("q", c) for c in range(8)] + \
                    [("ga", c) for c in range(8)] + [("pad", 0), ("pad", 1)]
            pend = []
            cstate = {}

            def unit_fn(ui):
                kind, i = units[ui]
                j = ui % 4
                if j == 0:
                    cstate["buf"] = fetch_slot(t, 4 + ui // 4)
                buf, rbuf = cstate["buf"]
                if kind in ("kd", "q"):
                    w = unit(buf, j)
                    pb, rpb = psr.get()
                    mm_group(pbank(pb), lambda k, w=w: w[:, k, :], lambda k: hTt[:, k, :], 8, reads=[rbuf] + rh, writes=[rpb])
                    if kind == "kd":
                        dst, rd = kT[:, i, 128:128 + T], r_kT[i]
                    else:
                        dst, rd = qmg[:, i, :], r_qmg[i]
                    pend.append(rope_chain(pb, rpb, dst, rd, t))
                    if len(pend) > 2:
                        rope_finish(*pend.pop(0))
                elif kind == "v" and i == 0:
                    wv = kview(buf)[:, :, j * 128:(j + 2) * 128]
                    for pr in range(2):
                        pb, rpb = psr.get()

                        def fn(e, pb=pb, pr=pr, wv=wv):
                            ins = None
                            for bb in range(2):
                                b = pr * 2 + bb
                                for k in range(8):
                                    ins = e.matmul(ps[:, pb, bb * 256:(bb + 1) * 256], lhsT=hTt[:, k, b * 128:(b + 1) * 128],
                                                   rhs=wv[:, k, :], start=(k == 0), stop=(k == 7))
                            return ins
                        S.op("pe", fn, reads=[rbuf] + rh, writes=[rpb])
                        S.op("act", lambda e, pb=pb, pr=pr: e.activation(
                            out=vtok[:, 1 + 2 * pr:3 + 2 * pr, :], in_=pbank(pb).rearrange("p (b m) -> p b m", b=2), func=AF.Copy),
                            reads=[rpb], writes=[r_vtok])
                elif kind == "ga":
                    w = unit(buf, j)
                    pb, rpb = psr.get()
                    mm_group(pbank(pb), lambda k, w=w: w[:, k, :], lambda k: hTt[:, k, :], 8, reads=[rbuf] + rh, writes=[rpb])
                    tg, rtg = wk.get()
                    S.op("act", lambda e, tg=tg, pb=pb: e.activation(out=tg[:, 0:T], in_=pbank(pb), func=AF.Tanh, scale=0.5),
                         reads=[rpb], writes=[rtg])
                    S.op("dve", lambda e, tg=tg, pb=pb, i=i: e.scalar_tensor_tensor(
                        out=sga[:, i, :], in0=tg[:, 0:T], scalar=1.0, in1=pbank(pb), op0=ALU.add, op1=ALU.mult),
                        reads=[rtg, rpb], writes=[r_sga[i]])

            def flush():
                while pend:
                    rope_finish(*pend.pop(0))
            return unit_fn, flush, len(units)

        def stage_BC(t, dsteps=None):
            frontB, frontB2, backB = make_B(t)
            unitC, flushC, NU = make_C(t)
            pendB = []
            ui = 0
            NP = 10
            for p_ in range(NP):
                if p_ < 8:
                    pendB.append(frontB(p_))
                if p_ >= 2:
                    backB(*pendB.pop(0))
                if p_ < 8:
                    frontB2(*pendB[-1])
                for _ in range(3):
                    if ui < NU:
                        unitC(ui)
                        ui += 1
                        if ui == NU:
                            flushC()
                if ui == NU and dsteps is not None and p_ >= 8:
                    psr.items = [(i, r_ps[i]) for i in (0, 1, 5, 6)]
                    for _ in range(5):
                        if dsteps:
                            dsteps.pop(0)()
            psr.items = [(i, r_ps[i]) for i in range(7)]

        def stage_D(t):
            ts = t % TPS
            S_BANKS = [(0, 0), (5, 0)]
            sring = Ring(S_BANKS)
            DEN, VALS = 2, 3
            rbv = psT[:].bitcast(F32)
            post2_pending = [None]
            steps = []
            for jq in range(4):
                first = (ts == 0 and jq == 0)
                lo = 128 if first else 0
                pend = []

                def s1(i, jq=jq, first=first, lo=lo):
                    c, g = i, i // 2
                    bk, _hf = sring.get()
                    rbk = [r_ps[bk], r_ps[bk + 1]]

                    def fn(e, bk=bk, g=g, c=c):
                        ins = None
                        for hh in range(2):
                            hb = hh * 64
                            if not first:
                                ins = e.matmul(ps[:, bk + hh, 0:128], lhsT=kT[hb:hb + 64, g, jq * 128:(jq + 1) * 128],
                                               rhs=qmg[hb:hb + 64, c, jq * 128:(jq + 1) * 128], start=True, stop=True)
                            ins = e.matmul(ps[:, bk + hh, 128:256], lhsT=kT[hb:hb + 64, g, 128 + jq * 128:128 + (jq + 1) * 128],
                                           rhs=qmg[hb:hb + 64, c, jq * 128:(jq + 1) * 128], start=True, stop=True)
                        return ins
                    S.op("pe", fn, reads=[r_kT[g], r_qmg[c]], writes=[rbk])
                    pt, rpt = pTr.get()
                    S.op("act", lambda e, pt=pt, bk=bk: e.activation(
                        out=pt[:, :, lo:256], in_=ps[:, bk:bk + 2, lo:256], func=AF.Exp, scale=0.125),
                        reads=[rbk], writes=[rpt])
                    S.op("pool", lambda e, pt=pt: e.tensor_tensor(out=pt[:, :, lo:256], in0=pt[:, :, lo:256], in1=maskb[:, :, lo:256], op=ALU.mult),
                         reads=[rpt, r_mask], writes=[rpt])
                    return (i, pt, rpt)

                def s4(i, pt, rpt, jq=jq, first=first):
                    c, g = i, i // 2

                    def fn(e, pt=pt, c=c, g=g):
                        ins = None
                        for hh in range(2):
                            h = 2 * c + hh
                            hb = hh * 64
                            vo = ps[hb:hb + 64, VALS + c // 4, (c % 4) * 128:(c % 4 + 1) * 128]
                            if not first:
                                e.matmul(vo, lhsT=vtok[:, jq, g * 64:(g + 1) * 64], rhs=pt[:, hh, 0:128], start=True, stop=False)
                            e.matmul(vo, lhsT=vtok[:, jq + 1, g * 64:(g + 1) * 64], rhs=pt[:, hh, 128:256], start=first, stop=True)
                            if not first:
                                ins = e.matmul(ps[0:16, DEN, 0:256], lhsT=ehb[:, h * 16:(h + 1) * 16], rhs=pt[:, hh, 0:256], start=(h == 0),
                                               stop=(h == 15))
                            else:
                                ins = e.matmul(ps[0:16, DEN, 128:256], lhsT=ehb[:, h * 16:(h + 1) * 16], rhs=pt[:, hh, 128:256],
                                               start=(h == 0), stop=(h == 15))
                        return ins
                    S.op("pe", fn, reads=[rpt, r_vtok, r_eh], writes=[r_ps[VALS], r_ps[VALS + 1], r_ps[DEN]])

                def post1(jq=jq, first=first):
                    di = jq % 2
                    bg, rbg = big[0], r_big[0]
                    S.op("act", lambda e, bg=bg: e.activation(out=bg[:].rearrange("p (a m) -> p a m", a=2), in_=ps[:, VALS:VALS + 2, :],
                                                              func=AF.Copy), reads=[r_ps[VALS], r_ps[VALS + 1]], writes=[rbg])
                    S.op("dve", lambda e, di=di: e.tensor_scalar(out=dsm[:, di, :], in0=ps[0:16, DEN, 128:256], scalar1=sinkt[:, 1:2],
                                                                  scalar2=None, op0=ALU.add), reads=[r_ps[DEN], r_sink], writes=[r_dsm[di]])
                    if not first:
                        S.op("dve", lambda e, di=di: e.tensor_tensor(out=dsm[:, di, :], in0=dsm[:, di, :], in1=ps[0:16, DEN, 0:128],
                                                                      op=ALU.add), reads=[r_ps[DEN], r_dsm[di]], writes=[r_dsm[di]])
                    S.op("dve", lambda e, di=di: e.reciprocal(out=dsm[:, di, :], in_=dsm[:, di, :]), reads=[r_dsm[di]], writes=[r_dsm[di]])

                def post2(jq=jq):
                    di = jq % 2
                    bg, rbg = big[0], r_big[0]
                    b1, rb1 = big[1], r_big[1]
                    for hf in range(2):
                        def fnb(e, di=di, hf=hf):
                            ins = None
                            for cc in range(4):
                                c = hf * 4 + cc
                                ins = e.matmul(rbv[:, cc * 128:(cc + 1) * 128], lhsT=bcs[:, c * 128:(c + 1) * 128],
                                               rhs=dsm[:, di, :], start=True, stop=True)
                            return ins
                        S.op("pe", fnb, reads=[r_bc, r_dsm[di]], writes=[r_psT])
                        S.op("dve", lambda e, bg=bg, b1=b1, hf=hf: e.tensor_tensor(
                            out=b1[:, hf * 512:(hf + 1) * 512], in0=rbv, in1=bg[:, hf * 512:(hf + 1) * 512], op=ALU.mult),
                            reads=[r_psT, rbg], writes=[rb1])
                    S.op("dve", lambda e, b1=b1, jq=jq: e.scalar_tensor_tensor(
                        out=ya[:, :, jq * 128:(jq + 1) * 128], in0=b1[:].rearrange("p (c m) -> p c m", c=8), scalar=0.5,
                        in1=sga[:, :, jq * 128:(jq + 1) * 128], op0=ALU.mult, op1=ALU.mult),
                        reads=[rb1] + r_sga, writes=[r_ya[jq]])

                SK = 1

                def hstep(i, s1=s1, s4=s4, pend=pend):
                    if i < 8:
                        pend.append(s1(i))
                    if i >= SK:
                        s4(*pend.pop(0))
                    if i == 3 and post2_pending[0] is not None:
                        post2_pending[0]()
                        post2_pending[0] = None
                for i in range(8 + SK):
                    steps.append(lambda i=i, hstep=hstep: hstep(i))

                def pstep(post1=post1, post2=post2):
                    post1()
                    post2_pending[0] = post2
                steps.append(pstep)

            def fin():
                post2_pending[0]()
                S.op("pool", lambda e: e.tensor_copy(out=kT[:, :, 0:128], in_=kT[:, :, T:T + 128]), reads=r_kT, writes=r_kT)
                S.op("pool", lambda e: e.tensor_copy(out=vtok[:, 0, :], in_=vtok[:, 4, :]), reads=[r_vtok], writes=[r_vtok])
            steps.append(fin)
            return steps

        def stage_E(t, a_next=None):
            hTt, rh = hT[t % 2], r_hT[t % 2]
            ebuf = {}
            for f in range(8):
                if a_next is not None:
                    stage_A(a_next, blocks=(f // 2,), part=("elem" if f % 2 == 0 else "tr"))
                if f % 2 == 0:
                    ebuf["x"] = fetch_slot(t, 10 + f)
                    ebuf["y"] = fetch_slot(t, 11 + f)
                (bx, rbx), (by, rby) = ebuf["x"], ebuf["y"]
                rbuf = [rbx, rby]
                wro, wao, wmr, wma = unit(bx, f % 2), unit(bx, 2 + f % 2), unit(by, f % 2), unit(by, 2 + f % 2)
                pc, rpc = psr.get()
                mm_group(pbank(pc), lambda k, w=wmr: w[:, k, :], lambda k: hTt[:, k, :], 8, reads=[rbuf] + rh, writes=[rpc])
                pd, rpd = psr.get()
                mm_group(pbank(pd), lambda k, w=wma: w[:, k, :], lambda k: hTt[:, k, :], 8, reads=[rbuf] + rh, writes=[rpd])
                pa, rpa = psr.get()
                mm_group(pbank(pa), lambda k, w=wro: w[:, k, :], lambda k: yr[:, k, :], 8, reads=[rbuf] + r_yr, writes=[rpa])
                pb, rpb = psr.get()
                mm_group(pbank(pb), lambda k, w=wao: w[:, k, :], lambda k: ya[:, k, :], 8, reads=[rbuf] + r_ya, writes=[rpb])
                tc_, rtc = wk.get()
                S.op("act", lambda e, tc_=tc_, pc=pc: e.activation(out=tc_[:, 0:T], in_=pbank(pc), func=AF.Tanh, scale=0.5),
                     reads=[rpc], writes=[rtc])
                td_, rtd = wk.get()
                S.op("act", lambda e, td_=td_, pd=pd: e.activation(out=td_[:, 0:T], in_=pbank(pd), func=AF.Tanh, scale=0.5),
                     reads=[rpd], writes=[rtd])
                S.op("dve", lambda e, tc_=tc_, pa=pa: e.scalar_tensor_tensor(out=tc_[:, 0:T], in0=tc_[:, 0:T], scalar=1.0, in1=pbank(pa),
                                                                            op0=ALU.add, op1=ALU.mult), reads=[rtc, rpa], writes=[rtc])
                S.op("dve", lambda e, td_=td_, pb=pb: e.scalar_tensor_tensor(out=td_[:, 0:T], in0=td_[:, 0:T], scalar=1.0, in1=pbank(pb),
                                                                            op0=ALU.add, op1=ALU.mult), reads=[rtd, rpb], writes=[rtd])
                S.op("pool", lambda e, tc_=tc_, td_=td_, f=f: e.tensor_tensor(out=qmg[:, f, :], in0=tc_[:, 0:T], in1=td_[:, 0:T], op=ALU.add),
                     reads=[rtc, rtd], writes=[r_qmg[f]])

        def stage_F(t):
            xs, rx = xset[t % 2], r_xset[t % 2]
            bufs = [fetch_slot(t, 18), fetch_slot(t, 19)]
            pairs = [(0, 1), (2, 3), (4, 5)]
            for b in range(4):
                p0, p1 = pairs[b % 3]

                def fn(e, b=b, p0=p0):
                    ins = None
                    for half in range(2):
                        w = bufs[half][0][:].rearrange("p (k m) -> p k m", k=8)
                        for k in range(8):
                            ins = e.matmul(ps[:, p0 + half, :], lhsT=qmg[:, k, b * 128:(b + 1) * 128], rhs=w[:, k, :],
                                           start=(k == 0), stop=(k == 7))
                    return ins
                S.op("pe", fn, reads=[bufs[0][1], bufs[1][1]] + r_qmg, writes=[r_ps[p0], r_ps[p1]])
                xb = xs[b]
                xv = xb[:].rearrange("p (a m) -> p a m", a=2)
                S.op("dve", lambda e, xv=xv, p0=p0: e.scalar_tensor_tensor(out=xv, in0=ps[:, p0:p0 + 2, :], scalar=0.5, in1=xv,
                                                                          op0=ALU.mult, op1=ALU.add),
                     reads=[r_ps[p0], r_ps[p1], rx[b]], writes=[rx[b]])
                si = stat_i[0] % 4
                stat_i[0] += 1
                rs = r_stat[si]
                bg, rbg = big[b % 2], r_big[b % 2]
                S.op("act", lambda e, bg=bg, xb=xb: e.activation(out=bg[:], in_=xb[:], func=AF.Square), reads=[rx[b]], writes=[rbg])
                S.op("dve", lambda e, bg=bg, si=si: e.tensor_reduce(out=stat[:, 4 * si:4 * si + 1], in_=bg[:], axis=AX.X, op=ALU.add),
                     reads=[rbg], writes=[rs])
                rstd_ops(si, rs)
                S.op("dve", lambda e, xb=xb, si=si: e.scalar_tensor_tensor(
                    out=xb[:], in0=xb[:], scalar=stat[:, 4 * si + 1:4 * si + 2], in1=fgrep[:], op0=ALU.mult, op1=ALU.mult),
                    reads=[rx[b], rs, r_fgrep], writes=[rx[b]])
                r0 = t * T + b * 128
                S.dma(lambda e, xb=xb, r0=r0: e.dma_start(out=out[r0:r0 + 128, :], in_=xb[:]), rx[b], reads=[rx[b]], qeng="pool", final=True)

        load_x(0)
        stage_A(0)
        stop = tuple(stop_after) if stop_after is not None else None
        for t in range(ntiles):
            more = (t + 1 < ntiles)
            dsteps = stage_D(t)
            if stop == ("B", t) or stop == ("C", t):
                stage_BC(t)
                break
            stage_BC(t, dsteps)
            if more:
                load_x(t + 1)
            while dsteps:
                dsteps.pop(0)()
            if stop == ("D", t):
                break
            stage_E(t, a_next=(t + 1 if more else None))
            if stop == ("E", t):
                break
            stage_F(t)
            if stop == ("F", t):
                break
        dumpable = {
            "hT0": (hT[0], r_hT[0], [128, 8, T], BF16), "yr": (yr, r_yr, [128, 8, T], BF16), "ya": (ya, r_ya, [128, 8, T], BF16),
            "qmg": (qmg, r_qmg, [128, 8, T], BF16), "sga": (sga, r_sga, [128, 8, T], BF16), "kT": (kT, r_kT, [128, 4, 128 + T], BF16),
            "vtok": (vtok, [r_vtok], [128, 5, 256], BF16), "vT": (vT, [r_vT], [128, 8, 8], F32), "vd": (vd, [r_vd], [128, 4, 8], F32),
            "x00": (xset[0][0], [r_xset[0][0]], [128, D], F32),
        }
        for name in dump:
            tens, rl, shape, dt = dumpable[name]
            dd = nc.dram_tensor("dbg_" + name, shape, dt, kind="ExternalOutput").ap()
            S.dma(lambda e, dd=dd, tens=tens: e.dma_start(out=dd, in_=tens[:]), rl[0], reads=rl, qeng="pool", final=True)
        S.finish("pool")
        S.run_block()
    return nc


def make_in_maps(inputs):
    consts = _consts()
    maps = []
    xs = np.ascontiguousarray(inputs["x"], dtype=np.float32)
    for c in range(NCORES):
        m = {"x": xs[2 * c:2 * c + 2].reshape(2 * SEQ, D)}
        for k in ("norm_g", "w_in", "conv_w", "conv_b", "lru_w_a", "lru_b_a", "lru_w_x", "lru_b_x", "lru_lambda",
                  "attn_sinks", "w_rnn_out", "w_attn_out", "w_o"):
            m[k] = np.ascontiguousarray(inputs[k], dtype=np.float32)
        m["final_norm_g"] = np.ascontiguousarray(inputs["final_norm_g"], dtype=np.float32).reshape(1, D)
        m.update(consts)
        maps.append(m)
    return maps


def kernel(**inputs):
    nc = build_program()
    in_maps = make_in_maps(inputs)
    res = run_bass_kernel_spmd(nc, in_maps, core_ids=list(range(NCORES)))
    outs = [np.asarray(r["out"], dtype=np.float32).reshape(2, SEQ, D) for r in res.results]
    return np.concatenate(outs, axis=0)
```

```python
import math
from contextlib import ExitStack

import numpy as np

import concourse.bass as bass
import concourse.mybir as mybir
from concourse.bass_utils import run_bass_kernel_spmd

F32 = mybir.dt.float32
BF16 = mybir.dt.bfloat16
AF = mybir.ActivationFunctionType
ALU = mybir.AluOpType
AX = mybir.AxisListType

NCORES = 8
SEQ = 2048
D = 1024
DIN = 6656
T = 512
TPS = SEQ // T
NT = 2 * TPS
OFF_U, OFF_G, OFF_Q, OFF_K, OFF_V, OFF_GA, OFF_MR, OFF_MA = 0, 1024, 2048, 3072, 3328, 3584, 4608, 5632
NSLOT = 20
EPS = 1e-6


class Res:
    __slots__ = ("name", "w", "rs", "dsem", "dcnt")

    def __init__(self, name):
        self.name = name
        self.w = None
        self.rs = {}
        self.dsem = None
        self.dcnt = 0


class Sched:
    CE = ("pe", "act", "dve", "pool")

    def __init__(self, nc, stack, same_engine_sync=True):
        self.nc = nc
        self.stack = stack
        self.q = {e: [] for e in self.CE + ("sp",)}
        self.sem = {e: stack.enter_context(nc.semaphore("s_" + e)) for e in self.CE}
        self.cnt = {e: 0 for e in self.CE}
        self.waited = {}
        self.same = same_engine_sync
        self.nsem = 0
        self.final = []
        self.pool_dmas = []

    @staticmethod
    def _flat(xs):
        out = []
        for x in xs:
            if isinstance(x, (list, tuple)):
                out.extend(Sched._flat(x))
            else:
                out.append(x)
        return out

    def _deps(self, reads, writes):
        deps = []
        for r in reads:
            if r.w is not None:
                deps.append(r.w)
        for r in writes:
            if r.w is not None:
                deps.append(r.w)
            deps.extend(r.rs.values())
        return deps

    def _need(self, eng, deps):
        best = {}
        for sem, val, src in deps:
            if src == eng and (eng == "pe" or not self.same):
                continue
            k = id(sem)
            if k not in best or val > best[k][1]:
                best[k] = (sem, val)
        out = []
        for k, (sem, val) in best.items():
            if self.waited.get((eng, k), 0) >= val:
                continue
            self.waited[(eng, k)] = val
            out.append((sem, val))
        return out

    @staticmethod
    def _addr(r, tok):
        k = id(tok[0])
        if k not in r.rs or r.rs[k][1] < tok[1]:
            r.rs[k] = tok

    def op(self, eng, fn, reads=(), writes=()):
        reads, writes = self._flat(reads), self._flat(writes)
        waits = self._need(eng, self._deps(reads, writes))
        self.cnt[eng] += 1
        tok = (self.sem[eng], self.cnt[eng], eng)
        self.q[eng].append((waits, fn, (self.sem[eng], 1)))
        for r in reads:
            self._addr(r, tok)
        for r in writes:
            r.w = tok
            r.rs = {}

    def dma(self, fn, owner, reads=(), writes=(), qeng="sp", final=False, skip_own=False):
        reads, writes = self._flat(reads), self._flat(writes)
        kind = 0 if qeng == "pool" else 1
        if owner.dsem is None:
            owner.dsem = [None, None]
            owner.dcnt = [0, 0]
        if owner.dsem[kind] is None:
            owner.dsem[kind] = self.stack.enter_context(self.nc.semaphore("d%d" % self.nsem))
            self.nsem += 1
        deps = self._deps(reads, writes)
        if skip_own:
            deps = [d for d in deps if d[0] is not owner.dsem[kind]]
        if qeng == "pool" and not skip_own and len(self.pool_dmas) >= 3:
            deps.append(self.pool_dmas[-3])
        waits = self._need(qeng, deps)
        owner.dcnt[kind] += 16
        tok = (owner.dsem[kind], owner.dcnt[kind], "dma")
        self.q[qeng].append((waits, fn, (owner.dsem[kind], 16)))
        for r in reads:
            self._addr(r, tok)
        for r in writes:
            r.w = tok
            r.rs = {}
        if final:
            self.final.append(tok)
        if qeng == "pool":
            if skip_own and self.pool_dmas and self.pool_dmas[-1][0] is tok[0]:
                self.pool_dmas[-1] = tok
            else:
                self.pool_dmas.append(tok)

    def finish(self, qeng="sp"):
        waits = self._need(qeng, self.final)
        self.q[qeng].append((waits, None, None))

    def replay(self, name, e):
        for waits, fn, inc in self.q[name]:
            for sem, val in waits:
                e.wait_ge(sem, val)
            if fn is None:
                continue
            ins = fn(e)
            if inc is not None:
                ins.then_inc(inc[0], inc[1])

    def run_block(self):
        with self.nc.Block() as block:
            @block.tensor
            def _(e):
                self.replay("pe", e)

            @block.scalar
            def _(e):
                self.replay("act", e)

            @block.vector
            def _(e):
                self.replay("dve", e)

            @block.gpsimd
            def _(e):
                self.replay("pool", e)

            @block.sync
            def _(e):
                self.replay("sp", e)


class Ring:
    def __init__(self, items):
        self.items = items
        self.i = 0

    def get(self):
        it = self.items[self.i % len(self.items)]
        self.i += 1
        return it


def _consts():
    c = {}
    c["c_ident"] = np.eye(128, dtype=np.float32)
    perm = np.zeros((128, 128), np.float32)
    for m in range(128):
        d = m % 64
        if d < 8:
            perm[m + 8, m] = 1.0
        elif d < 16:
            perm[m - 8, m] = 1.0
    c["c_perm"] = perm
    k = np.arange(128)[:, None]
    q = np.arange(128)[None, :]
    mask = np.concatenate([(q < k), (q >= k)], axis=1).astype(np.float32)
    c["c_mask"] = np.concatenate([mask, mask], axis=1)
    eh = np.zeros((128, 16, 16), np.float32)
    for h in range(16):
        eh[:, h, h] = 1.0
    c["c_eh"] = eh.reshape(128, 256)
    bc = np.zeros((16, 8, 128), np.float32)
    for cc in range(8):
        for p in range(128):
            bc[2 * cc + p // 64, cc, p] = 1.0
    c["c_bc"] = bc.reshape(16, 1024)
    pos = np.arange(SEQ, dtype=np.float32)
    inv_freq = (np.float32(500000.0) ** (-np.arange(0, 16, 2, dtype=np.float32) / np.float32(16))).astype(np.float32)
    ang = (pos[:, None] * inv_freq[None, :]).astype(np.float32)
    cos = np.cos(ang).astype(np.float32)
    sin = np.sin(ang).astype(np.float32)
    C = np.ones((128, SEQ), np.float32)
    Sg = np.zeros((128, SEQ), np.float32)
    for p in range(128):
        d = p % 64
        if d < 8:
            C[p] = cos[:, d]
            Sg[p] = -sin[:, d]
        elif d < 16:
            C[p] = cos[:, d - 8]
            Sg[p] = sin[:, d - 8]
    c["c_ropeC"] = C
    c["c_ropeS"] = Sg
    return c


def build_program(ntiles=NT, same_engine_sync=True, stop_after=None, dump=()):
    nc = bass.Bass("TRN2", target_bir_lowering=False)

    def din(name, shape):
        return nc.dram_tensor(name, shape, F32, kind="ExternalInput").ap()

    x = din("x", [2 * SEQ, D])
    norm_g = din("norm_g", [1, D])
    w_in = din("w_in", [1, D, DIN])
    conv_w = din("conv_w", [1, 4, D])
    conv_b = din("conv_b", [1, D])
    lru_w_a = din("lru_w_a", [1, 8, 128, 128])
    lru_b_a = din("lru_b_a", [1, D])
    lru_w_x = din("lru_w_x", [1, 8, 128, 128])
    lru_b_x = din("lru_b_x", [1, D])
    lru_lambda = din("lru_lambda", [1, D])
    attn_sinks = din("attn_sinks", [1, 16])
    w_rnn_out = din("w_rnn_out", [1, D, D])
    w_attn_out = din("w_attn_out", [1, D, D])
    w_o = din("w_o", [1, D, D])
    final_norm_g = din("final_norm_g", [1, D])
    c_ident = din("c_ident", [128, 128])
    c_perm = din("c_perm", [128, 128])
    c_mask = din("c_mask", [128, 512])
    c_eh = din("c_eh", [128, 256])
    c_bc = din("c_bc", [16, 1024])
    c_ropeC = din("c_ropeC", [128, SEQ])
    c_ropeS = din("c_ropeS", [128, SEQ])
    wstream = nc.dram_tensor("wstream", [NSLOT, 128, 4096], BF16, kind="Internal").ap()
    out = nc.dram_tensor("out", [2 * SEQ, D], F32, kind="ExternalOutput").ap()

    win_v = w_in.rearrange("o (k p) c -> p (o k) c", p=128)
    wro_v = w_rnn_out.rearrange("o (k p) c -> p (o k) c", p=128)
    wao_v = w_attn_out.rearrange("o (k p) c -> p (o k) c", p=128)
    wo_v = w_o.rearrange("o (k p) c -> p (o k) c", p=128)

    with ExitStack() as st:
        S = Sched(nc, st, same_engine_sync=same_engine_sync)

        def sb(name, shape, dt):
            return st.enter_context(nc.sbuf_tensor(name, shape, dt))

        ps = st.enter_context(nc.psum_tensor("ps", [128, 7, 512], F32))
        psT = st.enter_context(nc.psum_tensor("psT", [128, 1024], BF16))
        r_ps = [Res("ps%d" % i) for i in range(7)]
        r_psT = Res("psT")

        ident = sb("ident", [128, 128], BF16); r_ident = Res("ident")
        permb = sb("permb", [128, 128], BF16); r_perm = Res("perm")
        maskb = sb("maskb", [128, 2, 256], BF16); r_mask = Res("mask")
        ehb = sb("ehb", [128, 256], BF16); r_eh = Res("eh")
        bcs = sb("bcs", [16, 1024], F32); r_bc = Res("bc")
        grep = sb("grep", [128, D], F32); r_grep = Res("grep")
        fgrep = sb("fgrep", [128, D], F32); r_fgrep = Res("fgrep")
        mhalf = sb("mhalf", [128, 1], F32); r_mhalf = Res("mhalf")
        vrow = sb("vrow", [64, 128], F32); r_vrow = Res("vrow")
        identf = sb("identf", [128, 128], F32); r_identf = Res("identf")
        vT = sb("vT", [128, 8, 8], F32); r_vT = Res("vT")
        vd = sb("vd", [128, 4, 8], F32); r_vd = Res("vd")
        sinkt = sb("sinkt", [16, 2], F32); r_sink = Res("sink")
        lruA = sb("lruA", [128, 8, 128], BF16); r_lruA = Res("lruA")
        lruX = sb("lruX", [128, 8, 128], BF16); r_lruX = Res("lruX")

        wring = [sb("wring%d" % i, [128, 4096], BF16) for i in range(4)]
        r_wring = [Res("wring%d" % i) for i in range(4)]
        hT = [sb("hT%d" % i, [128, 8, T], BF16) for i in range(2)]
        r_hT = [[Res("hT%d_%d" % (i, b)) for b in range(4)] for i in range(2)]
        yr = sb("yr", [128, 8, T], BF16); r_yr = [Res("yr%d" % i) for i in range(8)]
        ya = sb("ya", [128, 8, T], BF16); r_ya = [Res("ya%d" % i) for i in range(4)]
        qmg = sb("qmg", [128, 8, T], BF16); r_qmg = [Res("qmg%d" % i) for i in range(8)]
        sga = sb("sga", [128, 8, T], BF16); r_sga = [Res("sga%d" % i) for i in range(8)]
        kT = sb("kT", [128, 4, 128 + T], BF16); r_kT = [Res("kT%d" % i) for i in range(4)]
        vtok = sb("vtok", [128, 5, 256], BF16); r_vtok = Res("vtok")
        halo = sb("halo", [128, 8, 4], F32); r_halo = [Res("halo%d" % i) for i in range(8)]
        hst = sb("hst", [128, 8], F32); r_hst = [Res("hst%d" % i) for i in range(8)]
        ropeC = [sb("ropeC%d" % i, [128, T], F32) for i in range(2)]
        ropeS = [sb("ropeS%d" % i, [128, T], F32) for i in range(2)]
        r_rope = [Res("rope%d" % i) for i in range(2)]
        xset = [[sb("x%d_%d" % (i, b), [128, D], F32) for b in range(4)] for i in range(2)]
        r_xset = [[Res("x%d_%d" % (i, b)) for b in range(4)] for i in range(2)]
        xnb = [sb("xnb%d" % i, [128, D], BF16) for i in range(2)]
        r_xnb = [Res("xnb%d" % i) for i in range(2)]
        big = [sb("big%d" % i, [128, D], F32) for i in range(2)]
        r_big = [Res("big%d" % i) for i in range(2)]
        stat = sb("stat", [128, 16], F32)
        r_stat = [Res("stat%d" % i) for i in range(4)]
        def mkring(name, n, shape, dt):
            return Ring([(sb("%s%d" % (name, i), shape, dt), Res("%s%d" % (name, i))) for i in range(n)])
        wk_ue = mkring("wue", 2, [128, 520], F32)
        wk_uc = mkring("wuc", 3, [128, 512], F32)
        wk_sg = mkring("wsg", 3, [128, 512], F32)
        wk = mkring("wk", 8, [128, 512], F32)
        wb_ucb = mkring("wucb", 3, [128, 512], BF16)
        wb = mkring("wb", 3, [128, 512], BF16)
        pT_t = [sb("pT%d" % i, [128, 2, 256], BF16) for i in range(3)]
        pTr = Ring([(pT_t[i], Res("pT%d" % i)) for i in range(3)])
        dsm = sb("dsm", [16, 2, 128], F32); r_dsm = [Res("dsm0"), Res("dsm1")]

        psr = Ring([(i, r_ps[i]) for i in range(7)])

        def pbank(i):
            return ps[:, i, :]

        def cast_load(dst_ap, src_ap, res, reads=(), skip_own=False):
            S.dma(lambda e, d=dst_ap, s=src_ap: e.dma_start(out=d, in_=s), res, reads=reads, writes=[res], qeng="pool",
                  skip_own=skip_own)

        def load(dst_ap, src_ap, res, slow=False):
            S.dma(lambda e, d=dst_ap, s=src_ap, sl=slow: e.dma_start(out=d, in_=s, allow_slow_non_contiguous=sl),
                  res, writes=[res])

        cast_load(ident[:], c_ident, r_ident)
        cast_load(permb[:], c_perm, r_perm)
        cast_load(maskb[:].rearrange("p a m -> p (a m)"), c_mask, r_mask)
        cast_load(ehb[:], c_eh, r_eh)
        cast_load(lruA[:], lru_w_a.rearrange("o n c d -> c (o n) d"), r_lruA)
        cast_load(lruX[:], lru_w_x.rearrange("o n c d -> c (o n) d"), r_lruX)
        load(bcs[:], c_bc, r_bc)
        load(grep[:], norm_g.partition_broadcast(128), r_grep)
        load(fgrep[:], final_norm_g.partition_broadcast(128), r_fgrep)
        vsrc = [conv_w[0, 0:1, :], conv_w[0, 1:2, :], conv_w[0, 2:3, :], conv_w[0, 3:4, :], conv_b, lru_b_a, lru_b_x, lru_lambda]
        for i, v in enumerate(vsrc):
            load(vrow[8 * i:8 * i + 8, :], v.rearrange("o (n p) -> (o n) p", p=128), r_vrow)
        load(identf[:], c_ident, r_identf)
        load(sinkt[:, 0:1], attn_sinks.rearrange("o h -> h o"), r_sink, slow=True)
        S.op("pe", lambda e: e.transpose(ps[:, 0, 0:64], vrow[:, :], identf[0:64, 0:64]), reads=[r_vrow, r_identf], writes=[r_ps[0]])
        S.op("act", lambda e: e.activation(out=vT[:].rearrange("p a b -> p (a b)"), in_=ps[:, 0, 0:64], func=AF.Copy),
             reads=[r_ps[0]], writes=[r_vT])

        S.op("pool", lambda e: e.memset(mhalf[:], -0.5), writes=[r_mhalf])
        S.op("dve", lambda e: e.tensor_scalar(out=vd[:, 0:2, :], in0=vT[:, 5:7, :], scalar1=0.5, scalar2=None, op0=ALU.mult),
             reads=[r_vT], writes=[r_vd])
        S.op("act", lambda e: e.activation(out=vd[:, 2, :], in_=vT[:, 7, :], func=AF.Exp, scale=-1.0), reads=[r_vT], writes=[r_vd])
        S.op("act", lambda e: e.activation(out=vd[:, 3, :], in_=vd[:, 2, :], func=AF.Ln, bias=1.0), reads=[r_vd], writes=[r_vd])
        S.op("dve", lambda e: e.tensor_scalar(out=vd[:, 2, :], in0=vd[:, 3, :], scalar1=-4.0, scalar2=None, op0=ALU.mult),
             reads=[r_vd], writes=[r_vd])
        S.op("dve", lambda e: e.tensor_scalar(out=vd[:, 3, :], in0=vd[:, 3, :], scalar1=-8.0, scalar2=None, op0=ALU.mult),
             reads=[r_vd], writes=[r_vd])
        S.op("act", lambda e: e.activation(out=sinkt[:, 1:2], in_=sinkt[:, 0:1], func=AF.Exp), reads=[r_sink], writes=[r_sink])

        def slot_units(s):
            res = []
            if s < 4:
                res.append((0, 256, win_v[:, :, OFF_U + 2 * s * 128: OFF_U + (2 * s + 2) * 128]))
                res.append((256, 512, win_v[:, :, OFF_G + 2 * s * 128: OFF_G + (2 * s + 2) * 128]))
            elif s == 4:
                for g in range(4):
                    for hf in range(2):
                        res.append((g * 128 + hf * 64, g * 128 + hf * 64 + 64, win_v[:, :, OFF_K + g * 64: OFF_K + (g + 1) * 64]))
            elif s == 5:
                res.append((0, 256, win_v[:, :, OFF_V:OFF_V + 256]))
                res.append((256, 512, win_v[:, :, OFF_Q:OFF_Q + 256]))
            elif s == 6:
                res.append((0, 512, win_v[:, :, OFF_Q + 256:OFF_Q + 768]))
            elif s == 7:
                res.append((0, 256, win_v[:, :, OFF_Q + 768:OFF_Q + 1024]))
                res.append((256, 512, win_v[:, :, OFF_GA:OFF_GA + 256]))
            elif s == 8:
                res.append((0, 512, win_v[:, :, OFF_GA + 256:OFF_GA + 768]))
            elif s == 9:
                res.append((0, 256, win_v[:, :, OFF_GA + 768:OFF_GA + 1024]))
            elif s < 18:
                i, y = (s - 10) // 2, (s - 10) % 2
                if y == 0:
                    res.append((0, 256, wro_v[:, :, 2 * i * 128:(2 * i + 2) * 128]))
                    res.append((256, 512, wao_v[:, :, 2 * i * 128:(2 * i + 2) * 128]))
                else:
                    res.append((0, 256, win_v[:, :, OFF_MR + 2 * i * 128: OFF_MR + (2 * i + 2) * 128]))
                    res.append((256, 512, win_v[:, :, OFF_MA + 2 * i * 128: OFF_MA + (2 * i + 2) * 128]))
            else:
                half = s - 18
                res.append((0, 512, wo_v[:, :, half * 512:(half + 1) * 512]))
            return res

        def kview(buf):
            return buf[:].rearrange("p (k c) -> p k c", k=8)

        wslot_i = [0]
        r_wstream = [Res("wstream%d" % i) for i in range(NSLOT)]

        def fetch_slot(t, s):
            i = wslot_i[0] % 4
            wslot_i[0] += 1
            buf, res = wring[i], r_wring[i]
            if t == 0:
                for ii, (c0, c1, src) in enumerate(slot_units(s)):
                    cast_load(kview(buf)[:, :, c0:c1], src, res, skip_own=(ii > 0))
                S.dma(lambda e, b=buf, s=s: e.dma_start(out=wstream[s], in_=b[:]), res, reads=[res], writes=[r_wstream[s]])
            else:
                S.dma(lambda e, b=buf, s=s: e.dma_start(out=b[:], in_=wstream[s]), res, reads=[r_wstream[s]], writes=[res])
            return buf, res

        def unit(buf, j):
            return kview(buf)[:, :, j * 128:(j + 1) * 128]

        def mm_group(out_ap, lhs_fn, rhs_fn, nk, reads, writes):
            def fn(e, out_ap=out_ap, lhs_fn=lhs_fn, rhs_fn=rhs_fn, nk=nk):
                ins = None
                for k in range(nk):
                    ins = e.matmul(out_ap, lhsT=lhs_fn(k), rhs=rhs_fn(k), start=(k == 0), stop=(k == nk - 1))
                return ins
            S.op("pe", fn, reads=reads, writes=writes)

        def load_x(t):
            xs, rx = xset[t % 2], r_xset[t % 2]
            for b in range(4):
                r0 = t * T + b * 128
                S.dma(lambda e, d=xs[b], r0=r0: e.dma_start(out=d[:], in_=x[r0:r0 + 128, :]), rx[b], writes=[rx[b]])
            pos0 = (t % TPS) * T
            i = t % 2
            S.dma(lambda e, i=i, p=pos0: e.dma_start(out=ropeC[i][:], in_=c_ropeC[:, p:p + T]), r_rope[i], writes=[r_rope[i]])
            S.dma(lambda e, i=i, p=pos0: e.dma_start(out=ropeS[i][:], in_=c_ropeS[:, p:p + T]), r_rope[i], writes=[r_rope[i]])

        def rstd_ops(sidx, rs):
            c0 = 4 * sidx
            S.op("pool", lambda e, c0=c0: e.tensor_scalar(out=stat[:, c0 + 1:c0 + 2], in0=stat[:, c0:c0 + 1], scalar1=1.0 / D,
                                                          scalar2=EPS, op0=ALU.mult, op1=ALU.add), reads=[rs], writes=[rs])
            S.op("pool", lambda e, c0=c0: e.tensor_tensor(out=stat[:, c0 + 1:c0 + 2], in0=stat[:, c0 + 1:c0 + 2], in1=mhalf[:],
                                                          op=ALU.pow), reads=[rs, r_mhalf], writes=[rs])

        stat_i = [0]

        def stage_A(t, blocks=(0, 1, 2, 3), part="both"):
            hTt, rh = hT[t % 2], r_hT[t % 2]
            xs, rx = xset[t % 2], r_xset[t % 2]
            for b in blocks:
                xb_, rxb = xnb[b % 2], r_xnb[b % 2]
                if part in ("both", "tr"):
                    def tr(e, xb_=xb_):
                        ins = None
                        for c in range(8):
                            ins = e.transpose(psT[:, c * 128:(c + 1) * 128], xb_[:, c * 128:(c + 1) * 128], ident[:])
                        return ins
                if part == "tr":
                    S.op("pe", tr, reads=[rxb, r_ident], writes=[r_psT])
                    S.op("act", lambda e, hTt=hTt, b=b: e.activation(
                        out=hTt[:, :, b * 128:(b + 1) * 128], in_=psT[:].rearrange("p (c m) -> p c m", c=8), func=AF.Copy),
                        reads=[r_psT], writes=[rh[b]])
                    continue
                si = stat_i[0] % 4
                stat_i[0] += 1
                rs = r_stat[si]
                bg, rbg = big[b % 2], r_big[b % 2]
                S.op("act", lambda e, bg=bg, xb=xs[b]: e.activation(out=bg[:], in_=xb[:], func=AF.Square), reads=[rx[b]], writes=[rbg])
                S.op("dve", lambda e, bg=bg, si=si: e.tensor_reduce(out=stat[:, 4 * si:4 * si + 1], in_=bg[:], axis=AX.X, op=ALU.add),
                     reads=[rbg], writes=[rs])
                rstd_ops(si, rs)
                S.op("dve", lambda e, xb_=xb_, xb=xs[b], si=si: e.scalar_tensor_tensor(
                    out=xb_[:], in0=xb[:], scalar=stat[:, 4 * si + 1:4 * si + 2], in1=grep[:], op0=ALU.mult, op1=ALU.mult),
                    reads=[rx[b], rs, r_grep], writes=[rxb])
                if part == "elem":
                    continue
                S.op("pe", tr, reads=[rxb, r_ident], writes=[r_psT])
                S.op("act", lambda e, hTt=hTt, b=b: e.activation(
                    out=hTt[:, :, b * 128:(b + 1) * 128], in_=psT[:].rearrange("p (c m) -> p c m", c=8), func=AF.Copy),
                    reads=[r_psT], writes=[rh[b]])

        def make_B(t):
            hTt, rh = hT[t % 2], r_hT[t % 2]
            first = (t % TPS == 0)
            bstate = {}

            def front(n, buf, rbuf, j0):
                wu, wg = unit(buf, j0), unit(buf, 2 + j0)
                bu, rbu = psr.get()
                mm_group(pbank(bu), lambda k, wu=wu: wu[:, k, :], lambda k: hTt[:, k, :], 8, reads=[rbuf] + rh, writes=[rbu])
                bgp, rbg_ = psr.get()
                mm_group(pbank(bgp), lambda k, wg=wg: wg[:, k, :], lambda k: hTt[:, k, :], 8, reads=[rbuf] + rh, writes=[rbg_])
                ue, rue = wk_ue.get()
                if first:
                    S.op("dve", lambda e, ue=ue: e.memset(ue[:, 0:3], 0.0), writes=[rue])
                else:
                    S.op("pool", lambda e, ue=ue, n=n: e.tensor_copy(out=ue[:, 0:3], in_=halo[:, n, 0:3]),
                         reads=[r_halo[n]], writes=[rue])
                S.op("act", lambda e, ue=ue, bu=bu: e.activation(out=ue[:, 3:3 + T], in_=pbank(bu), func=AF.Copy),
                     reads=[rbu, rue], writes=[rue])
                S.op("pool", lambda e, ue=ue, n=n: e.tensor_copy(out=halo[:, n, 0:3], in_=ue[:, T:T + 3]),
                     reads=[rue], writes=[r_halo[n]])
                sg, rsg = wk_sg.get()
                S.op("act", lambda e, sg=sg, bgp=bgp: e.activation(out=sg[:, 0:T], in_=pbank(bgp), func=AF.Tanh, scale=0.5),
                     reads=[rbg_], writes=[rsg])
                S.op("dve", lambda e, sg=sg, bgp=bgp: e.scalar_tensor_tensor(
                    out=sg[:, 0:T], in0=sg[:, 0:T], scalar=1.0, in1=pbank(bgp), op0=ALU.add, op1=ALU.mult),
                    reads=[rsg, rbg_], writes=[rsg])
                uc, ruc = wk_uc.get()
                S.op("pool", lambda e, ue=ue, uc=uc, n=n: e.tensor_scalar(
                    out=uc[:, 0:T], in0=ue[:, 3:3 + T], scalar1=vT[:, 3, n:n + 1], scalar2=vT[:, 4, n:n + 1], op0=ALU.mult, op1=ALU.add),
                    reads=[rue, r_vT], writes=[ruc])
                for j in (2,):
                    cq, rcq = wk.get()
                    S.op("pool", lambda e, ue=ue, cq=cq, n=n, j=j: e.tensor_scalar(
                        out=cq[:, 0:T], in0=ue[:, j:j + T], scalar1=vT[:, j, n:n + 1], scalar2=0.0, op0=ALU.mult, op1=ALU.add),
                        reads=[rue, r_vT], writes=[rcq])
                    S.op("pool", lambda e, cq=cq, uc=uc: e.tensor_tensor(out=uc[:, 0:T], in0=uc[:, 0:T], in1=cq[:, 0:T], op=ALU.add),
                         reads=[rcq, ruc], writes=[ruc])
                for j in (1, 0):
                    S.op("dve", lambda e, ue=ue, uc=uc, n=n, j=j: e.scalar_tensor_tensor(
                        out=uc[:, 0:T], in0=ue[:, j:j + T], scalar=vT[:, j, n:n + 1], in1=uc[:, 0:T], op0=ALU.mult, op1=ALU.add),
                        reads=[rue, r_vT, ruc], writes=[ruc])
                ucb, rucb = wb_ucb.get()
                return (n, uc, ruc, ucb, rucb, sg, rsg)

            def back(n, uc, ruc, ucb, rucb, sg, rsg):
                br, rbr = psr.get()
                mm_group(pbank(br), lambda k, n=n: lruA[:, n, :], lambda k, ucb=ucb: ucb[:], 1, reads=[r_lruA, rucb], writes=[rbr])
                bi, rbi = psr.get()
                mm_group(pbank(bi), lambda k, n=n: lruX[:, n, :], lambda k, ucb=ucb: ucb[:], 1, reads=[r_lruX, rucb], writes=[rbi])
                tr_, rtr = wk.get()
                S.op("act", lambda e, tr_=tr_, br=br, n=n: e.activation(out=tr_[:, 0:T], in_=pbank(br), func=AF.Tanh, scale=0.5,
                                                                       bias=vd[:, 0, n:n + 1]), reads=[rbr, r_vd], writes=[rtr])
                iu, riu = wk.get()
                S.op("act", lambda e, iu=iu, bi=bi, n=n: e.activation(out=iu[:, 0:T], in_=pbank(bi), func=AF.Tanh, scale=0.5,
                                                                     bias=vd[:, 1, n:n + 1]), reads=[rbi, r_vd], writes=[riu])
                a_, ra = wk.get()
                S.op("act", lambda e, a_=a_, tr_=tr_, n=n: e.activation(out=a_[:, 0:T], in_=tr_[:, 0:T], func=AF.Exp,
                                                                       scale=vd[:, 2, n:n + 1], bias=vd[:, 2, n:n + 1]),
                     reads=[rtr, r_vd], writes=[ra])
                s_, rs_ = wk.get()
                S.op("act", lambda e, s_=s_, tr_=tr_, n=n: e.activation(out=s_[:, 0:T], in_=tr_[:, 0:T], func=AF.Exp,
                                                                       scale=vd[:, 3, n:n + 1], bias=vd[:, 3, n:n + 1]),
                     reads=[rtr, r_vd], writes=[rs_])
                S.op("act", lambda e, s_=s_: e.activation(out=s_[:, 0:T], in_=s_[:, 0:T], func=AF.Sqrt, scale=-1.0, bias=1.0),
                     reads=[rs_], writes=[rs_])
                S.op("dve", lambda e, iu=iu, uc=uc: e.scalar_tensor_tensor(out=iu[:, 0:T], in0=iu[:, 0:T], scalar=1.0, in1=uc[:, 0:T],
                                                                          op0=ALU.add, op1=ALU.mult), reads=[riu, ruc], writes=[riu])
                S.op("dve", lambda e, iu=iu, s_=s_: e.scalar_tensor_tensor(out=iu[:, 0:T], in0=s_[:, 0:T], scalar=0.5, in1=iu[:, 0:T],
                                                                          op0=ALU.mult, op1=ALU.mult), reads=[riu, rs_], writes=[riu])
                h_, rh_ = wk.get()
                if first:
                    S.op("dve", lambda e, h_=h_, a_=a_, iu=iu: e.tensor_tensor_scan(
                        out=h_[:, 0:T], data0=a_[:, 0:T], data1=iu[:, 0:T], initial=0.0, op0=ALU.mult, op1=ALU.add),
                        reads=[ra, riu], writes=[rh_])
                else:
                    S.op("dve", lambda e, h_=h_, a_=a_, iu=iu, n=n: e.tensor_tensor_scan(
                        out=h_[:, 0:T], data0=a_[:, 0:T], data1=iu[:, 0:T], initial=hst[:, n:n + 1], op0=ALU.mult, op1=ALU.add),
                        reads=[ra, riu, r_hst[n]], writes=[rh_])
                S.op("pool", lambda e, h_=h_, n=n: e.tensor_copy(out=hst[:, n:n + 1], in_=h_[:, T - 1:T]),
                     reads=[rh_], writes=[r_hst[n]])
                S.op("dve", lambda e, h_=h_, sg=sg, n=n: e.scalar_tensor_tensor(
                    out=yr[:, n, :], in0=h_[:, 0:T], scalar=0.5, in1=sg[:, 0:T], op0=ALU.mult, op1=ALU.mult),
                    reads=[rh_, rsg], writes=[r_yr[n]])

            def front_n(n):
                if n % 2 == 0:
                    bstate["buf"] = fetch_slot(t, n // 2)
                buf, rbuf = bstate["buf"]
                return front(n, buf, rbuf, n % 2)

            def front_b(n, uc, ruc, ucb, rucb, sg, rsg):
                S.op("act", lambda e, uc=uc, ucb=ucb: e.activation(out=ucb[:], in_=uc[:, 0:T], func=AF.Copy), reads=[ruc], writes=[rucb])
            return front_n, front_b, back

        def rope_chain(pb, rpb, dst_ap, rdst, t):
            i = t % 2
            raw, rraw = wb.get()
            S.op("act", lambda e, raw=raw, pb=pb: e.activation(out=raw[:], in_=pbank(pb), func=AF.Copy), reads=[rpb], writes=[rraw])
            return (raw, rraw, dst_ap, rdst, i)

        def rope_finish(raw, rraw, dst_ap, rdst, i):
            p2, rp2 = psr.get()
            mm_group(pbank(p2), lambda k: permb[:], lambda k, raw=raw: raw[:], 1, reads=[r_perm, rraw], writes=[rp2])
            t2, rt2 = wk.get()
            S.op("pool", lambda e, t2=t2, raw=raw, i=i: e.tensor_tensor(out=t2[:, 0:T], in0=raw[:], in1=ropeC[i][:], op=ALU.mult),
                 reads=[rraw, r_rope[i]], writes=[rt2])
            t1, rt1 = wk.get()
            S.op("dve", lambda e, t1=t1, p2=p2, i=i: e.tensor_tensor(out=t1[:, 0:T], in0=pbank(p2), in1=ropeS[i][:], op=ALU.mult),
                 reads=[rp2, r_rope[i]], writes=[rt1])
            S.op("dve", lambda e, t1=t1, t2=t2, dst_ap=dst_ap: e.tensor_tensor(out=dst_ap, in0=t1[:, 0:T], in1=t2[:, 0:T], op=ALU.add),
                 reads=[rt1, rt2], writes=[rdst])

        def make_C(t):
            hTt, rh = hT[t % 2], r_hT[t % 2]
            units = [("kd", g) for g in range(4)] + [("v", 0), ("v", 1)] + [("q", c) for c in range(8)] + \
                    [("ga", c) for c in range(8)] + [("pad", 0), ("pad", 1)]
            pend = []
            cstate = {}

            def unit_fn(ui):
                kind, i = units[ui]
                j = ui % 4
                if j == 0:
                    cstate["buf"] = fetch_slot(t, 4 + ui // 4)
                buf, rbuf = cstate["buf"]
                if kind in ("kd", "q"):
                    w = unit(buf, j)
                    pb, rpb = psr.get()
                    mm_group(pbank(pb), lambda k, w=w: w[:, k, :], lambda k: hTt[:, k, :], 8, reads=[rbuf] + rh, writes=[rpb])
                    if kind == "kd":
                        dst, rd = kT[:, i, 128:128 + T], r_kT[i]
                    else:
                        dst, rd = qmg[:, i, :], r_qmg[i]
                    pend.append(rope_chain(pb, rpb, dst, rd, t))
                    if len(pend) > 2:
                        rope_finish(*pend.pop(0))
                elif kind == "v" and i == 0:
                    wv = kview(buf)[:, :, j * 128:(j + 2) * 128]
                    for pr in range(2):
                        pb, rpb = psr.get()

                        def fn(e, pb=pb, pr=pr, wv=wv):
                            ins = None
                            for bb in range(2):
                                b = pr * 2 + bb
                                for k in range(8):
                                    ins = e.matmul(ps[:, pb, bb * 256:(bb + 1) * 256], lhsT=hTt[:, k, b * 128:(b + 1) * 128],
                                                   rhs=wv[:, k, :], start=(k == 0), stop=(k == 7))
                            return ins
                        S.op("pe", fn, reads=[rbuf] + rh, writes=[rpb])
                        S.op("act", lambda e, pb=pb, pr=pr: e.activation(
                            out=vtok[:, 1 + 2 * pr:3 + 2 * pr, :], in_=pbank(pb).rearrange("p (b m) -> p b m", b=2), func=AF.Copy),
                            reads=[rpb], writes=[r_vtok])
                elif kind == "ga":
                    w = unit(buf, j)
                    pb, rpb = psr.get()
                    mm_group(pbank(pb), lambda k, w=w: w[:, k, :], lambda k: hTt[:, k, :], 8, reads=[rbuf] + rh, writes=[rpb])
                    tg, rtg = wk.get()
                    S.op("act", lambda e, tg=tg, pb=pb: e.activation(out=tg[:, 0:T], in_=pbank(pb), func=AF.Tanh, scale=0.5),
                         reads=[rpb], writes=[rtg])
                    S.op("dve", lambda e, tg=tg, pb=pb, i=i: e.scalar_tensor_tensor(
                        out=sga[:, i, :], in0=tg[:, 0:T], scalar=1.0, in1=pbank(pb), op0=ALU.add, op1=ALU.mult),
                        reads=[rtg, rpb], writes=[r_sga[i]])

            def flush():
                while pend:
                    rope_finish(*pend.pop(0))
            return unit_fn, flush, len(units)

        def stage_BC(t, dsteps=None):
            frontB, frontB2, backB = make_B(t)
            unitC, flushC, NU = make_C(t)
            pendB = []
            ui = 0
            NP = 10
            for p_ in range(NP):
                if p_ < 8:
                    pendB.append(frontB(p_))
                if p_ >= 2:
                    backB(*pendB.pop(0))
                if p_ < 8:
                    frontB2(*pendB[-1])
                for _ in range(3):
                    if ui < NU:
                        unitC(ui)
                        ui += 1
                        if ui == NU:
                            flushC()
                if ui == NU and dsteps is not None and p_ >= 8:
                    psr.items = [(i, r_ps[i]) for i in (0, 1, 5, 6)]
                    for _ in range(5):
                        if dsteps:
                            dsteps.pop(0)()
            psr.items = [(i, r_ps[i]) for i in range(7)]

        def stage_D(t):
            ts = t % TPS
            S_BANKS = [(0, 0), (5, 0)]
            sring = Ring(S_BANKS)
            DEN, VALS = 2, 3
            rbv = psT[:].bitcast(F32)
            post2_pending = [None]
            steps = []
            for jq in range(4):
                first = (ts == 0 and jq == 0)
                lo = 128 if first else 0
                pend = []

                def s1(i, jq=jq, first=first, lo=lo):
                    c, g = i, i // 2
                    bk, _hf = sring.get()
                    rbk = [r_ps[bk], r_ps[bk + 1]]

                    def fn(e, bk=bk, g=g, c=c):
                        ins = None
                        for hh in range(2):
                            hb = hh * 64
                            if not first:
                                ins = e.matmul(ps[:, bk + hh, 0:128], lhsT=kT[hb:hb + 64, g, jq * 128:(jq + 1) * 128],
                                               rhs=qmg[hb:hb + 64, c, jq * 128:(jq + 1) * 128], start=True, stop=True)
                            ins = e.matmul(ps[:, bk + hh, 128:256], lhsT=kT[hb:hb + 64, g, 128 + jq * 128:128 + (jq + 1) * 128],
                                           rhs=qmg[hb:hb + 64, c, jq * 128:(jq + 1) * 128], start=True, stop=True)
                        return ins
                    S.op("pe", fn, reads=[r_kT[g], r_qmg[c]], writes=[rbk])
                    pt, rpt = pTr.get()
                    S.op("act", lambda e, pt=pt, bk=bk: e.activation(
                        out=pt[:, :, lo:256], in_=ps[:, bk:bk + 2, lo:256], func=AF.Exp, scale=0.125),
                        reads=[rbk], writes=[rpt])
                    S.op("dve", lambda e, pt=pt: e.tensor_tensor(out=pt[:, :, lo:256], in0=pt[:, :, lo:256], in1=maskb[:, :, lo:256], op=ALU.mult),
                         reads=[rpt, r_mask], writes=[rpt])
                    return (i, pt, rpt)

                def s4(i, pt, rpt, jq=jq, first=first):
                    c, g = i, i // 2

                    def fn(e, pt=pt, c=c, g=g):
                        ins = None
                        for hh in range(2):
                            h = 2 * c + hh
                            hb = hh * 64
                            vo = ps[hb:hb + 64, VALS + c // 4, (c % 4) * 128:(c % 4 + 1) * 128]
                            if not first:
                                e.matmul(vo, lhsT=vtok[:, jq, g * 64:(g + 1) * 64], rhs=pt[:, hh, 0:128], start=True, stop=False)
                            e.matmul(vo, lhsT=vtok[:, jq + 1, g * 64:(g + 1) * 64], rhs=pt[:, hh, 128:256], start=first, stop=True)
                            if not first:
                                ins = e.matmul(ps[0:16, DEN, 0:256], lhsT=ehb[:, h * 16:(h + 1) * 16], rhs=pt[:, hh, 0:256], start=(h == 0),
                                               stop=(h == 15))
                            else:
                                ins = e.matmul(ps[0:16, DEN, 128:256], lhsT=ehb[:, h * 16:(h + 1) * 16], rhs=pt[:, hh, 128:256],
                                               start=(h == 0), stop=(h == 15))
                        return ins
                    S.op("pe", fn, reads=[rpt, r_vtok, r_eh], writes=[r_ps[VALS], r_ps[VALS + 1], r_ps[DEN]])

                def post1(jq=jq, first=first):
                    di = jq % 2
                    bg, rbg = big[0], r_big[0]
                    S.op("act", lambda e, bg=bg: e.activation(out=bg[:].rearrange("p (a m) -> p a m", a=2), in_=ps[:, VALS:VALS + 2, :],
                                                              func=AF.Copy), reads=[r_ps[VALS], r_ps[VALS + 1]], writes=[rbg])
                    S.op("dve", lambda e, di=di: e.tensor_scalar(out=dsm[:, di, :], in0=ps[0:16, DEN, 128:256], scalar1=sinkt[:, 1:2],
                                                                  scalar2=None, op0=ALU.add), reads=[r_ps[DEN], r_sink], writes=[r_dsm[di]])
                    if not first:
                        S.op("dve", lambda e, di=di: e.tensor_tensor(out=dsm[:, di, :], in0=dsm[:, di, :], in1=ps[0:16, DEN, 0:128],
                                                                      op=ALU.add), reads=[r_ps[DEN], r_dsm[di]], writes=[r_dsm[di]])
                    S.op("dve", lambda e, di=di: e.reciprocal(out=dsm[:, di, :], in_=dsm[:, di, :]), reads=[r_dsm[di]], writes=[r_dsm[di]])

                def post2(jq=jq):
                    di = jq % 2
                    bg, rbg = big[0], r_big[0]
                    b1, rb1 = big[1], r_big[1]
                    for hf in range(2):
                        def fnb(e, di=di, hf=hf):
                            ins = None
                            for cc in range(4):
                                c = hf * 4 + cc
                                ins = e.matmul(rbv[:, cc * 128:(cc + 1) * 128], lhsT=bcs[:, c * 128:(c + 1) * 128],
                                               rhs=dsm[:, di, :], start=True, stop=True)
                            return ins
                        S.op("pe", fnb, reads=[r_bc, r_dsm[di]], writes=[r_psT])
                        S.op("dve", lambda e, bg=bg, b1=b1, hf=hf: e.tensor_tensor(
                            out=b1[:, hf * 512:(hf + 1) * 512], in0=rbv, in1=bg[:, hf * 512:(hf + 1) * 512], op=ALU.mult),
                            reads=[r_psT, rbg], writes=[rb1])
                    S.op("dve", lambda e, b1=b1, jq=jq: e.scalar_tensor_tensor(
                        out=ya[:, :, jq * 128:(jq + 1) * 128], in0=b1[:].rearrange("p (c m) -> p c m", c=8), scalar=0.5,
                        in1=sga[:, :, jq * 128:(jq + 1) * 128], op0=ALU.mult, op1=ALU.mult),
                        reads=[rb1] + r_sga, writes=[r_ya[jq]])

                SK = 1

                def hstep(i, s1=s1, s4=s4, pend=pend):
                    if i < 8:
                        pend.append(s1(i))
                    if i >= SK:
                        s4(*pend.pop(0))
                    if i == 3 and post2_pending[0] is not None:
                        post2_pending[0]()
                        post2_pending[0] = None
                for i in range(8 + SK):
                    steps.append(lambda i=i, hstep=hstep: hstep(i))

                def pstep(post1=post1, post2=post2):
                    post1()
                    post2_pending[0] = post2
                steps.append(pstep)

            def fin():
                post2_pending[0]()
                S.op("pool", lambda e: e.tensor_copy(out=kT[:, :, 0:128], in_=kT[:, :, T:T + 128]), reads=r_kT, writes=r_kT)
                S.op("pool", lambda e: e.tensor_copy(out=vtok[:, 0, :], in_=vtok[:, 4, :]), reads=[r_vtok], writes=[r_vtok])
            steps.append(fin)
            return steps

        def stage_E(t, a_next=None):
            hTt, rh = hT[t % 2], r_hT[t % 2]
            ebuf = {}
            for f in range(8):
                if a_next is not None:
                    stage_A(a_next, blocks=(f // 2,), part=("elem" if f % 2 == 0 else "tr"))
                if f % 2 == 0:
                    ebuf["x"] = fetch_slot(t, 10 + f)
                    ebuf["y"] = fetch_slot(t, 11 + f)
                (bx, rbx), (by, rby) = ebuf["x"], ebuf["y"]
                rbuf = [rbx, rby]
                wro, wao, wmr, wma = unit(bx, f % 2), unit(bx, 2 + f % 2), unit(by, f % 2), unit(by, 2 + f % 2)
                pc, rpc = psr.get()
                mm_group(pbank(pc), lambda k, w=wmr: w[:, k, :], lambda k: hTt[:, k, :], 8, reads=[rbuf] + rh, writes=[rpc])
                pd, rpd = psr.get()
                mm_group(pbank(pd), lambda k, w=wma: w[:, k, :], lambda k: hTt[:, k, :], 8, reads=[rbuf] + rh, writes=[rpd])
                pa, rpa = psr.get()
                mm_group(pbank(pa), lambda k, w=wro: w[:, k, :], lambda k: yr[:, k, :], 8, reads=[rbuf] + r_yr, writes=[rpa])
                pb, rpb = psr.get()
                mm_group(pbank(pb), lambda k, w=wao: w[:, k, :], lambda k: ya[:, k, :], 8, reads=[rbuf] + r_ya, writes=[rpb])
                tc_, rtc = wk.get()
                S.op("act", lambda e, tc_=tc_, pc=pc: e.activation(out=tc_[:, 0:T], in_=pbank(pc), func=AF.Tanh, scale=0.5),
                     reads=[rpc], writes=[rtc])
                td_, rtd = wk.get()
                S.op("act", lambda e, td_=td_, pd=pd: e.activation(out=td_[:, 0:T], in_=pbank(pd), func=AF.Tanh, scale=0.5),
                     reads=[rpd], writes=[rtd])
                S.op("dve", lambda e, tc_=tc_, pa=pa: e.scalar_tensor_tensor(out=tc_[:, 0:T], in0=tc_[:, 0:T], scalar=1.0, in1=pbank(pa),
                                                                            op0=ALU.add, op1=ALU.mult), reads=[rtc, rpa], writes=[rtc])
                S.op("dve", lambda e, td_=td_, pb=pb: e.scalar_tensor_tensor(out=td_[:, 0:T], in0=td_[:, 0:T], scalar=1.0, in1=pbank(pb),
                                                                            op0=ALU.add, op1=ALU.mult), reads=[rtd, rpb], writes=[rtd])
                S.op("pool", lambda e, tc_=tc_, td_=td_, f=f: e.tensor_tensor(out=qmg[:, f, :], in0=tc_[:, 0:T], in1=td_[:, 0:T], op=ALU.add),
                     reads=[rtc, rtd], writes=[r_qmg[f]])

        def stage_F(t):
            xs, rx = xset[t % 2], r_xset[t % 2]
            bufs = [fetch_slot(t, 18), fetch_slot(t, 19)]
            pairs = [(0, 1), (2, 3), (4, 5)]
            for b in range(4):
                p0, p1 = pairs[b % 3]

                def fn(e, b=b, p0=p0):
                    ins = None
                    for half in range(2):
                        w = bufs[half][0][:].rearrange("p (k m) -> p k m", k=8)
                        for k in range(8):
                            ins = e.matmul(ps[:, p0 + half, :], lhsT=qmg[:, k, b * 128:(b + 1) * 128], rhs=w[:, k, :],
                                           start=(k == 0), stop=(k == 7))
                    return ins
                S.op("pe", fn, reads=[bufs[0][1], bufs[1][1]] + r_qmg, writes=[r_ps[p0], r_ps[p1]])
                xb = xs[b]
                xv = xb[:].rearrange("p (a m) -> p a m", a=2)
                S.op("dve", lambda e, xv=xv, p0=p0: e.scalar_tensor_tensor(out=xv, in0=ps[:, p0:p0 + 2, :], scalar=0.5, in1=xv,
                                                                          op0=ALU.mult, op1=ALU.add),
                     reads=[r_ps[p0], r_ps[p1], rx[b]], writes=[rx[b]])
                si = stat_i[0] % 4
                stat_i[0] += 1
                rs = r_stat[si]
                bg, rbg = big[b % 2], r_big[b % 2]
                S.op("act", lambda e, bg=bg, xb=xb: e.activation(out=bg[:], in_=xb[:], func=AF.Square), reads=[rx[b]], writes=[rbg])
                S.op("dve", lambda e, bg=bg, si=si: e.tensor_reduce(out=stat[:, 4 * si:4 * si + 1], in_=bg[:], axis=AX.X, op=ALU.add),
                     reads=[rbg], writes=[rs])
                rstd_ops(si, rs)
                S.op("dve", lambda e, xb=xb, si=si: e.scalar_tensor_tensor(
                    out=xb[:], in0=xb[:], scalar=stat[:, 4 * si + 1:4 * si + 2], in1=fgrep[:], op0=ALU.mult, op1=ALU.mult),
                    reads=[rx[b], rs, r_fgrep], writes=[rx[b]])
                r0 = t * T + b * 128
                S.dma(lambda e, xb=xb, r0=r0: e.dma_start(out=out[r0:r0 + 128, :], in_=xb[:]), rx[b], reads=[rx[b]], qeng="pool", final=True)

        load_x(0)
        stage_A(0)
        stop = tuple(stop_after) if stop_after is not None else None
        for t in range(ntiles):
            more = (t + 1 < ntiles)
            dsteps = stage_D(t)
            if stop == ("B", t) or stop == ("C", t):
                stage_BC(t)
                break
            stage_BC(t, dsteps)
            if more:
                load_x(t + 1)
            while dsteps:
                dsteps.pop(0)()
            if stop == ("D", t):
                break
            stage_E(t, a_next=(t + 1 if more else None))
            if stop == ("E", t):
                break
            stage_F(t)
            if stop == ("F", t):
                break
        dumpable = {
            "hT0": (hT[0], r_hT[0], [128, 8, T], BF16), "yr": (yr, r_yr, [128, 8, T], BF16), "ya": (ya, r_ya, [128, 8, T], BF16),
            "qmg": (qmg, r_qmg, [128, 8, T], BF16), "sga": (sga, r_sga, [128, 8, T], BF16), "kT": (kT, r_kT, [128, 4, 128 + T], BF16),
            "vtok": (vtok, [r_vtok], [128, 5, 256], BF16), "vT": (vT, [r_vT], [128, 8, 8], F32), "vd": (vd, [r_vd], [128, 4, 8], F32),
            "x00": (xset[0][0], [r_xset[0][0]], [128, D], F32),
        }
        for name in dump:
            tens, rl, shape, dt = dumpable[name]
            dd = nc.dram_tensor("dbg_" + name, shape, dt, kind="ExternalOutput").ap()
            S.dma(lambda e, dd=dd, tens=tens: e.dma_start(out=dd, in_=tens[:]), rl[0], reads=rl, qeng="pool", final=True)
        S.finish("pool")
        S.run_block()
    return nc


def make_in_maps(inputs):
    consts = _consts()
    maps = []
    xs = np.ascontiguousarray(inputs["x"], dtype=np.float32)
    for c in range(NCORES):
        m = {"x": xs[2 * c:2 * c + 2].reshape(2 * SEQ, D)}
        for k in ("norm_g", "w_in", "conv_w", "conv_b", "lru_w_a", "lru_b_a", "lru_w_x", "lru_b_x", "lru_lambda",
                  "attn_sinks", "w_rnn_out", "w_attn_out", "w_o"):
            m[k] = np.ascontiguousarray(inputs[k], dtype=np.float32)
        m["final_norm_g"] = np.ascontiguousarray(inputs["final_norm_g"], dtype=np.float32).reshape(1, D)
        m.update(consts)
        maps.append(m)
    return maps


def kernel(**inputs):
    nc = build_program()
    in_maps = make_in_maps(inputs)
    res = run_bass_kernel_spmd(nc, in_maps, core_ids=list(range(NCORES)))
    outs = [np.asarray(r["out"], dtype=np.float32).reshape(2, SEQ, D) for r in res.results]
    return np.concatenate(outs, axis=0)
```

```python
import math
from contextlib import ExitStack

import numpy as np

import concourse.bass as bass
import concourse.mybir as mybir
from concourse.bass_utils import run_bass_kernel_spmd

F32 = mybir.dt.float32
BF16 = mybir.dt.bfloat16
AF = mybir.ActivationFunctionType
ALU = mybir.AluOpType
AX = mybir.AxisListType

NCORES = 8
SEQ = 2048
D = 1024
DIN = 6656
T = 512
TPS = SEQ // T
NT = 2 * TPS
OFF_U, OFF_G, OFF_Q, OFF_K, OFF_V, OFF_GA, OFF_MR, OFF_MA = 0, 1024, 2048, 3072, 3328, 3584, 4608, 5632
NSLOT = 20
EPS = 1e-6


class Res:
    __slots__ = ("name", "w", "rs", "dsem", "dcnt")

    def __init__(self, name):
        self.name = name
        self.w = None
        self.rs = {}
        self.dsem = None
        self.dcnt = 0


class Sched:
    CE = ("pe", "act", "dve", "pool")

    def __init__(self, nc, stack, same_engine_sync=True):
        self.nc = nc
        self.stack = stack
        self.q = {e: [] for e in self.CE + ("sp",)}
        self.sem = {e: stack.enter_context(nc.semaphore("s_" + e)) for e in self.CE}
        self.cnt = {e: 0 for e in self.CE}
        self.waited = {}
        self.same = same_engine_sync
        self.nsem = 0
        self.final = []
        self.pool_dmas = []

    @staticmethod
    def _flat(xs):
        out = []
        for x in xs:
            if isinstance(x, (list, tuple)):
                out.extend(Sched._flat(x))
            else:
                out.append(x)
        return out

    def _deps(self, reads, writes):
        deps = []
        for r in reads:
            if r.w is not None:
                deps.append(r.w)
        for r in writes:
            if r.w is not None:
                deps.append(r.w)
            deps.extend(r.rs.values())
        return deps

    def _need(self, eng, deps):
        best = {}
        for sem, val, src in deps:
            if src == eng and (eng == "pe" or not self.same):
                continue
            k = id(sem)
            if k not in best or val > best[k][1]:
                best[k] = (sem, val)
        out = []
        for k, (sem, val) in best.items():
            if self.waited.get((eng, k), 0) >= val:
                continue
            self.waited[(eng, k)] = val
            out.append((sem, val))
        return out

    @staticmethod
    def _addr(r, tok):
        k = id(tok[0])
        if k not in r.rs or r.rs[k][1] < tok[1]:
            r.rs[k] = tok

    def op(self, eng, fn, reads=(), writes=()):
        reads, writes = self._flat(reads), self._flat(writes)
        waits = self._need(eng, self._deps(reads, writes))
        self.cnt[eng] += 1
        tok = (self.sem[eng], self.cnt[eng], eng)
        self.q[eng].append((waits, fn, (self.sem[eng], 1)))
        for r in reads:
            self._addr(r, tok)
        for r in writes:
            r.w = tok
            r.rs = {}

    def dma(self, fn, owner, reads=(), writes=(), qeng="sp", final=False, skip_own=False):
        reads, writes = self._flat(reads), self._flat(writes)
        kind = 0 if qeng == "pool" else 1
        if owner.dsem is None:
            owner.dsem = [None, None]
            owner.dcnt = [0, 0]
        if owner.dsem[kind] is None:
            owner.dsem[kind] = self.stack.enter_context(self.nc.semaphore("d%d" % self.nsem))
            self.nsem += 1
        deps = self._deps(reads, writes)
        if skip_own:
            deps = [d for d in deps if d[0] is not owner.dsem[kind]]
        if qeng == "pool" and not skip_own and len(self.pool_dmas) >= 3:
            deps.append(self.pool_dmas[-3])
        waits = self._need(qeng, deps)
        owner.dcnt[kind] += 16
        tok = (owner.dsem[kind], owner.dcnt[kind], "dma")
        self.q[qeng].append((waits, fn, (owner.dsem[kind], 16)))
        for r in reads:
            self._addr(r, tok)
        for r in writes:
            r.w = tok
            r.rs = {}
        if final:
            self.final.append(tok)
        if qeng == "pool":
            if skip_own and self.pool_dmas and self.pool_dmas[-1][0] is tok[0]:
                self.pool_dmas[-1] = tok
            else:
                self.pool_dmas.append(tok)

    def finish(self, qeng="sp"):
        waits = self._need(qeng, self.final)
        self.q[qeng].append((waits, None, None))

    def replay(self, name, e):
        for waits, fn, inc in self.q[name]:
            for sem, val in waits:
                e.wait_ge(sem, val)
            if fn is None:
                continue
            ins = fn(e)
            if inc is not None:
                ins.then_inc(inc[0], inc[1])

    def run_block(self):
        with self.nc.Block() as block:
            @block.tensor
            def _(e):
                self.replay("pe", e)

            @block.scalar
            def _(e):
                self.replay("act", e)

            @block.vector
            def _(e):
                self.replay("dve", e)

            @block.gpsimd
            def _(e):
                self.replay("pool", e)

            @block.sync
            def _(e):
                self.replay("sp", e)


class Ring:
    def __init__(self, items):
        self.items = items
        self.i = 0

    def get(self):
        it = self.items[self.i % len(self.items)]
        self.i += 1
        return it


def _consts():
    c = {}
    c["c_ident"] = np.eye(128, dtype=np.float32)
    perm = np.zeros((128, 128), np.float32)
    for m in range(128):
        d = m % 64
        if d < 8:
            perm[m + 8, m] = 1.0
        elif d < 16:
            perm[m - 8, m] = 1.0
    c["c_perm"] = perm
    k = np.arange(128)[:, None]
    q = np.arange(128)[None, :]
    mask = np.concatenate([(q < k), (q >= k)], axis=1).astype(np.float32)
    c["c_mask"] = np.concatenate([mask, mask], axis=1)
    eh = np.zeros((128, 16, 16), np.float32)
    for h in range(16):
        eh[:, h, h] = 1.0
    c["c_eh"] = eh.reshape(128, 256)
    bc = np.zeros((16, 8, 128), np.float32)
    for cc in range(8):
        for p in range(128):
            bc[2 * cc + p // 64, cc, p] = 1.0
    c["c_bc"] = bc.reshape(16, 1024)
    pos = np.arange(SEQ, dtype=np.float32)
    inv_freq = (np.float32(500000.0) ** (-np.arange(0, 16, 2, dtype=np.float32) / np.float32(16))).astype(np.float32)
    ang = (pos[:, None] * inv_freq[None, :]).astype(np.float32)
    cos = np.cos(ang).astype(np.float32)
    sin = np.sin(ang).astype(np.float32)
    C = np.ones((128, SEQ), np.float32)
    Sg = np.zeros((128, SEQ), np.float32)
    for p in range(128):
        d = p % 64
        if d < 8:
            C[p] = cos[:, d]
            Sg[p] = -sin[:, d]
        elif d < 16:
            C[p] = cos[:, d - 8]
            Sg[p] = sin[:, d - 8]
    c["c_ropeC"] = C
    c["c_ropeS"] = Sg
    return c


def build_program(ntiles=NT, same_engine_sync=True, stop_after=None, dump=()):
    nc = bass.Bass("TRN2", target_bir_lowering=False)

    def din(name, shape):
        return nc.dram_tensor(name, shape, F32, kind="ExternalInput").ap()

    x = din("x", [2 * SEQ, D])
    norm_g = din("norm_g", [1, D])
    w_in = din("w_in", [1, D, DIN])
    conv_w = din("conv_w", [1, 4, D])
    conv_b = din("conv_b", [1, D])
    lru_w_a = din("lru_w_a", [1, 8, 128, 128])
    lru_b_a = din("lru_b_a", [1, D])
    lru_w_x = din("lru_w_x", [1, 8, 128, 128])
    lru_b_x = din("lru_b_x", [1, D])
    lru_lambda = din("lru_lambda", [1, D])
    attn_sinks = din("attn_sinks", [1, 16])
    w_rnn_out = din("w_rnn_out", [1, D, D])
    w_attn_out = din("w_attn_out", [1, D, D])
    w_o = din("w_o", [1, D, D])
    final_norm_g = din("final_norm_g", [1, D])
    c_ident = din("c_ident", [128, 128])
    c_perm = din("c_perm", [128, 128])
    c_mask = din("c_mask", [128, 512])
    c_eh = din("c_eh", [128, 256])
    c_bc = din("c_bc", [16, 1024])
    c_ropeC = din("c_ropeC", [128, SEQ])
    c_ropeS = din("c_ropeS", [128, SEQ])
    wstream = nc.dram_tensor("wstream", [NSLOT, 128, 4096], BF16, kind="Internal").ap()
    out = nc.dram_tensor("out", [2 * SEQ, D], F32, kind="ExternalOutput").ap()

    win_v = w_in.rearrange("o (k p) c -> p (o k) c", p=128)
    wro_v = w_rnn_out.rearrange("o (k p) c -> p (o k) c", p=128)
    wao_v = w_attn_out.rearrange("o (k p) c -> p (o k) c", p=128)
    wo_v = w_o.rearrange("o (k p) c -> p (o k) c", p=128)

    with ExitStack() as st:
        S = Sched(nc, st, same_engine_sync=same_engine_sync)

        def sb(name, shape, dt):
            return st.enter_context(nc.sbuf_tensor(name, shape, dt))

        ps = st.enter_context(nc.psum_tensor("ps", [128, 7, 512], F32))
        psT = st.enter_context(nc.psum_tensor("psT", [128, 1024], BF16))
        r_ps = [Res("ps%d" % i) for i in range(7)]
        r_psT = Res("psT")

        ident = sb("ident", [128, 128], BF16); r_ident = Res("ident")
        permb = sb("permb", [128, 128], BF16); r_perm = Res("perm")
        maskb = sb("maskb", [128, 2, 256], BF16); r_mask = Res("mask")
        ehb = sb("ehb", [128, 256], BF16); r_eh = Res("eh")
        bcs = sb("bcs", [16, 1024], F32); r_bc = Res("bc")
        grep = sb("grep", [128, D], F32); r_grep = Res("grep")
        fgrep = sb("fgrep", [128, D], F32); r_fgrep = Res("fgrep")
        mhalf = sb("mhalf", [128, 1], F32); r_mhalf = Res("mhalf")
        vrow = sb("vrow", [64, 128], F32); r_vrow = Res("vrow")
        identf = sb("identf", [128, 128], F32); r_identf = Res("identf")
        vT = sb("vT", [128, 8, 8], F32); r_vT = Res("vT")
        vd = sb("vd", [128, 4, 8], F32); r_vd = Res("vd")
        sinkt = sb("sinkt", [16, 2], F32); r_sink = Res("sink")
        lruA = sb("lruA", [128, 8, 128], BF16); r_lruA = Res("lruA")
        lruX = sb("lruX", [128, 8, 128], BF16); r_lruX = Res("lruX")

        wring = [sb("wring%d" % i, [128, 4096], BF16) for i in range(4)]
        r_wring = [Res("wring%d" % i) for i in range(4)]
        hT = [sb("hT%d" % i, [128, 8, T], BF16) for i in range(2)]
        r_hT = [[Res("hT%d_%d" % (i, b)) for b in range(4)] for i in range(2)]
        yr = sb("yr", [128, 8, T], BF16); r_yr = [Res("yr%d" % i) for i in range(8)]
        ya = sb("ya", [128, 8, T], BF16); r_ya = [Res("ya%d" % i) for i in range(4)]
        qmg = sb("qmg", [128, 8, T], BF16); r_qmg = [Res("qmg%d" % i) for i in range(8)]
        sga = sb("sga", [128, 8, T], BF16); r_sga = [Res("sga%d" % i) for i in range(8)]
        kT = sb("kT", [128, 4, 128 + T], BF16); r_kT = [Res("kT%d" % i) for i in range(4)]
        vtok = sb("vtok", [128, 5, 256], BF16); r_vtok = Res("vtok")
        halo = sb("halo", [128, 8, 4], F32); r_halo = [Res("halo%d" % i) for i in range(8)]
        hst = sb("hst", [128, 8], F32); r_hst = [Res("hst%d" % i) for i in range(8)]
        ropeC = [sb("ropeC%d" % i, [128, T], F32) for i in range(2)]
        ropeS = [sb("ropeS%d" % i, [128, T], F32) for i in range(2)]
        r_rope = [Res("rope%d" % i) for i in range(2)]
        xset = [[sb("x%d_%d" % (i, b), [128, D], F32) for b in range(4)] for i in range(2)]
        r_xset = [[Res("x%d_%d" % (i, b)) for b in range(4)] for i in range(2)]
        xnb = [sb("xnb%d" % i, [128, D], BF16) for i in range(2)]
        r_xnb = [Res("xnb%d" % i) for i in range(2)]
        big = [sb("big%d" % i, [128, D], F32) for i in range(2)]
        r_big = [Res("big%d" % i) for i in range(2)]
        stat = sb("stat", [128, 16], F32)
        r_stat = [Res("stat%d" % i) for i in range(4)]
        def mkring(name, n, shape, dt):
            return Ring([(sb("%s%d" % (name, i), shape, dt), Res("%s%d" % (name, i))) for i in range(n)])
        wk_ue = mkring("wue", 2, [128, 520], F32)
        wk_uc = mkring("wuc", 3, [128, 512], F32)
        wk_sg = mkring("wsg", 3, [128, 512], F32)
        wk = mkring("wk", 8, [128, 512], F32)
        wb_ucb = mkring("wucb", 3, [128, 512], BF16)
        wb = mkring("wb", 3, [128, 512], BF16)
        pT_t = [sb("pT%d" % i, [128, 2, 256], BF16) for i in range(3)]
        pTr = Ring([(pT_t[i], Res("pT%d" % i)) for i in range(3)])
        dsm = sb("dsm", [16, 2, 128], F32); r_dsm = [Res("dsm0"), Res("dsm1")]

        psr = Ring([(i, r_ps[i]) for i in range(7)])

        def pbank(i):
            return ps[:, i, :]

        def cast_load(dst_ap, src_ap, res, reads=(), skip_own=False):
            S.dma(lambda e, d=dst_ap, s=src_ap: e.dma_start(out=d, in_=s), res, reads=reads, writes=[res], qeng="pool",
                  skip_own=skip_own)

        def load(dst_ap, src_ap, res, slow=False):
            S.dma(lambda e, d=dst_ap, s=src_ap, sl=slow: e.dma_start(out=d, in_=s, allow_slow_non_contiguous=sl),
                  res, writes=[res])

        cast_load(ident[:], c_ident, r_ident)
        cast_load(permb[:], c_perm, r_perm)
        cast_load(maskb[:].rearrange("p a m -> p (a m)"), c_mask, r_mask)
        cast_load(ehb[:], c_eh, r_eh)
        cast_load(lruA[:], lru_w_a.rearrange("o n c d -> c (o n) d"), r_lruA)
        cast_load(lruX[:], lru_w_x.rearrange("o n c d -> c (o n) d"), r_lruX)
        load(bcs[:], c_bc, r_bc)
        load(grep[:], norm_g.partition_broadcast(128), r_grep)
        load(fgrep[:], final_norm_g.partition_broadcast(128), r_fgrep)
        vsrc = [conv_w[0, 0:1, :], conv_w[0, 1:2, :], conv_w[0, 2:3, :], conv_w[0, 3:4, :], conv_b, lru_b_a, lru_b_x, lru_lambda]
        for i, v in enumerate(vsrc):
            load(vrow[8 * i:8 * i + 8, :], v.rearrange("o (n p) -> (o n) p", p=128), r_vrow)
        load(identf[:], c_ident, r_identf)
        load(sinkt[:, 0:1], attn_sinks.rearrange("o h -> h o"), r_sink, slow=True)
        S.op("pe", lambda e: e.transpose(ps[:, 0, 0:64], vrow[:, :], identf[0:64, 0:64]), reads=[r_vrow, r_identf], writes=[r_ps[0]])
        S.op("act", lambda e: e.activation(out=vT[:].rearrange("p a b -> p (a b)"), in_=ps[:, 0, 0:64], func=AF.Copy),
             reads=[r_ps[0]], writes=[r_vT])

        S.op("pool", lambda e: e.memset(mhalf[:], -0.5), writes=[r_mhalf])
        S.op("dve", lambda e: e.tensor_scalar(out=vd[:, 0:2, :], in0=vT[:, 5:7, :], scalar1=0.5, scalar2=None, op0=ALU.mult),
             reads=[r_vT], writes=[r_vd])
        S.op("act", lambda e: e.activation(out=vd[:, 2, :], in_=vT[:, 7, :], func=AF.Exp, scale=-1.0), reads=[r_vT], writes=[r_vd])
        S.op("act", lambda e: e.activation(out=vd[:, 3, :], in_=vd[:, 2, :], func=AF.Ln, bias=1.0), reads=[r_vd], writes=[r_vd])
        S.op("dve", lambda e: e.tensor_scalar(out=vd[:, 2, :], in0=vd[:, 3, :], scalar1=-4.0, scalar2=None, op0=ALU.mult),
             reads=[r_vd], writes=[r_vd])
        S.op("dve", lambda e: e.tensor_scalar(out=vd[:, 3, :], in0=vd[:, 3, :], scalar1=-8.0, scalar2=None, op0=ALU.mult),
             reads=[r_vd], writes=[r_vd])
        S.op("act", lambda e: e.activation(out=sinkt[:, 1:2], in_=sinkt[:, 0:1], func=AF.Exp), reads=[r_sink], writes=[r_sink])

        def slot_units(s):
            res = []
            if s < 4:
                res.append((0, 256, win_v[:, :, OFF_U + 2 * s * 128: OFF_U + (2 * s + 2) * 128]))
                res.append((256, 512, win_v[:, :, OFF_G + 2 * s * 128: OFF_G + (2 * s + 2) * 128]))
            elif s == 4:
                for g in range(4):
                    for hf in range(2):
                        res.append((g * 128 + hf * 64, g * 128 + hf * 64 + 64, win_v[:, :, OFF_K + g * 64: OFF_K + (g + 1) * 64]))
            elif s == 5:
                res.append((0, 256, win_v[:, :, OFF_V:OFF_V + 256]))
                res.append((256, 512, win_v[:, :, OFF_Q:OFF_Q + 256]))
            elif s == 6:
                res.append((0, 512, win_v[:, :, OFF_Q + 256:OFF_Q + 768]))
            elif s == 7:
                res.append((0, 256, win_v[:, :, OFF_Q + 768:OFF_Q + 1024]))
                res.append((256, 512, win_v[:, :, OFF_GA:OFF_GA + 256]))
            elif s == 8:
                res.append((0, 512, win_v[:, :, OFF_GA + 256:OFF_GA + 768]))
            elif s == 9:
                res.append((0, 256, win_v[:, :, OFF_GA + 768:OFF_GA + 1024]))
            elif s < 18:
                i, y = (s - 10) // 2, (s - 10) % 2
                if y == 0:
                    res.append((0, 256, wro_v[:, :, 2 * i * 128:(2 * i + 2) * 128]))
                    res.append((256, 512, wao_v[:, :, 2 * i * 128:(2 * i + 2) * 128]))
                else:
                    res.append((0, 256, win_v[:, :, OFF_MR + 2 * i * 128: OFF_MR + (2 * i + 2) * 128]))
                    res.append((256, 512, win_v[:, :, OFF_MA + 2 * i * 128: OFF_MA + (2 * i + 2) * 128]))
            else:
                half = s - 18
                res.append((0, 512, wo_v[:, :, half * 512:(half + 1) * 512]))
            return res

        def kview(buf):
            return buf[:].rearrange("p (k c) -> p k c", k=8)

        wslot_i = [0]
        stg_i = [0]
        cast_i = [0]
        r_wstream = [Res("wstream%d" % i) for i in range(NSLOT)]

        def fetch_slot(t, s):
            i = wslot_i[0] % 4
            wslot_i[0] += 1
            buf, res = wring[i], r_wring[i]
            if t == 0:
                pieces = []
                for (c0, c1, src) in slot_units(s):
                    w = c1 - c0
                    for o in range(0, w, 128):
                        ww = min(128, w - o)
                        pieces.append((c0 + o, ww, src[:, :, o:o + ww]))
                last_src = [None, None]
                for (c0, ww, src) in pieces:
                    key = str(src)
                    if last_src[0] == key:
                        stg, rstg = last_src[1]
                    else:
                        i_st = stg_i[0] % 4
                        stg_i[0] += 1
                        stg, rstg = xset[1][i_st], r_xset[1][i_st]
                        sv = stg[:].rearrange("p (k c) -> p k c", k=8)[:, :, 0:ww]
                        S.dma(lambda e, sv=sv, src=src: e.dma_start(out=sv, in_=src), rstg, writes=[rstg])
                        last_src[0], last_src[1] = key, (stg, rstg)
                    sv = stg[:].rearrange("p (k c) -> p k c", k=8)[:, :, 0:ww]
                    dv = kview(buf)[:, :, c0:c0 + ww]
                    if cast_i[0] % 2 == 0:
                        S.op("act", lambda e, dv=dv, sv=sv: e.activation(out=dv, in_=sv, func=AF.Copy), reads=[rstg], writes=[res])
                    else:
                        S.op("dve", lambda e, dv=dv, sv=sv: e.tensor_copy(out=dv, in_=sv), reads=[rstg], writes=[res])
                    cast_i[0] += 1
                S.dma(lambda e, b=buf, s=s: e.dma_start(out=wstream[s], in_=b[:]), res, reads=[res], writes=[r_wstream[s]])
            else:
                S.dma(lambda e, b=buf, s=s: e.dma_start(out=b[:], in_=wstream[s]), res, reads=[r_wstream[s]], writes=[res])
            return buf, res

        def unit(buf, j):
            return kview(buf)[:, :, j * 128:(j + 1) * 128]

        def mm_group(out_ap, lhs_fn, rhs_fn, nk, reads, writes):
            def fn(e, out_ap=out_ap, lhs_fn=lhs_fn, rhs_fn=rhs_fn, nk=nk):
                ins = None
                for k in range(nk):
                    ins = e.matmul(out_ap, lhsT=lhs_fn(k), rhs=rhs_fn(k), start=(k == 0), stop=(k == nk - 1))
                return ins
            S.op("pe", fn, reads=reads, writes=writes)

        def load_x(t):
            xs, rx = xset[t % 2], r_xset[t % 2]
            for b in range(4):
                r0 = t * T + b * 128
                S.dma(lambda e, d=xs[b], r0=r0: e.dma_start(out=d[:], in_=x[r0:r0 + 128, :]), rx[b], writes=[rx[b]])
            pos0 = (t % TPS) * T
            i = t % 2
            S.dma(lambda e, i=i, p=pos0: e.dma_start(out=ropeC[i][:], in_=c_ropeC[:, p:p + T]), r_rope[i], writes=[r_rope[i]])
            S.dma(lambda e, i=i, p=pos0: e.dma_start(out=ropeS[i][:], in_=c_ropeS[:, p:p + T]), r_rope[i], writes=[r_rope[i]])

        def rstd_ops(sidx, rs):
            c0 = 4 * sidx
            S.op("pool", lambda e, c0=c0: e.tensor_scalar(out=stat[:, c0 + 1:c0 + 2], in0=stat[:, c0:c0 + 1], scalar1=1.0 / D,
                                                          scalar2=EPS, op0=ALU.mult, op1=ALU.add), reads=[rs], writes=[rs])
            S.op("pool", lambda e, c0=c0: e.tensor_tensor(out=stat[:, c0 + 1:c0 + 2], in0=stat[:, c0 + 1:c0 + 2], in1=mhalf[:],
                                                          op=ALU.pow), reads=[rs, r_mhalf], writes=[rs])

        stat_i = [0]

        def stage_A(t, blocks=(0, 1, 2, 3), part="both"):
            hTt, rh = hT[t % 2], r_hT[t % 2]
            xs, rx = xset[t % 2], r_xset[t % 2]
            for b in blocks:
                xb_, rxb = xnb[b % 2], r_xnb[b % 2]
                if part in ("both", "tr"):
                    def tr(e, xb_=xb_):
                        ins = None
                        for c in range(8):
                            ins = e.transpose(psT[:, c * 128:(c + 1) * 128], xb_[:, c * 128:(c + 1) * 128], ident[:])
                        return ins
                if part == "tr":
                    S.op("pe", tr, reads=[rxb, r_ident], writes=[r_psT])
                    S.op("act", lambda e, hTt=hTt, b=b: e.activation(
                        out=hTt[:, :, b * 128:(b + 1) * 128], in_=psT[:].rearrange("p (c m) -> p c m", c=8), func=AF.Copy),
                        reads=[r_psT], writes=[rh[b]])
                    continue
                si = stat_i[0] % 4
                stat_i[0] += 1
                rs = r_stat[si]
                bg, rbg = big[b % 2], r_big[b % 2]
                S.op("act", lambda e, bg=bg, xb=xs[b]: e.activation(out=bg[:], in_=xb[:], func=AF.Square), reads=[rx[b]], writes=[rbg])
                S.op("dve", lambda e, bg=bg, si=si: e.tensor_reduce(out=stat[:, 4 * si:4 * si + 1], in_=bg[:], axis=AX.X, op=ALU.add),
                     reads=[rbg], writes=[rs])
                rstd_ops(si, rs)
                S.op("dve", lambda e, xb_=xb_, xb=xs[b], si=si: e.scalar_tensor_tensor(
                    out=xb_[:], in0=xb[:], scalar=stat[:, 4 * si + 1:4 * si + 2], in1=grep[:], op0=ALU.mult, op1=ALU.mult),
                    reads=[rx[b], rs, r_grep], writes=[rxb])
                if part == "elem":
                    continue
                S.op("pe", tr, reads=[rxb, r_ident], writes=[r_psT])
                S.op("act", lambda e, hTt=hTt, b=b: e.activation(
                    out=hTt[:, :, b * 128:(b + 1) * 128], in_=psT[:].rearrange("p (c m) -> p c m", c=8), func=AF.Copy),
                    reads=[r_psT], writes=[rh[b]])

        def make_B(t):
            hTt, rh = hT[t % 2], r_hT[t % 2]
            first = (t % TPS == 0)
            bstate = {}

            def front(n, buf, rbuf, j0):
                wu, wg = unit(buf, j0), unit(buf, 2 + j0)
                bu, rbu = psr.get()
                mm_group(pbank(bu), lambda k, wu=wu: wu[:, k, :], lambda k: hTt[:, k, :], 8, reads=[rbuf] + rh, writes=[rbu])
                bgp, rbg_ = psr.get()
                mm_group(pbank(bgp), lambda k, wg=wg: wg[:, k, :], lambda k: hTt[:, k, :], 8, reads=[rbuf] + rh, writes=[rbg_])
                ue, rue = wk_ue.get()
                if first:
                    S.op("dve", lambda e, ue=ue: e.memset(ue[:, 0:3], 0.0), writes=[rue])
                else:
                    S.op("pool", lambda e, ue=ue, n=n: e.tensor_copy(out=ue[:, 0:3], in_=halo[:, n, 0:3]),
                         reads=[r_halo[n]], writes=[rue])
                S.op("act", lambda e, ue=ue, bu=bu: e.activation(out=ue[:, 3:3 + T], in_=pbank(bu), func=AF.Copy),
                     reads=[rbu, rue], writes=[rue])
                S.op("pool", lambda e, ue=ue, n=n: e.tensor_copy(out=halo[:, n, 0:3], in_=ue[:, T:T + 3]),
                     reads=[rue], writes=[r_halo[n]])
                sg, rsg = wk_sg.get()
                S.op("act", lambda e, sg=sg, bgp=bgp: e.activation(out=sg[:, 0:T], in_=pbank(bgp), func=AF.Tanh, scale=0.5),
                     reads=[rbg_], writes=[rsg])
                S.op("dve", lambda e, sg=sg, bgp=bgp: e.scalar_tensor_tensor(
                    out=sg[:, 0:T], in0=sg[:, 0:T], scalar=1.0, in1=pbank(bgp), op0=ALU.add, op1=ALU.mult),
                    reads=[rsg, rbg_], writes=[rsg])
                uc, ruc = wk_uc.get()
                S.op("pool", lambda e, ue=ue, uc=uc, n=n: e.tensor_scalar(
                    out=uc[:, 0:T], in0=ue[:, 3:3 + T], scalar1=vT[:, 3, n:n + 1], scalar2=vT[:, 4, n:n + 1], op0=ALU.mult, op1=ALU.add),
                    reads=[rue, r_vT], writes=[ruc])
                for j in (2,):
                    cq, rcq = wk.get()
                    S.op("pool", lambda e, ue=ue, cq=cq, n=n, j=j: e.tensor_scalar(
                        out=cq[:, 0:T], in0=ue[:, j:j + T], scalar1=vT[:, j, n:n + 1], scalar2=0.0, op0=ALU.mult, op1=ALU.add),
                        reads=[rue, r_vT], writes=[rcq])
                    S.op("pool", lambda e, cq=cq, uc=uc: e.tensor_tensor(out=uc[:, 0:T], in0=uc[:, 0:T], in1=cq[:, 0:T], op=ALU.add),
                         reads=[rcq, ruc], writes=[ruc])
                for j in (1, 0):
                    S.op("dve", lambda e, ue=ue, uc=uc, n=n, j=j: e.scalar_tensor_tensor(
                        out=uc[:, 0:T], in0=ue[:, j:j + T], scalar=vT[:, j, n:n + 1], in1=uc[:, 0:T], op0=ALU.mult, op1=ALU.add),
                        reads=[rue, r_vT, ruc], writes=[ruc])
                ucb, rucb = wb_ucb.get()
                return (n, uc, ruc, ucb, rucb, sg, rsg)

            def back(n, uc, ruc, ucb, rucb, sg, rsg):
                br, rbr = psr.get()
                mm_group(pbank(br), lambda k, n=n: lruA[:, n, :], lambda k, ucb=ucb: ucb[:], 1, reads=[r_lruA, rucb], writes=[rbr])
                bi, rbi = psr.get()
                mm_group(pbank(bi), lambda k, n=n: lruX[:, n, :], lambda k, ucb=ucb: ucb[:], 1, reads=[r_lruX, rucb], writes=[rbi])
                tr_, rtr = wk.get()
                S.op("act", lambda e, tr_=tr_, br=br, n=n: e.activation(out=tr_[:, 0:T], in_=pbank(br), func=AF.Tanh, scale=0.5,
                                                                       bias=vd[:, 0, n:n + 1]), reads=[rbr, r_vd], writes=[rtr])
                iu, riu = wk.get()
                S.op("act", lambda e, iu=iu, bi=bi, n=n: e.activation(out=iu[:, 0:T], in_=pbank(bi), func=AF.Tanh, scale=0.5,
                                                                     bias=vd[:, 1, n:n + 1]), reads=[rbi, r_vd], writes=[riu])
                a_, ra = wk.get()
                S.op("act", lambda e, a_=a_, tr_=tr_, n=n: e.activation(out=a_[:, 0:T], in_=tr_[:, 0:T], func=AF.Exp,
                                                                       scale=vd[:, 2, n:n + 1], bias=vd[:, 2, n:n + 1]),
                     reads=[rtr, r_vd], writes=[ra])
                s_, rs_ = wk.get()
                S.op("act", lambda e, s_=s_, tr_=tr_, n=n: e.activation(out=s_[:, 0:T], in_=tr_[:, 0:T], func=AF.Exp,
                                                                       scale=vd[:, 3, n:n + 1], bias=vd[:, 3, n:n + 1]),
                     reads=[rtr, r_vd], writes=[rs_])
                S.op("act", lambda e, s_=s_: e.activation(out=s_[:, 0:T], in_=s_[:, 0:T], func=AF.Sqrt, scale=-1.0, bias=1.0),
                     reads=[rs_], writes=[rs_])
                S.op("dve", lambda e, iu=iu, uc=uc: e.scalar_tensor_tensor(out=iu[:, 0:T], in0=iu[:, 0:T], scalar=1.0, in1=uc[:, 0:T],
                                                                          op0=ALU.add, op1=ALU.mult), reads=[riu, ruc], writes=[riu])
                S.op("dve", lambda e, iu=iu, s_=s_: e.scalar_tensor_tensor(out=iu[:, 0:T], in0=s_[:, 0:T], scalar=0.5, in1=iu[:, 0:T],
                                                                          op0=ALU.mult, op1=ALU.mult), reads=[riu, rs_], writes=[riu])
                h_, rh_ = wk.get()
                if first:
                    S.op("dve", lambda e, h_=h_, a_=a_, iu=iu: e.tensor_tensor_scan(
                        out=h_[:, 0:T], data0=a_[:, 0:T], data1=iu[:, 0:T], initial=0.0, op0=ALU.mult, op1=ALU.add),
                        reads=[ra, riu], writes=[rh_])
                else:
                    S.op("dve", lambda e, h_=h_, a_=a_, iu=iu, n=n: e.tensor_tensor_scan(
                        out=h_[:, 0:T], data0=a_[:, 0:T], data1=iu[:, 0:T], initial=hst[:, n:n + 1], op0=ALU.mult, op1=ALU.add),
                        reads=[ra, riu, r_hst[n]], writes=[rh_])
                S.op("pool", lambda e, h_=h_, n=n: e.tensor_copy(out=hst[:, n:n + 1], in_=h_[:, T - 1:T]),
                     reads=[rh_], writes=[r_hst[n]])
                S.op("dve", lambda e, h_=h_, sg=sg, n=n: e.scalar_tensor_tensor(
                    out=yr[:, n, :], in0=h_[:, 0:T], scalar=0.5, in1=sg[:, 0:T], op0=ALU.mult, op1=ALU.mult),
                    reads=[rh_, rsg], writes=[r_yr[n]])

            def front_n(n):
                if n % 2 == 0:
                    bstate["buf"] = fetch_slot(t, n // 2)
                buf, rbuf = bstate["buf"]
                return front(n, buf, rbuf, n % 2)

            def front_b(n, uc, ruc, ucb, rucb, sg, rsg):
                S.op("act", lambda e, uc=uc, ucb=ucb: e.activation(out=ucb[:], in_=uc[:, 0:T], func=AF.Copy), reads=[ruc], writes=[rucb])
            return front_n, front_b, back

        def rope_chain(pb, rpb, dst_ap, rdst, t):
            i = t % 2
            raw, rraw = wb.get()
            S.op("act", lambda e, raw=raw, pb=pb: e.activation(out=raw[:], in_=pbank(pb), func=AF.Copy), reads=[rpb], writes=[rraw])
            return (raw, rraw, dst_ap, rdst, i)

        def rope_finish(raw, rraw, dst_ap, rdst, i):
            p2, rp2 = psr.get()
            mm_group(pbank(p2), lambda k: permb[:], lambda k, raw=raw: raw[:], 1, reads=[r_perm, rraw], writes=[rp2])
            t2, rt2 = wk.get()
            S.op("pool", lambda e, t2=t2, raw=raw, i=i: e.tensor_tensor(out=t2[:, 0:T], in0=raw[:], in1=ropeC[i][:], op=ALU.mult),
                 reads=[rraw, r_rope[i]], writes=[rt2])
            t1, rt1 = wk.get()
            S.op("dve", lambda e, t1=t1, p2=p2, i=i: e.tensor_tensor(out=t1[:, 0:T], in0=pbank(p2), in1=ropeS[i][:], op=ALU.mult),
                 reads=[rp2, r_rope[i]], writes=[rt1])
            S.op("dve", lambda e, t1=t1, t2=t2, dst_ap=dst_ap: e.tensor_tensor(out=dst_ap, in0=t1[:, 0:T], in1=t2[:, 0:T], op=ALU.add),
                 reads=[rt1, rt2], writes=[rdst])

        def make_C(t):
            hTt, rh = hT[t % 2], r_hT[t % 2]
            units = [("kd", g) for g in range(4)] + [("v", 0), ("v", 1)] + [("q", c) for c in range(8)] + \
                    [("ga", c) for c in range(8)] + [("pad", 0), ("pad", 1)]
            pend = []
            cstate = {}

            def unit_fn(ui):
                kind, i = units[ui]
                j = ui % 4
                if j == 0:
                    cstate["buf"] = fetch_slot(t, 4 + ui // 4)
                buf, rbuf = cstate["buf"]
                if kind in ("kd", "q"):
                    w = unit(buf, j)
                    pb, rpb = psr.get()
                    mm_group(pbank(pb), lambda k, w=w: w[:, k, :], lambda k: hTt[:, k, :], 8, reads=[rbuf] + rh, writes=[rpb])
                    if kind == "kd":
                        dst, rd = kT[:, i, 128:128 + T], r_kT[i]
                    else:
                        dst, rd = qmg[:, i, :], r_qmg[i]
                    pend.append(rope_chain(pb, rpb, dst, rd, t))
                    if len(pend) > 2:
                        rope_finish(*pend.pop(0))
                elif kind == "v" and i == 0:
                    wv = kview(buf)[:, :, j * 128:(j + 2) * 128]
                    for pr in range(2):
                        pb, rpb = psr.get()

                        def fn(e, pb=pb, pr=pr, wv=wv):
                            ins = None
                            for bb in range(2):
                                b = pr * 2 + bb
                                for k in range(8):
                                    ins = e.matmul(ps[:, pb, bb * 256:(bb + 1) * 256], lhsT=hTt[:, k, b * 128:(b + 1) * 128],
                                                   rhs=wv[:, k, :], start=(k == 0), stop=(k == 7))
                            return ins
                        S.op("pe", fn, reads=[rbuf] + rh, writes=[rpb])
                        S.op("act", lambda e, pb=pb, pr=pr: e.activation(
                            out=vtok[:, 1 + 2 * pr:3 + 2 * pr, :], in_=pbank(pb).rearrange("p (b m) -> p b m", b=2), func=AF.Copy),
                            reads=[rpb], writes=[r_vtok])
                elif kind == "ga":
                    w = unit(buf, j)
                    pb, rpb = psr.get()
                    mm_group(pbank(pb), lambda k, w=w: w[:, k, :], lambda k: hTt[:, k, :], 8, reads=[rbuf] + rh, writes=[rpb])
                    tg, rtg = wk.get()
                    S.op("act", lambda e, tg=tg, pb=pb: e.activation(out=tg[:, 0:T], in_=pbank(pb), func=AF.Tanh, scale=0.5),
                         reads=[rpb], writes=[rtg])
                    S.op("dve", lambda e, tg=tg, pb=pb, i=i: e.scalar_tensor_tensor(
                        out=sga[:, i, :], in0=tg[:, 0:T], scalar=1.0, in1=pbank(pb), op0=ALU.add, op1=ALU.mult),
                        reads=[rtg, rpb], writes=[r_sga[i]])

            def flush():
                while pend:
                    rope_finish(*pend.pop(0))
            return unit_fn, flush, len(units)

        def stage_BC(t, dsteps=None):
            frontB, frontB2, backB = make_B(t)
            unitC, flushC, NU = make_C(t)
            pendB = []
            ui = 0
            NP = 10
            for p_ in range(NP):
                if p_ < 8:
                    pendB.append(frontB(p_))
                if p_ >= 2:
                    backB(*pendB.pop(0))
                if p_ < 8:
                    frontB2(*pendB[-1])
                for _ in range(3):
                    if ui < NU:
                        unitC(ui)
                        ui += 1
                        if ui == NU:
                            flushC()
                if ui == NU and dsteps is not None and p_ >= 8:
                    psr.items = [(i, r_ps[i]) for i in (0, 1, 5, 6)]
                    for _ in range(5):
                        if dsteps:
                            dsteps.pop(0)()
            psr.items = [(i, r_ps[i]) for i in range(7)]

        def stage_D(t):
            ts = t % TPS
            S_BANKS = [(0, 0), (5, 0)]
            sring = Ring(S_BANKS)
            DEN, VALS = 2, 3
            rbv = psT[:].bitcast(F32)
            post2_pending = [None]
            steps = []
            for jq in range(4):
                first = (ts == 0 and jq == 0)
                lo = 128 if first else 0
                pend = []

                def s1(i, jq=jq, first=first, lo=lo):
                    c, g = i, i // 2
                    bk, _hf = sring.get()
                    rbk = [r_ps[bk], r_ps[bk + 1]]

                    def fn(e, bk=bk, g=g, c=c):
                        ins = None
                        for hh in range(2):
                            hb = hh * 64
                            if not first:
                                ins = e.matmul(ps[:, bk + hh, 0:128], lhsT=kT[hb:hb + 64, g, jq * 128:(jq + 1) * 128],
                                               rhs=qmg[hb:hb + 64, c, jq * 128:(jq + 1) * 128], start=True, stop=True)
                            ins = e.matmul(ps[:, bk + hh, 128:256], lhsT=kT[hb:hb + 64, g, 128 + jq * 128:128 + (jq + 1) * 128],
                                           rhs=qmg[hb:hb + 64, c, jq * 128:(jq + 1) * 128], start=True, stop=True)
                        return ins
                    S.op("pe", fn, reads=[r_kT[g], r_qmg[c]], writes=[rbk])
                    pt, rpt = pTr.get()
                    S.op("act", lambda e, pt=pt, bk=bk: e.activation(
                        out=pt[:, :, lo:256], in_=ps[:, bk:bk + 2, lo:256], func=AF.Exp, scale=0.125),
                        reads=[rbk], writes=[rpt])
                    S.op("dve", lambda e, pt=pt: e.tensor_tensor(out=pt[:, :, lo:256], in0=pt[:, :, lo:256], in1=maskb[:, :, lo:256], op=ALU.mult),
                         reads=[rpt, r_mask], writes=[rpt])
                    return (i, pt, rpt)

                def s4(i, pt, rpt, jq=jq, first=first):
                    c, g = i, i // 2

                    def fn(e, pt=pt, c=c, g=g):
                        ins = None
                        for hh in range(2):
                            h = 2 * c + hh
                            hb = hh * 64
                            vo = ps[hb:hb + 64, VALS + c // 4, (c % 4) * 128:(c % 4 + 1) * 128]
                            if not first:
                                e.matmul(vo, lhsT=vtok[:, jq, g * 64:(g + 1) * 64], rhs=pt[:, hh, 0:128], start=True, stop=False)
                            e.matmul(vo, lhsT=vtok[:, jq + 1, g * 64:(g + 1) * 64], rhs=pt[:, hh, 128:256], start=first, stop=True)
                            if not first:
                                ins = e.matmul(ps[0:16, DEN, 0:256], lhsT=ehb[:, h * 16:(h + 1) * 16], rhs=pt[:, hh, 0:256], start=(h == 0),
                                               stop=(h == 15))
                            else:
                                ins = e.matmul(ps[0:16, DEN, 128:256], lhsT=ehb[:, h * 16:(h + 1) * 16], rhs=pt[:, hh, 128:256],
                                               start=(h == 0), stop=(h == 15))
                        return ins
                    S.op("pe", fn, reads=[rpt, r_vtok, r_eh], writes=[r_ps[VALS], r_ps[VALS + 1], r_ps[DEN]])

                def post1(jq=jq, first=first):
                    di = jq % 2
                    bg, rbg = big[0], r_big[0]
                    S.op("act", lambda e, bg=bg: e.activation(out=bg[:].rearrange("p (a m) -> p a m", a=2), in_=ps[:, VALS:VALS + 2, :],
                                                              func=AF.Copy), reads=[r_ps[VALS], r_ps[VALS + 1]], writes=[rbg])
                    S.op("dve", lambda e, di=di: e.tensor_scalar(out=dsm[:, di, :], in0=ps[0:16, DEN, 128:256], scalar1=sinkt[:, 1:2],
                                                                  scalar2=None, op0=ALU.add), reads=[r_ps[DEN], r_sink], writes=[r_dsm[di]])
                    if not first:
                        S.op("dve", lambda e, di=di: e.tensor_tensor(out=dsm[:, di, :], in0=dsm[:, di, :], in1=ps[0:16, DEN, 0:128],
                                                                      op=ALU.add), reads=[r_ps[DEN], r_dsm[di]], writes=[r_dsm[di]])
                    S.op("dve", lambda e, di=di: e.reciprocal(out=dsm[:, di, :], in_=dsm[:, di, :]), reads=[r_dsm[di]], writes=[r_dsm[di]])

                def post2(jq=jq):
                    di = jq % 2
                    bg, rbg = big[0], r_big[0]
                    b1, rb1 = big[1], r_big[1]
                    for hf in range(2):
                        def fnb(e, di=di, hf=hf):
                            ins = None
                            for cc in range(4):
                                c = hf * 4 + cc
                                ins = e.matmul(rbv[:, cc * 128:(cc + 1) * 128], lhsT=bcs[:, c * 128:(c + 1) * 128],
                                               rhs=dsm[:, di, :], start=True, stop=True)
                            return ins
                        S.op("pe", fnb, reads=[r_bc, r_dsm[di]], writes=[r_psT])
                        S.op("dve", lambda e, bg=bg, b1=b1, hf=hf: e.tensor_tensor(
                            out=b1[:, hf * 512:(hf + 1) * 512], in0=rbv, in1=bg[:, hf * 512:(hf + 1) * 512], op=ALU.mult),
                            reads=[r_psT, rbg], writes=[rb1])
                    S.op("dve", lambda e, b1=b1, jq=jq: e.scalar_tensor_tensor(
                        out=ya[:, :, jq * 128:(jq + 1) * 128], in0=b1[:].rearrange("p (c m) -> p c m", c=8), scalar=0.5,
                        in1=sga[:, :, jq * 128:(jq + 1) * 128], op0=ALU.mult, op1=ALU.mult),
                        reads=[rb1] + r_sga, writes=[r_ya[jq]])

                SK = 1

                def hstep(i, s1=s1, s4=s4, pend=pend):
                    if i < 8:
                        pend.append(s1(i))
                    if i >= SK:
                        s4(*pend.pop(0))
                    if i == 3 and post2_pending[0] is not None:
                        post2_pending[0]()
                        post2_pending[0] = None
                for i in range(8 + SK):
                    steps.append(lambda i=i, hstep=hstep: hstep(i))

                def pstep(post1=post1, post2=post2):
                    post1()
                    post2_pending[0] = post2
                steps.append(pstep)

            def fin():
                post2_pending[0]()
                S.op("pool", lambda e: e.tensor_copy(out=kT[:, :, 0:128], in_=kT[:, :, T:T + 128]), reads=r_kT, writes=r_kT)
                S.op("pool", lambda e: e.tensor_copy(out=vtok[:, 0, :], in_=vtok[:, 4, :]), reads=[r_vtok], writes=[r_vtok])
            steps.append(fin)
            return steps

        def stage_E(t, a_next=None):
            hTt, rh = hT[t % 2], r_hT[t % 2]
            ebuf = {}
            for f in range(8):
                if a_next is not None:
                    stage_A(a_next, blocks=(f // 2,), part=("elem" if f % 2 == 0 else "tr"))
                if f % 2 == 0:
                    ebuf["x"] = fetch_slot(t, 10 + f)
                    ebuf["y"] = fetch_slot(t, 11 + f)
                (bx, rbx), (by, rby) = ebuf["x"], ebuf["y"]
                rbuf = [rbx, rby]
                wro, wao, wmr, wma = unit(bx, f % 2), unit(bx, 2 + f % 2), unit(by, f % 2), unit(by, 2 + f % 2)
                pc, rpc = psr.get()
                mm_group(pbank(pc), lambda k, w=wmr: w[:, k, :], lambda k: hTt[:, k, :], 8, reads=[rbuf] + rh, writes=[rpc])
                pd, rpd = psr.get()
                mm_group(pbank(pd), lambda k, w=wma: w[:, k, :], lambda k: hTt[:, k, :], 8, reads=[rbuf] + rh, writes=[rpd])
                pa, rpa = psr.get()
                mm_group(pbank(pa), lambda k, w=wro: w[:, k, :], lambda k: yr[:, k, :], 8, reads=[rbuf] + r_yr, writes=[rpa])
                pb, rpb = psr.get()
                mm_group(pbank(pb), lambda k, w=wao: w[:, k, :], lambda k: ya[:, k, :], 8, reads=[rbuf] + r_ya, writes=[rpb])
                tc_, rtc = wk.get()
                S.op("act", lambda e, tc_=tc_, pc=pc: e.activation(out=tc_[:, 0:T], in_=pbank(pc), func=AF.Tanh, scale=0.5),
                     reads=[rpc], writes=[rtc])
                td_, rtd = wk.get()
                S.op("act", lambda e, td_=td_, pd=pd: e.activation(out=td_[:, 0:T], in_=pbank(pd), func=AF.Tanh, scale=0.5),
                     reads=[rpd], writes=[rtd])
                S.op("dve", lambda e, tc_=tc_, pa=pa: e.scalar_tensor_tensor(out=tc_[:, 0:T], in0=tc_[:, 0:T], scalar=1.0, in1=pbank(pa),
                                                                            op0=ALU.add, op1=ALU.mult), reads=[rtc, rpa], writes=[rtc])
                S.op("dve", lambda e, td_=td_, pb=pb: e.scalar_tensor_tensor(out=td_[:, 0:T], in0=td_[:, 0:T], scalar=1.0, in1=pbank(pb),
                                                                            op0=ALU.add, op1=ALU.mult), reads=[rtd, rpb], writes=[rtd])
                S.op("pool", lambda e, tc_=tc_, td_=td_, f=f: e.tensor_tensor(out=qmg[:, f, :], in0=tc_[:, 0:T], in1=td_[:, 0:T], op=ALU.add),
                     reads=[rtc, rtd], writes=[r_qmg[f]])

        def stage_F(t):
            xs, rx = xset[t % 2], r_xset[t % 2]
            bufs = [fetch_slot(t, 18), fetch_slot(t, 19)]
            pairs = [(0, 1), (2, 3), (4, 5)]
            for b in range(4):
                p0, p1 = pairs[b % 3]

                def fn(e, b=b, p0=p0):
                    ins = None
                    for half in range(2):
                        w = bufs[half][0][:].rearrange("p (k m) -> p k m", k=8)
                        for k in range(8):
                            ins = e.matmul(ps[:, p0 + half, :], lhsT=qmg[:, k, b * 128:(b + 1) * 128], rhs=w[:, k, :],
                                           start=(k == 0), stop=(k == 7))
                    return ins
                S.op("pe", fn, reads=[bufs[0][1], bufs[1][1]] + r_qmg, writes=[r_ps[p0], r_ps[p1]])
                xb = xs[b]
                xv = xb[:].rearrange("p (a m) -> p a m", a=2)
                S.op("dve", lambda e, xv=xv, p0=p0: e.scalar_tensor_tensor(out=xv, in0=ps[:, p0:p0 + 2, :], scalar=0.5, in1=xv,
                                                                          op0=ALU.mult, op1=ALU.add),
                     reads=[r_ps[p0], r_ps[p1], rx[b]], writes=[rx[b]])
                si = stat_i[0] % 4
                stat_i[0] += 1
                rs = r_stat[si]
                bg, rbg = big[b % 2], r_big[b % 2]
                S.op("act", lambda e, bg=bg, xb=xb: e.activation(out=bg[:], in_=xb[:], func=AF.Square), reads=[rx[b]], writes=[rbg])
                S.op("dve", lambda e, bg=bg, si=si: e.tensor_reduce(out=stat[:, 4 * si:4 * si + 1], in_=bg[:], axis=AX.X, op=ALU.add),
                     reads=[rbg], writes=[rs])
                rstd_ops(si, rs)
                S.op("dve", lambda e, xb=xb, si=si: e.scalar_tensor_tensor(
                    out=xb[:], in0=xb[:], scalar=stat[:, 4 * si + 1:4 * si + 2], in1=fgrep[:], op0=ALU.mult, op1=ALU.mult),
                    reads=[rx[b], rs, r_fgrep], writes=[rx[b]])
                r0 = t * T + b * 128
                S.dma(lambda e, xb=xb, r0=r0: e.dma_start(out=out[r0:r0 + 128, :], in_=xb[:]), rx[b], reads=[rx[b]], qeng="pool", final=True)

        load_x(0)
        stage_A(0)
        stop = tuple(stop_after) if stop_after is not None else None
        for t in range(ntiles):
            more = (t + 1 < ntiles)
            dsteps = stage_D(t)
            if stop == ("B", t) or stop == ("C", t):
                stage_BC(t)
                break
            stage_BC(t, dsteps)
            if more and t > 0:
                load_x(t + 1)
            while dsteps:
                dsteps.pop(0)()
            if stop == ("D", t):
                break
            stage_E(t, a_next=(t + 1 if (more and t > 0) else None))
            if stop == ("E", t):
                break
            stage_F(t)
            if stop == ("F", t):
                break
            if more and t == 0:
                load_x(1)
                stage_A(1)
        dumpable = {
            "hT0": (hT[0], r_hT[0], [128, 8, T], BF16), "yr": (yr, r_yr, [128, 8, T], BF16), "ya": (ya, r_ya, [128, 8, T], BF16),
            "qmg": (qmg, r_qmg, [128, 8, T], BF16), "sga": (sga, r_sga, [128, 8, T], BF16), "kT": (kT, r_kT, [128, 4, 128 + T], BF16),
            "vtok": (vtok, [r_vtok], [128, 5, 256], BF16), "vT": (vT, [r_vT], [128, 8, 8], F32), "vd": (vd, [r_vd], [128, 4, 8], F32),
            "x00": (xset[0][0], [r_xset[0][0]], [128, D], F32),
        }
        for name in dump:
            tens, rl, shape, dt = dumpable[name]
            dd = nc.dram_tensor("dbg_" + name, shape, dt, kind="ExternalOutput").ap()
            S.dma(lambda e, dd=dd, tens=tens: e.dma_start(out=dd, in_=tens[:]), rl[0], reads=rl, qeng="pool", final=True)
        S.finish("pool")
        S.run_block()
    return nc


def make_in_maps(inputs):
    consts = _consts()
    maps = []
    xs = np.ascontiguousarray(inputs["x"], dtype=np.float32)
    for c in range(NCORES):
        m = {"x": xs[2 * c:2 * c + 2].reshape(2 * SEQ, D)}
        for k in ("norm_g", "w_in", "conv_w", "conv_b", "lru_w_a", "lru_b_a", "lru_w_x", "lru_b_x", "lru_lambda",
                  "attn_sinks", "w_rnn_out", "w_attn_out", "w_o"):
            m[k] = np.ascontiguousarray(inputs[k], dtype=np.float32)
        m["final_norm_g"] = np.ascontiguousarray(inputs["final_norm_g"], dtype=np.float32).reshape(1, D)
        m.update(consts)
        maps.append(m)
    return maps


def kernel(**inputs):
    nc = build_program()
    in_maps = make_in_maps(inputs)
    res = run_bass_kernel_spmd(nc, in_maps, core_ids=list(range(NCORES)))
    outs = [np.asarray(r["out"], dtype=np.float32).reshape(2, SEQ, D) for r in res.results]
    return np.concatenate(outs, axis=0)
```

```python
import math
from contextlib import ExitStack

import numpy as np

import concourse.bass as bass
import concourse.mybir as mybir
from concourse.bass_utils import run_bass_kernel_spmd

F32 = mybir.dt.float32
BF16 = mybir.dt.bfloat16
AF = mybir.ActivationFunctionType
ALU = mybir.AluOpType
AX = mybir.AxisListType

NCORES = 8
SEQ = 2048
D = 1024
DIN = 6656
T = 512
TPS = SEQ // T
NT = 2 * TPS
OFF_U, OFF_G, OFF_Q, OFF_K, OFF_V, OFF_GA, OFF_MR, OFF_MA = 0, 1024, 2048, 3072, 3328, 3584, 4608, 5632
NSLOT = 20
EPS = 1e-6


class Res:
    __slots__ = ("name", "w", "rs", "dsem", "dcnt")

    def __init__(self, name):
        self.name = name
        self.w = None
        self.rs = {}
        self.dsem = None
        self.dcnt = 0


class Sched:
    CE = ("pe", "act", "dve", "pool")

    def __init__(self, nc, stack, same_engine_sync=True):
        self.nc = nc
        self.stack = stack
        self.q = {e: [] for e in self.CE + ("sp",)}
        self.sem = {e: stack.enter_context(nc.semaphore("s_" + e)) for e in self.CE}
        self.cnt = {e: 0 for e in self.CE}
        self.waited = {}
        self.same = same_engine_sync
        self.nsem = 0
        self.final = []
        self.pool_dmas = []

    @staticmethod
    def _flat(xs):
        out = []
        for x in xs:
            if isinstance(x, (list, tuple)):
                out.extend(Sched._flat(x))
            else:
                out.append(x)
        return out

    def _deps(self, reads, writes):
        deps = []
        for r in reads:
            if r.w is not None:
                deps.append(r.w)
        for r in writes:
            if r.w is not None:
                deps.append(r.w)
            deps.extend(r.rs.values())
        return deps

    def _need(self, eng, deps):
        best = {}
        for sem, val, src in deps:
            if src == eng and (eng == "pe" or not self.same):
                continue
            k = id(sem)
            if k not in best or val > best[k][1]:
                best[k] = (sem, val)
        out = []
        for k, (sem, val) in best.items():
            if self.waited.get((eng, k), 0) >= val:
                continue
            self.waited[(eng, k)] = val
            out.append((sem, val))
        return out

    @staticmethod
    def _addr(r, tok):
        k = id(tok[0])
        if k not in r.rs or r.rs[k][1] < tok[1]:
            r.rs[k] = tok

    def op(self, eng, fn, reads=(), writes=()):
        reads, writes = self._flat(reads), self._flat(writes)
        waits = self._need(eng, self._deps(reads, writes))
        self.cnt[eng] += 1
        tok = (self.sem[eng], self.cnt[eng], eng)
        self.q[eng].append((waits, fn, (self.sem[eng], 1)))
        for r in reads:
            self._addr(r, tok)
        for r in writes:
            r.w = tok
            r.rs = {}

    def dma(self, fn, owner, reads=(), writes=(), qeng="sp", final=False, skip_own=False):
        reads, writes = self._flat(reads), self._flat(writes)
        kind = 0 if qeng == "pool" else 1
        if owner.dsem is None:
            owner.dsem = [None, None]
            owner.dcnt = [0, 0]
        if owner.dsem[kind] is None:
            owner.dsem[kind] = self.stack.enter_context(self.nc.semaphore("d%d" % self.nsem))
            self.nsem += 1
        deps = self._deps(reads, writes)
        if skip_own:
            deps = [d for d in deps if d[0] is not owner.dsem[kind]]
        if qeng == "pool" and not skip_own and len(self.pool_dmas) >= 3:
            deps.append(self.pool_dmas[-3])
        waits = self._need(qeng, deps)
        owner.dcnt[kind] += 16
        tok = (owner.dsem[kind], owner.dcnt[kind], "dma")
        self.q[qeng].append((waits, fn, (owner.dsem[kind], 16)))
        for r in reads:
            self._addr(r, tok)
        for r in writes:
            r.w = tok
            r.rs = {}
        if final:
            self.final.append(tok)
        if qeng == "pool":
            if skip_own and self.pool_dmas and self.pool_dmas[-1][0] is tok[0]:
                self.pool_dmas[-1] = tok
            else:
                self.pool_dmas.append(tok)

    def finish(self, qeng="sp"):
        waits = self._need(qeng, self.final)
        self.q[qeng].append((waits, None, None))

    def replay(self, name, e):
        for waits, fn, inc in self.q[name]:
            for sem, val in waits:
                e.wait_ge(sem, val)
            if fn is None:
                continue
            ins = fn(e)
            if inc is not None:
                ins.then_inc(inc[0], inc[1])

    def run_block(self):
        with self.nc.Block() as block:
            @block.tensor
            def _(e):
                self.replay("pe", e)

            @block.scalar
            def _(e):
                self.replay("act", e)

            @block.vector
            def _(e):
                self.replay("dve", e)

            @block.gpsimd
            def _(e):
                self.replay("pool", e)

            @block.sync
            def _(e):
                self.replay("sp", e)


class Ring:
    def __init__(self, items):
        self.items = items
        self.i = 0

    def get(self):
        it = self.items[self.i % len(self.items)]
        self.i += 1
        return it


def _consts():
    c = {}
    c["c_ident"] = np.eye(128, dtype=np.float32)
    perm = np.zeros((128, 128), np.float32)
    for m in range(128):
        d = m % 64
        if d < 8:
            perm[m + 8, m] = 1.0
        elif d < 16:
            perm[m - 8, m] = 1.0
    c["c_perm"] = perm
    k = np.arange(128)[:, None]
    q = np.arange(128)[None, :]
    mask = np.concatenate([(q < k), (q >= k)], axis=1).astype(np.float32)
    c["c_mask"] = np.concatenate([mask, mask], axis=1)
    eh = np.zeros((128, 16, 16), np.float32)
    for h in range(16):
        eh[:, h, h] = 1.0
    c["c_eh"] = eh.reshape(128, 256)
    bc = np.zeros((16, 8, 128), np.float32)
    for cc in range(8):
        for p in range(128):
            bc[2 * cc + p // 64, cc, p] = 1.0
    c["c_bc"] = bc.reshape(16, 1024)
    pos = np.arange(SEQ, dtype=np.float32)
    inv_freq = (np.float32(500000.0) ** (-np.arange(0, 16, 2, dtype=np.float32) / np.float32(16))).astype(np.float32)
    ang = (pos[:, None] * inv_freq[None, :]).astype(np.float32)
    cos = np.cos(ang).astype(np.float32)
    sin = np.sin(ang).astype(np.float32)
    C = np.ones((128, SEQ), np.float32)
    Sg = np.zeros((128, SEQ), np.float32)
    for p in range(128):
        d = p % 64
        if d < 8:
            C[p] = cos[:, d]
            Sg[p] = -sin[:, d]
        elif d < 16:
            C[p] = cos[:, d - 8]
            Sg[p] = sin[:, d - 8]
    c["c_ropeC"] = C
    c["c_ropeS"] = Sg
    return c


def build_program(ntiles=NT, same_engine_sync=True, stop_after=None, dump=()):
    nc = bass.Bass("TRN2", target_bir_lowering=False)

    def din(name, shape):
        return nc.dram_tensor(name, shape, F32, kind="ExternalInput").ap()

    x = din("x", [2 * SEQ, D])
    norm_g = din("norm_g", [1, D])
    w_in = din("w_in", [1, D, DIN])
    conv_w = din("conv_w", [1, 4, D])
    conv_b = din("conv_b", [1, D])
    lru_w_a = din("lru_w_a", [1, 8, 128, 128])
    lru_b_a = din("lru_b_a", [1, D])
    lru_w_x = din("lru_w_x", [1, 8, 128, 128])
    lru_b_x = din("lru_b_x", [1, D])
    lru_lambda = din("lru_lambda", [1, D])
    attn_sinks = din("attn_sinks", [1, 16])
    w_rnn_out = din("w_rnn_out", [1, D, D])
    w_attn_out = din("w_attn_out", [1, D, D])
    w_o = din("w_o", [1, D, D])
    final_norm_g = din("final_norm_g", [1, D])
    c_ident = din("c_ident", [128, 128])
    c_perm = din("c_perm", [128, 128])
    c_mask = din("c_mask", [128, 512])
    c_eh = din("c_eh", [128, 256])
    c_bc = din("c_bc", [16, 1024])
    c_ropeC = din("c_ropeC", [128, SEQ])
    c_ropeS = din("c_ropeS", [128, SEQ])
    wstream = nc.dram_tensor("wstream", [NSLOT, 128, 4096], BF16, kind="Internal").ap()
    out = nc.dram_tensor("out", [2 * SEQ, D], F32, kind="ExternalOutput").ap()

    win_v = w_in.rearrange("o (k p) c -> p (o k) c", p=128)
    wro_v = w_rnn_out.rearrange("o (k p) c -> p (o k) c", p=128)
    wao_v = w_attn_out.rearrange("o (k p) c -> p (o k) c", p=128)
    wo_v = w_o.rearrange("o (k p) c -> p (o k) c", p=128)

    with ExitStack() as st:
        S = Sched(nc, st, same_engine_sync=same_engine_sync)

        def sb(name, shape, dt):
            return st.enter_context(nc.sbuf_tensor(name, shape, dt))

        ps = st.enter_context(nc.psum_tensor("ps", [128, 7, 512], F32))
        psT = st.enter_context(nc.psum_tensor("psT", [128, 1024], BF16))
        r_ps = [Res("ps%d" % i) for i in range(7)]
        r_psT = Res("psT")

        ident = sb("ident", [128, 128], BF16); r_ident = Res("ident")
        permb = sb("permb", [128, 128], BF16); r_perm = Res("perm")
        maskb = sb("maskb", [128, 2, 256], BF16); r_mask = Res("mask")
        ehb = sb("ehb", [128, 256], BF16); r_eh = Res("eh")
        bcs = sb("bcs", [16, 1024], F32); r_bc = Res("bc")
        grep = sb("grep", [128, D], F32); r_grep = Res("grep")
        fgrep = sb("fgrep", [128, D], F32); r_fgrep = Res("fgrep")
        mhalf = sb("mhalf", [128, 1], F32); r_mhalf = Res("mhalf")
        vrow = sb("vrow", [64, 128], F32); r_vrow = Res("vrow")
        identf = sb("identf", [128, 128], F32); r_identf = Res("identf")
        vT = sb("vT", [128, 8, 8], F32); r_vT = Res("vT")
        vd = sb("vd", [128, 4, 8], F32); r_vd = Res("vd")
        sinkt = sb("sinkt", [16, 2], F32); r_sink = Res("sink")
        lruA = sb("lruA", [128, 8, 128], BF16); r_lruA = Res("lruA")
        lruX = sb("lruX", [128, 8, 128], BF16); r_lruX = Res("lruX")

        wring = [sb("wring%d" % i, [128, 4096], BF16) for i in range(4)]
        r_wring = [Res("wring%d" % i) for i in range(4)]
        hT = [sb("hT%d" % i, [128, 8, T], BF16) for i in range(2)]
        r_hT = [[Res("hT%d_%d" % (i, b)) for b in range(4)] for i in range(2)]
        yr = sb("yr", [128, 8, T], BF16); r_yr = [Res("yr%d" % i) for i in range(8)]
        ya = sb("ya", [128, 8, T], BF16); r_ya = [Res("ya%d" % i) for i in range(4)]
        qmg = sb("qmg", [128, 8, T], BF16); r_qmg = [Res("qmg%d" % i) for i in range(8)]
        sga = sb("sga", [128, 8, T], BF16); r_sga = [Res("sga%d" % i) for i in range(8)]
        kT = sb("kT", [128, 4, 128 + T], BF16); r_kT = [Res("kT%d" % i) for i in range(4)]
        vtok = sb("vtok", [128, 5, 256], BF16); r_vtok = Res("vtok")
        halo = sb("halo", [128, 8, 4], F32); r_halo = [Res("halo%d" % i) for i in range(8)]
        hst = sb("hst", [128, 8], F32); r_hst = [Res("hst%d" % i) for i in range(8)]
        ropeC = [sb("ropeC%d" % i, [128, T], F32) for i in range(2)]
        ropeS = [sb("ropeS%d" % i, [128, T], F32) for i in range(2)]
        r_rope = [Res("rope%d" % i) for i in range(2)]
        xset = [[sb("x%d_%d" % (i, b), [128, D], F32) for b in range(4)] for i in range(2)]
        r_xset = [[Res("x%d_%d" % (i, b)) for b in range(4)] for i in range(2)]
        xnb = [sb("xnb%d" % i, [128, D], BF16) for i in range(2)]
        r_xnb = [Res("xnb%d" % i) for i in range(2)]
        big = [sb("big%d" % i, [128, D], F32) for i in range(2)]
        r_big = [Res("big%d" % i) for i in range(2)]
        stat = sb("stat", [128, 16], F32)
        r_stat = [Res("stat%d" % i) for i in range(4)]
        def mkring(name, n, shape, dt):
            return Ring([(sb("%s%d" % (name, i), shape, dt), Res("%s%d" % (name, i))) for i in range(n)])
        wk_ue = mkring("wue", 2, [128, 520], F32)
        wk_uc = mkring("wuc", 3, [128, 512], F32)
        wk_sg = mkring("wsg", 3, [128, 512], F32)
        wk = mkring("wk", 8, [128, 512], F32)
        wb_ucb = mkring("wucb", 3, [128, 512], BF16)
        wb = mkring("wb", 3, [128, 512], BF16)
        pT_t = [sb("pT%d" % i, [128, 2, 256], BF16) for i in range(3)]
        pTr = Ring([(pT_t[i], Res("pT%d" % i)) for i in range(3)])
        dsm = sb("dsm", [16, 2, 128], F32); r_dsm = [Res("dsm0"), Res("dsm1")]

        psr = Ring([(i, r_ps[i]) for i in range(7)])

        def pbank(i):
            return ps[:, i, :]

        def cast_load(dst_ap, src_ap, res, reads=(), skip_own=False):
            S.dma(lambda e, d=dst_ap, s=src_ap: e.dma_start(out=d, in_=s), res, reads=reads, writes=[res], qeng="pool",
                  skip_own=skip_own)

        def load(dst_ap, src_ap, res, slow=False):
            S.dma(lambda e, d=dst_ap, s=src_ap, sl=slow: e.dma_start(out=d, in_=s, allow_slow_non_contiguous=sl),
                  res, writes=[res])

        cast_load(ident[:], c_ident, r_ident)
        cast_load(permb[:], c_perm, r_perm)
        cast_load(maskb[:].rearrange("p a m -> p (a m)"), c_mask, r_mask)
        cast_load(ehb[:], c_eh, r_eh)
        cast_load(lruA[:], lru_w_a.rearrange("o n c d -> c (o n) d"), r_lruA)
        cast_load(lruX[:], lru_w_x.rearrange("o n c d -> c (o n) d"), r_lruX)
        load(bcs[:], c_bc, r_bc)
        load(grep[:], norm_g.partition_broadcast(128), r_grep)
        load(fgrep[:], final_norm_g.partition_broadcast(128), r_fgrep)
        vsrc = [conv_w[0, 0:1, :], conv_w[0, 1:2, :], conv_w[0, 2:3, :], conv_w[0, 3:4, :], conv_b, lru_b_a, lru_b_x, lru_lambda]
        for i, v in enumerate(vsrc):
            load(vrow[8 * i:8 * i + 8, :], v.rearrange("o (n p) -> (o n) p", p=128), r_vrow)
        load(identf[:], c_ident, r_identf)
        load(sinkt[:, 0:1], attn_sinks.rearrange("o h -> h o"), r_sink, slow=True)
        S.op("pe", lambda e: e.transpose(ps[:, 0, 0:64], vrow[:, :], identf[0:64, 0:64]), reads=[r_vrow, r_identf], writes=[r_ps[0]])
        S.op("act", lambda e: e.activation(out=vT[:].rearrange("p a b -> p (a b)"), in_=ps[:, 0, 0:64], func=AF.Copy),
             reads=[r_ps[0]], writes=[r_vT])

        S.op("pool", lambda e: e.memset(mhalf[:], -0.5), writes=[r_mhalf])
        S.op("dve", lambda e: e.tensor_scalar(out=vd[:, 0:2, :], in0=vT[:, 5:7, :], scalar1=0.5, scalar2=None, op0=ALU.mult),
             reads=[r_vT], writes=[r_vd])
        S.op("act", lambda e: e.activation(out=vd[:, 2, :], in_=vT[:, 7, :], func=AF.Exp, scale=-1.0), reads=[r_vT], writes=[r_vd])
        S.op("act", lambda e: e.activation(out=vd[:, 3, :], in_=vd[:, 2, :], func=AF.Ln, bias=1.0), reads=[r_vd], writes=[r_vd])
        S.op("dve", lambda e: e.tensor_scalar(out=vd[:, 2, :], in0=vd[:, 3, :], scalar1=-4.0, scalar2=None, op0=ALU.mult),
             reads=[r_vd], writes=[r_vd])
        S.op("dve", lambda e: e.tensor_scalar(out=vd[:, 3, :], in0=vd[:, 3, :], scalar1=-8.0, scalar2=None, op0=ALU.mult),
             reads=[r_vd], writes=[r_vd])
        S.op("act", lambda e: e.activation(out=sinkt[:, 1:2], in_=sinkt[:, 0:1], func=AF.Exp), reads=[r_sink], writes=[r_sink])

        def slot_units(s):
            res = []
            if s < 4:
                res.append((0, 256, win_v[:, :, OFF_U + 2 * s * 128: OFF_U + (2 * s + 2) * 128]))
                res.append((256, 512, win_v[:, :, OFF_G + 2 * s * 128: OFF_G + (2 * s + 2) * 128]))
            elif s == 4:
                for g in range(4):
                    for hf in range(2):
                        res.append((g * 128 + hf * 64, g * 128 + hf * 64 + 64, win_v[:, :, OFF_K + g * 64: OFF_K + (g + 1) * 64]))
            elif s == 5:
                res.append((0, 256, win_v[:, :, OFF_V:OFF_V + 256]))
                res.append((256, 512, win_v[:, :, OFF_Q:OFF_Q + 256]))
            elif s == 6:
                res.append((0, 512, win_v[:, :, OFF_Q + 256:OFF_Q + 768]))
            elif s == 7:
                res.append((0, 256, win_v[:, :, OFF_Q + 768:OFF_Q + 1024]))
                res.append((256, 512, win_v[:, :, OFF_GA:OFF_GA + 256]))
            elif s == 8:
                res.append((0, 512, win_v[:, :, OFF_GA + 256:OFF_GA + 768]))
            elif s == 9:
                res.append((0, 256, win_v[:, :, OFF_GA + 768:OFF_GA + 1024]))
            elif s < 18:
                i, y = (s - 10) // 2, (s - 10) % 2
                if y == 0:
                    res.append((0, 256, wro_v[:, :, 2 * i * 128:(2 * i + 2) * 128]))
                    res.append((256, 512, wao_v[:, :, 2 * i * 128:(2 * i + 2) * 128]))
                else:
                    res.append((0, 256, win_v[:, :, OFF_MR + 2 * i * 128: OFF_MR + (2 * i + 2) * 128]))
                    res.append((256, 512, win_v[:, :, OFF_MA + 2 * i * 128: OFF_MA + (2 * i + 2) * 128]))
            else:
                half = s - 18
                res.append((0, 512, wo_v[:, :, half * 512:(half + 1) * 512]))
            return res

        def kview(buf):
            return buf[:].rearrange("p (k c) -> p k c", k=8)

        wslot_i = [0]
        stg_i = [0]
        cast_i = [0]
        r_wstream = [Res("wstream%d" % i) for i in range(NSLOT)]

        def fetch_slot(t, s):
            i = wslot_i[0] % 4
            wslot_i[0] += 1
            buf, res = wring[i], r_wring[i]
            if t == 0:
                pieces = []
                for (c0, c1, src) in slot_units(s):
                    ww = c1 - c0
                    nk = min(8, 1024 // ww)
                    for kq in range(0, 8, nk):
                        pieces.append((c0, ww, kq, nk, src[:, kq:kq + nk, :]))
                last_src = [None, None]
                for (c0, ww, kq, nk, src) in pieces:
                    key = str(src)
                    if last_src[0] == key:
                        stg, rstg = last_src[1]
                    else:
                        i_st = stg_i[0] % 4
                        stg_i[0] += 1
                        stg, rstg = xset[1][i_st], r_xset[1][i_st]
                        sv = stg[:, 0:nk * ww].rearrange("p (k c) -> p k c", k=nk)
                        S.dma(lambda e, sv=sv, src=src: e.dma_start(out=sv, in_=src), rstg, writes=[rstg])
                        last_src[0], last_src[1] = key, (stg, rstg)
                    sv = stg[:, 0:nk * ww].rearrange("p (k c) -> p k c", k=nk)
                    dv = kview(buf)[:, kq:kq + nk, c0:c0 + ww]
                    if cast_i[0] % 2 == 0:
                        S.op("act", lambda e, dv=dv, sv=sv: e.activation(out=dv, in_=sv, func=AF.Copy), reads=[rstg], writes=[res])
                    else:
                        S.op("dve", lambda e, dv=dv, sv=sv: e.tensor_copy(out=dv, in_=sv), reads=[rstg], writes=[res])
                    cast_i[0] += 1
                S.dma(lambda e, b=buf, s=s: e.dma_start(out=wstream[s], in_=b[:]), res, reads=[res], writes=[r_wstream[s]])
            else:
                S.dma(lambda e, b=buf, s=s: e.dma_start(out=b[:], in_=wstream[s]), res, reads=[r_wstream[s]], writes=[res])
            return buf, res

        def unit(buf, j):
            return kview(buf)[:, :, j * 128:(j + 1) * 128]

        def mm_group(out_ap, lhs_fn, rhs_fn, nk, reads, writes):
            def fn(e, out_ap=out_ap, lhs_fn=lhs_fn, rhs_fn=rhs_fn, nk=nk):
                ins = None
                for k in range(nk):
                    ins = e.matmul(out_ap, lhsT=lhs_fn(k), rhs=rhs_fn(k), start=(k == 0), stop=(k == nk - 1))
                return ins
            S.op("pe", fn, reads=reads, writes=writes)

        def load_x(t):
            xs, rx = xset[t % 2], r_xset[t % 2]
            for b in range(4):
                r0 = t * T + b * 128
                S.dma(lambda e, d=xs[b], r0=r0: e.dma_start(out=d[:], in_=x[r0:r0 + 128, :]), rx[b], writes=[rx[b]])
            pos0 = (t % TPS) * T
            i = t % 2
            S.dma(lambda e, i=i, p=pos0: e.dma_start(out=ropeC[i][:], in_=c_ropeC[:, p:p + T]), r_rope[i], writes=[r_rope[i]])
            S.dma(lambda e, i=i, p=pos0: e.dma_start(out=ropeS[i][:], in_=c_ropeS[:, p:p + T]), r_rope[i], writes=[r_rope[i]])

        def rstd_ops(sidx, rs):
            c0 = 4 * sidx
            S.op("pool", lambda e, c0=c0: e.tensor_scalar(out=stat[:, c0 + 1:c0 + 2], in0=stat[:, c0:c0 + 1], scalar1=1.0 / D,
                                                          scalar2=EPS, op0=ALU.mult, op1=ALU.add), reads=[rs], writes=[rs])
            S.op("pool", lambda e, c0=c0: e.tensor_tensor(out=stat[:, c0 + 1:c0 + 2], in0=stat[:, c0 + 1:c0 + 2], in1=mhalf[:],
                                                          op=ALU.pow), reads=[rs, r_mhalf], writes=[rs])

        stat_i = [0]

        def stage_A(t, blocks=(0, 1, 2, 3), part="both"):
            hTt, rh = hT[t % 2], r_hT[t % 2]
            xs, rx = xset[t % 2], r_xset[t % 2]
            for b in blocks:
                xb_, rxb = xnb[b % 2], r_xnb[b % 2]
                if part in ("both", "tr"):
                    def tr(e, xb_=xb_):
                        ins = None
                        for c in range(8):
                            ins = e.transpose(psT[:, c * 128:(c + 1) * 128], xb_[:, c * 128:(c + 1) * 128], ident[:])
                        return ins
                if part == "tr":
                    S.op("pe", tr, reads=[rxb, r_ident], writes=[r_psT])
                    S.op("act", lambda e, hTt=hTt, b=b: e.activation(
                        out=hTt[:, :, b * 128:(b + 1) * 128], in_=psT[:].rearrange("p (c m) -> p c m", c=8), func=AF.Copy),
                        reads=[r_psT], writes=[rh[b]])
                    continue
                si = stat_i[0] % 4
                stat_i[0] += 1
                rs = r_stat[si]
                bg, rbg = big[b % 2], r_big[b % 2]
                S.op("act", lambda e, bg=bg, xb=xs[b]: e.activation(out=bg[:], in_=xb[:], func=AF.Square), reads=[rx[b]], writes=[rbg])
                S.op("dve", lambda e, bg=bg, si=si: e.tensor_reduce(out=stat[:, 4 * si:4 * si + 1], in_=bg[:], axis=AX.X, op=ALU.add),
                     reads=[rbg], writes=[rs])
                rstd_ops(si, rs)
                S.op("dve", lambda e, xb_=xb_, xb=xs[b], si=si: e.scalar_tensor_tensor(
                    out=xb_[:], in0=xb[:], scalar=stat[:, 4 * si + 1:4 * si + 2], in1=grep[:], op0=ALU.mult, op1=ALU.mult),
                    reads=[rx[b], rs, r_grep], writes=[rxb])
                if part == "elem":
                    continue
                S.op("pe", tr, reads=[rxb, r_ident], writes=[r_psT])
                S.op("act", lambda e, hTt=hTt, b=b: e.activation(
                    out=hTt[:, :, b * 128:(b + 1) * 128], in_=psT[:].rearrange("p (c m) -> p c m", c=8), func=AF.Copy),
                    reads=[r_psT], writes=[rh[b]])

        def make_B(t):
            hTt, rh = hT[t % 2], r_hT[t % 2]
            first = (t % TPS == 0)
            bstate = {}

            def front(n, buf, rbuf, j0):
                wu, wg = unit(buf, j0), unit(buf, 2 + j0)
                bu, rbu = psr.get()
                mm_group(pbank(bu), lambda k, wu=wu: wu[:, k, :], lambda k: hTt[:, k, :], 8, reads=[rbuf] + rh, writes=[rbu])
                bgp, rbg_ = psr.get()
                mm_group(pbank(bgp), lambda k, wg=wg: wg[:, k, :], lambda k: hTt[:, k, :], 8, reads=[rbuf] + rh, writes=[rbg_])
                ue, rue = wk_ue.get()
                if first:
                    S.op("dve", lambda e, ue=ue: e.memset(ue[:, 0:3], 0.0), writes=[rue])
                else:
                    S.op("pool", lambda e, ue=ue, n=n: e.tensor_copy(out=ue[:, 0:3], in_=halo[:, n, 0:3]),
                         reads=[r_halo[n]], writes=[rue])
                S.op("act", lambda e, ue=ue, bu=bu: e.activation(out=ue[:, 3:3 + T], in_=pbank(bu), func=AF.Copy),
                     reads=[rbu, rue], writes=[rue])
                S.op("pool", lambda e, ue=ue, n=n: e.tensor_copy(out=halo[:, n, 0:3], in_=ue[:, T:T + 3]),
                     reads=[rue], writes=[r_halo[n]])
                sg, rsg = wk_sg.get()
                S.op("act", lambda e, sg=sg, bgp=bgp: e.activation(out=sg[:, 0:T], in_=pbank(bgp), func=AF.Tanh, scale=0.5),
                     reads=[rbg_], writes=[rsg])
                S.op("dve", lambda e, sg=sg, bgp=bgp: e.scalar_tensor_tensor(
                    out=sg[:, 0:T], in0=sg[:, 0:T], scalar=1.0, in1=pbank(bgp), op0=ALU.add, op1=ALU.mult),
                    reads=[rsg, rbg_], writes=[rsg])
                uc, ruc = wk_uc.get()
                S.op("pool", lambda e, ue=ue, uc=uc, n=n: e.tensor_scalar(
                    out=uc[:, 0:T], in0=ue[:, 3:3 + T], scalar1=vT[:, 3, n:n + 1], scalar2=vT[:, 4, n:n + 1], op0=ALU.mult, op1=ALU.add),
                    reads=[rue, r_vT], writes=[ruc])
                for j in (2,):
                    cq, rcq = wk.get()
                    S.op("pool", lambda e, ue=ue, cq=cq, n=n, j=j: e.tensor_scalar(
                        out=cq[:, 0:T], in0=ue[:, j:j + T], scalar1=vT[:, j, n:n + 1], scalar2=0.0, op0=ALU.mult, op1=ALU.add),
                        reads=[rue, r_vT], writes=[rcq])
                    S.op("pool", lambda e, cq=cq, uc=uc: e.tensor_tensor(out=uc[:, 0:T], in0=uc[:, 0:T], in1=cq[:, 0:T], op=ALU.add),
                         reads=[rcq, ruc], writes=[ruc])
                for j in (1, 0):
                    S.op("dve", lambda e, ue=ue, uc=uc, n=n, j=j: e.scalar_tensor_tensor(
                        out=uc[:, 0:T], in0=ue[:, j:j + T], scalar=vT[:, j, n:n + 1], in1=uc[:, 0:T], op0=ALU.mult, op1=ALU.add),
                        reads=[rue, r_vT, ruc], writes=[ruc])
                ucb, rucb = wb_ucb.get()
                return (n, uc, ruc, ucb, rucb, sg, rsg)

            def back(n, uc, ruc, ucb, rucb, sg, rsg):
                br, rbr = psr.get()
                mm_group(pbank(br), lambda k, n=n: lruA[:, n, :], lambda k, ucb=ucb: ucb[:], 1, reads=[r_lruA, rucb], writes=[rbr])
                bi, rbi = psr.get()
                mm_group(pbank(bi), lambda k, n=n: lruX[:, n, :], lambda k, ucb=ucb: ucb[:], 1, reads=[r_lruX, rucb], writes=[rbi])
                tr_, rtr = wk.get()
                S.op("act", lambda e, tr_=tr_, br=br, n=n: e.activation(out=tr_[:, 0:T], in_=pbank(br), func=AF.Tanh, scale=0.5,
                                                                       bias=vd[:, 0, n:n + 1]), reads=[rbr, r_vd], writes=[rtr])
                iu, riu = wk.get()
                S.op("act", lambda e, iu=iu, bi=bi, n=n: e.activation(out=iu[:, 0:T], in_=pbank(bi), func=AF.Tanh, scale=0.5,
                                                                     bias=vd[:, 1, n:n + 1]), reads=[rbi, r_vd], writes=[riu])
                a_, ra = wk.get()
                S.op("act", lambda e, a_=a_, tr_=tr_, n=n: e.activation(out=a_[:, 0:T], in_=tr_[:, 0:T], func=AF.Exp,
                                                                       scale=vd[:, 2, n:n + 1], bias=vd[:, 2, n:n + 1]),
                     reads=[rtr, r_vd], writes=[ra])
                s_, rs_ = wk.get()
                S.op("act", lambda e, s_=s_, tr_=tr_, n=n: e.activation(out=s_[:, 0:T], in_=tr_[:, 0:T], func=AF.Exp,
                                                                       scale=vd[:, 3, n:n + 1], bias=vd[:, 3, n:n + 1]),
                     reads=[rtr, r_vd], writes=[rs_])
                S.op("act", lambda e, s_=s_: e.activation(out=s_[:, 0:T], in_=s_[:, 0:T], func=AF.Sqrt, scale=-1.0, bias=1.0),
                     reads=[rs_], writes=[rs_])
                S.op("dve", lambda e, iu=iu, uc=uc: e.scalar_tensor_tensor(out=iu[:, 0:T], in0=iu[:, 0:T], scalar=1.0, in1=uc[:, 0:T],
                                                                          op0=ALU.add, op1=ALU.mult), reads=[riu, ruc], writes=[riu])
                S.op("dve", lambda e, iu=iu, s_=s_: e.scalar_tensor_tensor(out=iu[:, 0:T], in0=s_[:, 0:T], scalar=0.5, in1=iu[:, 0:T],
                                                                          op0=ALU.mult, op1=ALU.mult), reads=[riu, rs_], writes=[riu])
                h_, rh_ = wk.get()
                if first:
                    S.op("dve", lambda e, h_=h_, a_=a_, iu=iu: e.tensor_tensor_scan(
                        out=h_[:, 0:T], data0=a_[:, 0:T], data1=iu[:, 0:T], initial=0.0, op0=ALU.mult, op1=ALU.add),
                        reads=[ra, riu], writes=[rh_])
                else:
                    S.op("dve", lambda e, h_=h_, a_=a_, iu=iu, n=n: e.tensor_tensor_scan(
                        out=h_[:, 0:T], data0=a_[:, 0:T], data1=iu[:, 0:T], initial=hst[:, n:n + 1], op0=ALU.mult, op1=ALU.add),
                        reads=[ra, riu, r_hst[n]], writes=[rh_])
                S.op("pool", lambda e, h_=h_, n=n: e.tensor_copy(out=hst[:, n:n + 1], in_=h_[:, T - 1:T]),
                     reads=[rh_], writes=[r_hst[n]])
                S.op("dve", lambda e, h_=h_, sg=sg, n=n: e.scalar_tensor_tensor(
                    out=yr[:, n, :], in0=h_[:, 0:T], scalar=0.5, in1=sg[:, 0:T], op0=ALU.mult, op1=ALU.mult),
                    reads=[rh_, rsg], writes=[r_yr[n]])

            def front_n(n):
                if n % 2 == 0:
                    bstate["buf"] = fetch_slot(t, n // 2)
                buf, rbuf = bstate["buf"]
                return front(n, buf, rbuf, n % 2)

            def front_b(n, uc, ruc, ucb, rucb, sg, rsg):
                S.op("act", lambda e, uc=uc, ucb=ucb: e.activation(out=ucb[:], in_=uc[:, 0:T], func=AF.Copy), reads=[ruc], writes=[rucb])
            return front_n, front_b, back

        def rope_chain(pb, rpb, dst_ap, rdst, t):
            i = t % 2
            raw, rraw = wb.get()
            S.op("act", lambda e, raw=raw, pb=pb: e.activation(out=raw[:], in_=pbank(pb), func=AF.Copy), reads=[rpb], writes=[rraw])
            return (raw, rraw, dst_ap, rdst, i)

        def rope_finish(raw, rraw, dst_ap, rdst, i):
            p2, rp2 = psr.get()
            mm_group(pbank(p2), lambda k: permb[:], lambda k, raw=raw: raw[:], 1, reads=[r_perm, rraw], writes=[rp2])
            t2, rt2 = wk.get()
            S.op("pool", lambda e, t2=t2, raw=raw, i=i: e.tensor_tensor(out=t2[:, 0:T], in0=raw[:], in1=ropeC[i][:], op=ALU.mult),
                 reads=[rraw, r_rope[i]], writes=[rt2])
            t1, rt1 = wk.get()
            S.op("dve", lambda e, t1=t1, p2=p2, i=i: e.tensor_tensor(out=t1[:, 0:T], in0=pbank(p2), in1=ropeS[i][:], op=ALU.mult),
                 reads=[rp2, r_rope[i]], writes=[rt1])
            S.op("dve", lambda e, t1=t1, t2=t2, dst_ap=dst_ap: e.tensor_tensor(out=dst_ap, in0=t1[:, 0:T], in1=t2[:, 0:T], op=ALU.add),
                 reads=[rt1, rt2], writes=[rdst])

        def make_C(t):
            hTt, rh = hT[t % 2], r_hT[t % 2]
            units = [("kd", g) for g in range(4)] + [("v", 0), ("v", 1)] + [("q", c) for c in range(8)] + \
                    [("ga", c) for c in range(8)] + [("pad", 0), ("pad", 1)]
            pend = []
            cstate = {}

            def unit_fn(ui):
                kind, i = units[ui]
                j = ui % 4
                if j == 0:
                    cstate["buf"] = fetch_slot(t, 4 + ui // 4)
                buf, rbuf = cstate["buf"]
                if kind in ("kd", "q"):
                    w = unit(buf, j)
                    pb, rpb = psr.get()
                    mm_group(pbank(pb), lambda k, w=w: w[:, k, :], lambda k: hTt[:, k, :], 8, reads=[rbuf] + rh, writes=[rpb])
                    if kind == "kd":
                        dst, rd = kT[:, i, 128:128 + T], r_kT[i]
                    else:
                        dst, rd = qmg[:, i, :], r_qmg[i]
                    pend.append(rope_chain(pb, rpb, dst, rd, t))
                    if len(pend) > 2:
                        rope_finish(*pend.pop(0))
                elif kind == "v" and i == 0:
                    wv = kview(buf)[:, :, j * 128:(j + 2) * 128]
                    for pr in range(2):
                        pb, rpb = psr.get()

                        def fn(e, pb=pb, pr=pr, wv=wv):
                            ins = None
                            for bb in range(2):
                                b = pr * 2 + bb
                                for k in range(8):
                                    ins = e.matmul(ps[:, pb, bb * 256:(bb + 1) * 256], lhsT=hTt[:, k, b * 128:(b + 1) * 128],
                                                   rhs=wv[:, k, :], start=(k == 0), stop=(k == 7))
                            return ins
                        S.op("pe", fn, reads=[rbuf] + rh, writes=[rpb])
                        S.op("act", lambda e, pb=pb, pr=pr: e.activation(
                            out=vtok[:, 1 + 2 * pr:3 + 2 * pr, :], in_=pbank(pb).rearrange("p (b m) -> p b m", b=2), func=AF.Copy),
                            reads=[rpb], writes=[r_vtok])
                elif kind == "ga":
                    w = unit(buf, j)
                    pb, rpb = psr.get()
                    mm_group(pbank(pb), lambda k, w=w: w[:, k, :], lambda k: hTt[:, k, :], 8, reads=[rbuf] + rh, writes=[rpb])
                    tg, rtg = wk.get()
                    S.op("act", lambda e, tg=tg, pb=pb: e.activation(out=tg[:, 0:T], in_=pbank(pb), func=AF.Tanh, scale=0.5),
                         reads=[rpb], writes=[rtg])
                    S.op("dve", lambda e, tg=tg, pb=pb, i=i: e.scalar_tensor_tensor(
                        out=sga[:, i, :], in0=tg[:, 0:T], scalar=1.0, in1=pbank(pb), op0=ALU.add, op1=ALU.mult),
                        reads=[rtg, rpb], writes=[r_sga[i]])

            def flush():
                while pend:
                    rope_finish(*pend.pop(0))
            return unit_fn, flush, len(units)

        def stage_BC(t, dsteps=None):
            frontB, frontB2, backB = make_B(t)
            unitC, flushC, NU = make_C(t)
            pendB = []
            ui = 0
            NP = 10
            for p_ in range(NP):
                if p_ < 8:
                    pendB.append(frontB(p_))
                if p_ >= 2:
                    backB(*pendB.pop(0))
                if p_ < 8:
                    frontB2(*pendB[-1])
                for _ in range(3):
                    if ui < NU:
                        unitC(ui)
                        ui += 1
                        if ui == NU:
                            flushC()
                if ui == NU and dsteps is not None and p_ >= 8:
                    psr.items = [(i, r_ps[i]) for i in (0, 1, 5, 6)]
                    for _ in range(5):
                        if dsteps:
                            dsteps.pop(0)()
            psr.items = [(i, r_ps[i]) for i in range(7)]

        def stage_D(t):
            ts = t % TPS
            S_BANKS = [(0, 0), (5, 0)]
            sring = Ring(S_BANKS)
            DEN, VALS = 2, 3
            rbv = psT[:].bitcast(F32)
            post2_pending = [None]
            steps = []
            for jq in range(4):
                first = (ts == 0 and jq == 0)
                lo = 128 if first else 0
                pend = []

                def s1(i, jq=jq, first=first, lo=lo):
                    c, g = i, i // 2
                    bk, _hf = sring.get()
                    rbk = [r_ps[bk], r_ps[bk + 1]]

                    def fn(e, bk=bk, g=g, c=c):
                        ins = None
                        for hh in range(2):
                            hb = hh * 64
                            if not first:
                                ins = e.matmul(ps[:, bk + hh, 0:128], lhsT=kT[hb:hb + 64, g, jq * 128:(jq + 1) * 128],
                                               rhs=qmg[hb:hb + 64, c, jq * 128:(jq + 1) * 128], start=True, stop=True)
                            ins = e.matmul(ps[:, bk + hh, 128:256], lhsT=kT[hb:hb + 64, g, 128 + jq * 128:128 + (jq + 1) * 128],
                                           rhs=qmg[hb:hb + 64, c, jq * 128:(jq + 1) * 128], start=True, stop=True)
                        return ins
                    S.op("pe", fn, reads=[r_kT[g], r_qmg[c]], writes=[rbk])
                    pt, rpt = pTr.get()
                    S.op("act", lambda e, pt=pt, bk=bk: e.activation(
                        out=pt[:, :, lo:256], in_=ps[:, bk:bk + 2, lo:256], func=AF.Exp, scale=0.125),
                        reads=[rbk], writes=[rpt])
                    S.op("dve", lambda e, pt=pt: e.tensor_tensor(out=pt[:, :, lo:256], in0=pt[:, :, lo:256], in1=maskb[:, :, lo:256], op=ALU.mult),
                         reads=[rpt, r_mask], writes=[rpt])
                    return (i, pt, rpt)

                def s4(i, pt, rpt, jq=jq, first=first):
                    c, g = i, i // 2

                    def fn(e, pt=pt, c=c, g=g):
                        ins = None
                        for hh in range(2):
                            h = 2 * c + hh
                            hb = hh * 64
                            vo = ps[hb:hb + 64, VALS + c // 4, (c % 4) * 128:(c % 4 + 1) * 128]
                            if not first:
                                e.matmul(vo, lhsT=vtok[:, jq, g * 64:(g + 1) * 64], rhs=pt[:, hh, 0:128], start=True, stop=False)
                            e.matmul(vo, lhsT=vtok[:, jq + 1, g * 64:(g + 1) * 64], rhs=pt[:, hh, 128:256], start=first, stop=True)
                            if not first:
                                ins = e.matmul(ps[0:16, DEN, 0:256], lhsT=ehb[:, h * 16:(h + 1) * 16], rhs=pt[:, hh, 0:256], start=(h == 0),
                                               stop=(h == 15))
                            else:
                                ins = e.matmul(ps[0:16, DEN, 128:256], lhsT=ehb[:, h * 16:(h + 1) * 16], rhs=pt[:, hh, 128:256],
                                               start=(h == 0), stop=(h == 15))
                        return ins
                    S.op("pe", fn, reads=[rpt, r_vtok, r_eh], writes=[r_ps[VALS], r_ps[VALS + 1], r_ps[DEN]])

                def post1(jq=jq, first=first):
                    di = jq % 2
                    bg, rbg = big[0], r_big[0]
                    S.op("act", lambda e, bg=bg: e.activation(out=bg[:].rearrange("p (a m) -> p a m", a=2), in_=ps[:, VALS:VALS + 2, :],
                                                              func=AF.Copy), reads=[r_ps[VALS], r_ps[VALS + 1]], writes=[rbg])
                    S.op("dve", lambda e, di=di: e.tensor_scalar(out=dsm[:, di, :], in0=ps[0:16, DEN, 128:256], scalar1=sinkt[:, 1:2],
                                                                  scalar2=None, op0=ALU.add), reads=[r_ps[DEN], r_sink], writes=[r_dsm[di]])
                    if not first:
                        S.op("dve", lambda e, di=di: e.tensor_tensor(out=dsm[:, di, :], in0=dsm[:, di, :], in1=ps[0:16, DEN, 0:128],
                                                                      op=ALU.add), reads=[r_ps[DEN], r_dsm[di]], writes=[r_dsm[di]])
                    S.op("dve", lambda e, di=di: e.reciprocal(out=dsm[:, di, :], in_=dsm[:, di, :]), reads=[r_dsm[di]], writes=[r_dsm[di]])

                def post2(jq=jq):
                    di = jq % 2
                    bg, rbg = big[0], r_big[0]
                    b1, rb1 = big[1], r_big[1]
                    for hf in range(2):
                        def fnb(e, di=di, hf=hf):
                            ins = None
                            for cc in range(4):
                                c = hf * 4 + cc
                                ins = e.matmul(rbv[:, cc * 128:(cc + 1) * 128], lhsT=bcs[:, c * 128:(c + 1) * 128],
                                               rhs=dsm[:, di, :], start=True, stop=True)
                            return ins
                        S.op("pe", fnb, reads=[r_bc, r_dsm[di]], writes=[r_psT])
                        S.op("dve", lambda e, bg=bg, b1=b1, hf=hf: e.tensor_tensor(
                            out=b1[:, hf * 512:(hf + 1) * 512], in0=rbv, in1=bg[:, hf * 512:(hf + 1) * 512], op=ALU.mult),
                            reads=[r_psT, rbg], writes=[rb1])
                    S.op("dve", lambda e, b1=b1, jq=jq: e.scalar_tensor_tensor(
                        out=ya[:, :, jq * 128:(jq + 1) * 128], in0=b1[:].rearrange("p (c m) -> p c m", c=8), scalar=0.5,
                        in1=sga[:, :, jq * 128:(jq + 1) * 128], op0=ALU.mult, op1=ALU.mult),
                        reads=[rb1] + r_sga, writes=[r_ya[jq]])

                SK = 1

                def hstep(i, s1=s1, s4=s4, pend=pend):
                    if i < 8:
                        pend.append(s1(i))
                    if i >= SK:
                        s4(*pend.pop(0))
                    if i == 3 and post2_pending[0] is not None:
                        post2_pending[0]()
                        post2_pending[0] = None
                for i in range(8 + SK):
                    steps.append(lambda i=i, hstep=hstep: hstep(i))

                def pstep(post1=post1, post2=post2):
                    post1()
                    post2_pending[0] = post2
                steps.append(pstep)

            def fin():
                post2_pending[0]()
                S.op("pool", lambda e: e.tensor_copy(out=kT[:, :, 0:128], in_=kT[:, :, T:T + 128]), reads=r_kT, writes=r_kT)
                S.op("pool", lambda e: e.tensor_copy(out=vtok[:, 0, :], in_=vtok[:, 4, :]), reads=[r_vtok], writes=[r_vtok])
            steps.append(fin)
            return steps

        def stage_E(t, a_next=None):
            hTt, rh = hT[t % 2], r_hT[t % 2]
            ebuf = {}
            for f in range(8):
                if a_next is not None:
                    stage_A(a_next, blocks=(f // 2,), part=("elem" if f % 2 == 0 else "tr"))
                if f % 2 == 0:
                    ebuf["x"] = fetch_slot(t, 10 + f)
                    ebuf["y"] = fetch_slot(t, 11 + f)
                (bx, rbx), (by, rby) = ebuf["x"], ebuf["y"]
                rbuf = [rbx, rby]
                wro, wao, wmr, wma = unit(bx, f % 2), unit(bx, 2 + f % 2), unit(by, f % 2), unit(by, 2 + f % 2)
                pc, rpc = psr.get()
                mm_group(pbank(pc), lambda k, w=wmr: w[:, k, :], lambda k: hTt[:, k, :], 8, reads=[rbuf] + rh, writes=[rpc])
                pd, rpd = psr.get()
                mm_group(pbank(pd), lambda k, w=wma: w[:, k, :], lambda k: hTt[:, k, :], 8, reads=[rbuf] + rh, writes=[rpd])
                pa, rpa = psr.get()
                mm_group(pbank(pa), lambda k, w=wro: w[:, k, :], lambda k: yr[:, k, :], 8, reads=[rbuf] + r_yr, writes=[rpa])
                pb, rpb = psr.get()
                mm_group(pbank(pb), lambda k, w=wao: w[:, k, :], lambda k: ya[:, k, :], 8, reads=[rbuf] + r_ya, writes=[rpb])
                tc_, rtc = wk.get()
                S.op("act", lambda e, tc_=tc_, pc=pc: e.activation(out=tc_[:, 0:T], in_=pbank(pc), func=AF.Tanh, scale=0.5),
                     reads=[rpc], writes=[rtc])
                td_, rtd = wk.get()
                S.op("act", lambda e, td_=td_, pd=pd: e.activation(out=td_[:, 0:T], in_=pbank(pd), func=AF.Tanh, scale=0.5),
                     reads=[rpd], writes=[rtd])
                S.op("dve", lambda e, tc_=tc_, pa=pa: e.scalar_tensor_tensor(out=tc_[:, 0:T], in0=tc_[:, 0:T], scalar=1.0, in1=pbank(pa),
                                                                            op0=ALU.add, op1=ALU.mult), reads=[rtc, rpa], writes=[rtc])
                S.op("dve", lambda e, td_=td_, pb=pb: e.scalar_tensor_tensor(out=td_[:, 0:T], in0=td_[:, 0:T], scalar=1.0, in1=pbank(pb),
                                                                            op0=ALU.add, op1=ALU.mult), reads=[rtd, rpb], writes=[rtd])
                S.op("pool", lambda e, tc_=tc_, td_=td_, f=f: e.tensor_tensor(out=qmg[:, f, :], in0=tc_[:, 0:T], in1=td_[:, 0:T], op=ALU.add),
                     reads=[rtc, rtd], writes=[r_qmg[f]])

        def stage_F(t):
            xs, rx = xset[t % 2], r_xset[t % 2]
            bufs = [fetch_slot(t, 18), fetch_slot(t, 19)]
            pairs = [(0, 1), (2, 3), (4, 5)]
            for b in range(4):
                p0, p1 = pairs[b % 3]

                def fn(e, b=b, p0=p0):
                    ins = None
                    for half in range(2):
                        w = bufs[half][0][:].rearrange("p (k m) -> p k m", k=8)
                        for k in range(8):
                            ins = e.matmul(ps[:, p0 + half, :], lhsT=qmg[:, k, b * 128:(b + 1) * 128], rhs=w[:, k, :],
                                           start=(k == 0), stop=(k == 7))
                    return ins
                S.op("pe", fn, reads=[bufs[0][1], bufs[1][1]] + r_qmg, writes=[r_ps[p0], r_ps[p1]])
                xb = xs[b]
                xv = xb[:].rearrange("p (a m) -> p a m", a=2)
                S.op("dve", lambda e, xv=xv, p0=p0: e.scalar_tensor_tensor(out=xv, in0=ps[:, p0:p0 + 2, :], scalar=0.5, in1=xv,
                                                                          op0=ALU.mult, op1=ALU.add),
                     reads=[r_ps[p0], r_ps[p1], rx[b]], writes=[rx[b]])
                si = stat_i[0] % 4
                stat_i[0] += 1
                rs = r_stat[si]
                bg, rbg = big[b % 2], r_big[b % 2]
                S.op("act", lambda e, bg=bg, xb=xb: e.activation(out=bg[:], in_=xb[:], func=AF.Square), reads=[rx[b]], writes=[rbg])
                S.op("dve", lambda e, bg=bg, si=si: e.tensor_reduce(out=stat[:, 4 * si:4 * si + 1], in_=bg[:], axis=AX.X, op=ALU.add),
                     reads=[rbg], writes=[rs])
                rstd_ops(si, rs)
                S.op("dve", lambda e, xb=xb, si=si: e.scalar_tensor_tensor(
                    out=xb[:], in0=xb[:], scalar=stat[:, 4 * si + 1:4 * si + 2], in1=fgrep[:], op0=ALU.mult, op1=ALU.mult),
                    reads=[rx[b], rs, r_fgrep], writes=[rx[b]])
                r0 = t * T + b * 128
                S.dma(lambda e, xb=xb, r0=r0: e.dma_start(out=out[r0:r0 + 128, :], in_=xb[:]), rx[b], reads=[rx[b]], qeng="pool", final=True)

        load_x(0)
        stage_A(0)
        stop = tuple(stop_after) if stop_after is not None else None
        for t in range(ntiles):
            more = (t + 1 < ntiles)
            dsteps = stage_D(t)
            if stop == ("B", t) or stop == ("C", t):
                stage_BC(t)
                break
            stage_BC(t, dsteps)
            if more and t > 0:
                load_x(t + 1)
            while dsteps:
                dsteps.pop(0)()
            if stop == ("D", t):
                break
            stage_E(t, a_next=(t + 1 if (more and t > 0) else None))
            if stop == ("E", t):
                break
            stage_F(t)
            if stop == ("F", t):
                break
            if more and t == 0:
                load_x(1)
                stage_A(1)
        dumpable = {
            "hT0": (hT[0], r_hT[0], [128, 8, T], BF16), "yr": (yr, r_yr, [128, 8, T], BF16), "ya": (ya, r_ya, [128, 8, T], BF16),
            "qmg": (qmg, r_qmg, [128, 8, T], BF16), "sga": (sga, r_sga, [128, 8, T], BF16), "kT": (kT, r_kT, [128, 4, 128 + T], BF16),
            "vtok": (vtok, [r_vtok], [128, 5, 256], BF16), "vT": (vT, [r_vT], [128, 8, 8], F32), "vd": (vd, [r_vd], [128, 4, 8], F32),
            "x00": (xset[0][0], [r_xset[0][0]], [128, D], F32),
        }
        for name in dump:
            tens, rl, shape, dt = dumpable[name]
            dd = nc.dram_tensor("dbg_" + name, shape, dt, kind="ExternalOutput").ap()
            S.dma(lambda e, dd=dd, tens=tens: e.dma_start(out=dd, in_=tens[:]), rl[0], reads=rl, qeng="pool", final=True)
        S.finish("pool")
        S.run_block()
    return nc


def make_in_maps(inputs):
    consts = _consts()
    maps = []
    xs = np.ascontiguousarray(inputs["x"], dtype=np.float32)
    for c in range(NCORES):
        m = {"x": xs[2 * c:2 * c + 2].reshape(2 * SEQ, D)}
        for k in ("norm_g", "w_in", "conv_w", "conv_b", "lru_w_a", "lru_b_a", "lru_w_x", "lru_b_x", "lru_lambda",
                  "attn_sinks", "w_rnn_out", "w_attn_out", "w_o"):
            m[k] = np.ascontiguousarray(inputs[k], dtype=np.float32)
        m["final_norm_g"] = np.ascontiguousarray(inputs["final_norm_g"], dtype=np.float32).reshape(1, D)
        m.update(consts)
        maps.append(m)
    return maps


def kernel(**inputs):
    nc = build_program()
    in_maps = make_in_maps(inputs)
    res = run_bass_kernel_spmd(nc, in_maps, core_ids=list(range(NCORES)))
    outs = [np.asarray(r["out"], dtype=np.float32).reshape(2, SEQ, D) for r in res.results]
    return np.concatenate(outs, axis=0)
```

```python
import math
from contextlib import ExitStack

import numpy as np

import concourse.bass as bass
import concourse.mybir as mybir
from concourse.bass_utils import run_bass_kernel_spmd

F32 = mybir.dt.float32
BF16 = mybir.dt.bfloat16
AF = mybir.ActivationFunctionType
ALU = mybir.AluOpType
AX = mybir.AxisListType

NCORES = 8
SEQ = 2048
D = 1024
DIN = 6656
T = 512
TPS = SEQ // T
NT = 2 * TPS
OFF_U, OFF_G, OFF_Q, OFF_K, OFF_V, OFF_GA, OFF_MR, OFF_MA = 0, 1024, 2048, 3072, 3328, 3584, 4608, 5632
NSLOT = 20
EPS = 1e-6


class Res:
    __slots__ = ("name", "w", "rs", "dsem", "dcnt")

    def __init__(self, name):
        self.name = name
        self.w = None
        self.rs = {}
        self.dsem = None
        self.dcnt = 0


class Sched:
    CE = ("pe", "act", "dve", "pool")

    def __init__(self, nc, stack, same_engine_sync=True):
        self.nc = nc
        self.stack = stack
        self.q = {e: [] for e in self.CE + ("sp",)}
        self.sem = {e: stack.enter_context(nc.semaphore("s_" + e)) for e in self.CE}
        self.cnt = {e: 0 for e in self.CE}
        self.waited = {}
        self.same = same_engine_sync
        self.nsem = 0
        self.final = []
        self.pool_dmas = []

    @staticmethod
    def _flat(xs):
        out = []
        for x in xs:
            if isinstance(x, (list, tuple)):
                out.extend(Sched._flat(x))
            else:
                out.append(x)
        return out

    def _deps(self, reads, writes):
        deps = []
        for r in reads:
            if r.w is not None:
                deps.append(r.w)
        for r in writes:
            if r.w is not None:
                deps.append(r.w)
            deps.extend(r.rs.values())
        return deps

    def _need(self, eng, deps):
        best = {}
        for sem, val, src in deps:
            if src == eng and (eng == "pe" or not self.same):
                continue
            k = id(sem)
            if k not in best or val > best[k][1]:
                best[k] = (sem, val)
        out = []
        for k, (sem, val) in best.items():
            if self.waited.get((eng, k), 0) >= val:
                continue
            self.waited[(eng, k)] = val
            out.append((sem, val))
        return out

    @staticmethod
    def _addr(r, tok):
        k = id(tok[0])
        if k not in r.rs or r.rs[k][1] < tok[1]:
            r.rs[k] = tok

    def op(self, eng, fn, reads=(), writes=()):
        reads, writes = self._flat(reads), self._flat(writes)
        waits = self._need(eng, self._deps(reads, writes))
        self.cnt[eng] += 1
        tok = (self.sem[eng], self.cnt[eng], eng)
        self.q[eng].append((waits, fn, (self.sem[eng], 1)))
        for r in reads:
            self._addr(r, tok)
        for r in writes:
            r.w = tok
            r.rs = {}

    def dma(self, fn, owner, reads=(), writes=(), qeng="sp", final=False, skip_own=False):
        reads, writes = self._flat(reads), self._flat(writes)
        kind = 0 if qeng == "pool" else 1
        if owner.dsem is None:
            owner.dsem = [None, None]
            owner.dcnt = [0, 0]
        if owner.dsem[kind] is None:
            owner.dsem[kind] = self.stack.enter_context(self.nc.semaphore("d%d" % self.nsem))
            self.nsem += 1
        deps = self._deps(reads, writes)
        if skip_own:
            deps = [d for d in deps if d[0] is not owner.dsem[kind]]
        if qeng == "pool" and not skip_own and len(self.pool_dmas) >= 3:
            deps.append(self.pool_dmas[-3])
        waits = self._need(qeng, deps)
        owner.dcnt[kind] += 16
        tok = (owner.dsem[kind], owner.dcnt[kind], "dma")
        self.q[qeng].append((waits, fn, (owner.dsem[kind], 16)))
        for r in reads:
            self._addr(r, tok)
        for r in writes:
            r.w = tok
            r.rs = {}
        if final:
            self.final.append(tok)
        if qeng == "pool":
            if skip_own and self.pool_dmas and self.pool_dmas[-1][0] is tok[0]:
                self.pool_dmas[-1] = tok
            else:
                self.pool_dmas.append(tok)

    def finish(self, qeng="sp"):
        waits = self._need(qeng, self.final)
        self.q[qeng].append((waits, None, None))

    def replay(self, name, e):
        for waits, fn, inc in self.q[name]:
            for sem, val in waits:
                e.wait_ge(sem, val)
            if fn is None:
                continue
            ins = fn(e)
            if inc is not None:
                ins.then_inc(inc[0], inc[1])

    def run_block(self):
        with self.nc.Block() as block:
            @block.tensor
            def _(e):
                self.replay("pe", e)

            @block.scalar
            def _(e):
                self.replay("act", e)

            @block.vector
            def _(e):
                self.replay("dve", e)

            @block.gpsimd
            def _(e):
                self.replay("pool", e)

            @block.sync
            def _(e):
                self.replay("sp", e)


class Ring:
    def __init__(self, items):
        self.items = items
        self.i = 0

    def get(self):
        it = self.items[self.i % len(self.items)]
        self.i += 1
        return it


def _consts():
    c = {}
    c["c_ident"] = np.eye(128, dtype=np.float32)
    perm = np.zeros((128, 128), np.float32)
    for m in range(128):
        d = m % 64
        if d < 8:
            perm[m + 8, m] = 1.0
        elif d < 16:
            perm[m - 8, m] = 1.0
    c["c_perm"] = perm
    k = np.arange(128)[:, None]
    q = np.arange(128)[None, :]
    mask = np.concatenate([(q < k), (q >= k)], axis=1).astype(np.float32)
    c["c_mask"] = np.concatenate([mask, mask], axis=1)
    eh = np.zeros((128, 16, 16), np.float32)
    for h in range(16):
        eh[:, h, h] = 1.0
    c["c_eh"] = eh.reshape(128, 256)
    bc = np.zeros((16, 8, 128), np.float32)
    for cc in range(8):
        for p in range(128):
            bc[2 * cc + p // 64, cc, p] = 1.0
    c["c_bc"] = bc.reshape(16, 1024)
    pos = np.arange(SEQ, dtype=np.float32)
    inv_freq = (np.float32(500000.0) ** (-np.arange(0, 16, 2, dtype=np.float32) / np.float32(16))).astype(np.float32)
    ang = (pos[:, None] * inv_freq[None, :]).astype(np.float32)
    cos = np.cos(ang).astype(np.float32)
    sin = np.sin(ang).astype(np.float32)
    C = np.ones((128, SEQ), np.float32)
    Sg = np.zeros((128, SEQ), np.float32)
    for p in range(128):
        d = p % 64
        if d < 8:
            C[p] = cos[:, d]
            Sg[p] = -sin[:, d]
        elif d < 16:
            C[p] = cos[:, d - 8]
            Sg[p] = sin[:, d - 8]
    c["c_ropeC"] = C
    c["c_ropeS"] = Sg
    return c


def build_program(ntiles=NT, same_engine_sync=True, stop_after=None, dump=()):
    nc = bass.Bass("TRN2", target_bir_lowering=False)

    def din(name, shape):
        return nc.dram_tensor(name, shape, F32, kind="ExternalInput").ap()

    x = din("x", [2 * SEQ, D])
    norm_g = din("norm_g", [1, D])
    w_in = din("w_in", [1, D, DIN])
    conv_w = din("conv_w", [1, 4, D])
    conv_b = din("conv_b", [1, D])
    lru_w_a = din("lru_w_a", [1, 8, 128, 128])
    lru_b_a = din("lru_b_a", [1, D])
    lru_w_x = din("lru_w_x", [1, 8, 128, 128])
    lru_b_x = din("lru_b_x", [1, D])
    lru_lambda = din("lru_lambda", [1, D])
    attn_sinks = din("attn_sinks", [1, 16])
    w_rnn_out = din("w_rnn_out", [1, D, D])
    w_attn_out = din("w_attn_out", [1, D, D])
    w_o = din("w_o", [1, D, D])
    final_norm_g = din("final_norm_g", [1, D])
    c_ident = din("c_ident", [128, 128])
    c_perm = din("c_perm", [128, 128])
    c_mask = din("c_mask", [128, 512])
    c_eh = din("c_eh", [128, 256])
    c_bc = din("c_bc", [16, 1024])
    c_ropeC = din("c_ropeC", [128, SEQ])
    c_ropeS = din("c_ropeS", [128, SEQ])
    wstream = nc.dram_tensor("wstream", [NSLOT, 128, 4096], BF16, kind="Internal").ap()
    out = nc.dram_tensor("out", [2 * SEQ, D], F32, kind="ExternalOutput").ap()

    win_v = w_in.rearrange("o (k p) c -> p (o k) c", p=128)
    wro_v = w_rnn_out.rearrange("o (k p) c -> p (o k) c", p=128)
    wao_v = w_attn_out.rearrange("o (k p) c -> p (o k) c", p=128)
    wo_v = w_o.rearrange("o (k p) c -> p (o k) c", p=128)

    with ExitStack() as st:
        S = Sched(nc, st, same_engine_sync=same_engine_sync)

        def sb(name, shape, dt):
            return st.enter_context(nc.sbuf_tensor(name, shape, dt))

        ps = st.enter_context(nc.psum_tensor("ps", [128, 7, 512], F32))
        psT = st.enter_context(nc.psum_tensor("psT", [128, 1024], BF16))
        r_ps = [Res("ps%d" % i) for i in range(7)]
        r_psT = Res("psT")

        ident = sb("ident", [128, 128], BF16); r_ident = Res("ident")
        permb = sb("permb", [128, 128], BF16); r_perm = Res("perm")
        maskb = sb("maskb", [128, 2, 256], BF16); r_mask = Res("mask")
        ehb = sb("ehb", [128, 256], BF16); r_eh = Res("eh")
        bcs = sb("bcs", [16, 1024], F32); r_bc = Res("bc")
        grep = sb("grep", [128, D], F32); r_grep = Res("grep")
        fgrep = sb("fgrep", [128, D], F32); r_fgrep = Res("fgrep")
        mhalf = sb("mhalf", [128, 1], F32); r_mhalf = Res("mhalf")
        vrow = sb("vrow", [64, 128], F32); r_vrow = Res("vrow")
        identf = sb("identf", [128, 128], F32); r_identf = Res("identf")
        vT = sb("vT", [128, 8, 8], F32); r_vT = Res("vT")
        vd = sb("vd", [128, 4, 8], F32); r_vd = Res("vd")
        sinkt = sb("sinkt", [16, 2], F32); r_sink = Res("sink")
        lruA = sb("lruA", [128, 8, 128], BF16); r_lruA = Res("lruA")
        lruX = sb("lruX", [128, 8, 128], BF16); r_lruX = Res("lruX")

        wring = [sb("wring%d" % i, [128, 4096], BF16) for i in range(4)]
        r_wring = [Res("wring%d" % i) for i in range(4)]
        hT = [sb("hT%d" % i, [128, 8, T], BF16) for i in range(2)]
        r_hT = [[Res("hT%d_%d" % (i, b)) for b in range(4)] for i in range(2)]
        yr = sb("yr", [128, 8, T], BF16); r_yr = [Res("yr%d" % i) for i in range(8)]
        ya = sb("ya", [128, 8, T], BF16); r_ya = [Res("ya%d" % i) for i in range(4)]
        qmg = sb("qmg", [128, 8, T], BF16); r_qmg = [Res("qmg%d" % i) for i in range(8)]
        sga = sb("sga", [128, 8, T], BF16); r_sga = [Res("sga%d" % i) for i in range(8)]
        kT = sb("kT", [128, 4, 128 + T], BF16); r_kT = [Res("kT%d" % i) for i in range(4)]
        vtok = sb("vtok", [128, 5, 256], BF16); r_vtok = Res("vtok")
        halo = sb("halo", [128, 8, 4], F32); r_halo = [Res("halo%d" % i) for i in range(8)]
        hst = sb("hst", [128, 8], F32); r_hst = [Res("hst%d" % i) for i in range(8)]
        ropeC = [sb("ropeC%d" % i, [128, T], F32) for i in range(2)]
        ropeS = [sb("ropeS%d" % i, [128, T], F32) for i in range(2)]
        r_rope = [Res("rope%d" % i) for i in range(2)]
        xset = [[sb("x%d_%d" % (i, b), [128, D], F32) for b in range(4)] for i in range(2)]
        r_xset = [[Res("x%d_%d" % (i, b)) for b in range(4)] for i in range(2)]
        xnb = [sb("xnb%d" % i, [128, D], BF16) for i in range(2)]
        r_xnb = [Res("xnb%d" % i) for i in range(2)]
        big = [sb("big%d" % i, [128, D], F32) for i in range(2)]
        r_big = [Res("big%d" % i) for i in range(2)]
        stat = sb("stat", [128, 16], F32)
        r_stat = [Res("stat%d" % i) for i in range(4)]
        def mkring(name, n, shape, dt):
            return Ring([(sb("%s%d" % (name, i), shape, dt), Res("%s%d" % (name, i))) for i in range(n)])
        wk_ue = mkring("wue", 2, [128, 520], F32)
        wk_uc = mkring("wuc", 3, [128, 512], F32)
        wk_sg = mkring("wsg", 3, [128, 512], F32)
        wk = mkring("wk", 8, [128, 512], F32)
        wb_ucb = mkring("wucb", 3, [128, 512], BF16)
        wb = mkring("wb", 3, [128, 512], BF16)
        pT_t = [sb("pT%d" % i, [128, 2, 256], BF16) for i in range(3)]
        pTr = Ring([(pT_t[i], Res("pT%d" % i)) for i in range(3)])
        dsm = sb("dsm", [16, 2, 128], F32); r_dsm = [Res("dsm0"), Res("dsm1")]

        psr = Ring([(i, r_ps[i]) for i in range(7)])

        def pbank(i):
            return ps[:, i, :]

        def cast_load(dst_ap, src_ap, res, reads=(), skip_own=False):
            S.dma(lambda e, d=dst_ap, s=src_ap: e.dma_start(out=d, in_=s), res, reads=reads, writes=[res], qeng="pool",
                  skip_own=skip_own)

        def load(dst_ap, src_ap, res, slow=False):
            S.dma(lambda e, d=dst_ap, s=src_ap, sl=slow: e.dma_start(out=d, in_=s, allow_slow_non_contiguous=sl),
                  res, writes=[res])

        cast_load(ident[:], c_ident, r_ident)
        cast_load(permb[:], c_perm, r_perm)
        cast_load(maskb[:].rearrange("p a m -> p (a m)"), c_mask, r_mask)
        cast_load(ehb[:], c_eh, r_eh)
        cast_load(lruA[:], lru_w_a.rearrange("o n c d -> c (o n) d"), r_lruA)
        cast_load(lruX[:], lru_w_x.rearrange("o n c d -> c (o n) d"), r_lruX)
        load(bcs[:], c_bc, r_bc)
        load(grep[:], norm_g.partition_broadcast(128), r_grep)
        load(fgrep[:], final_norm_g.partition_broadcast(128), r_fgrep)
        vsrc = [conv_w[0, 0:1, :], conv_w[0, 1:2, :], conv_w[0, 2:3, :], conv_w[0, 3:4, :], conv_b, lru_b_a, lru_b_x, lru_lambda]
        for i, v in enumerate(vsrc):
            load(vrow[8 * i:8 * i + 8, :], v.rearrange("o (n p) -> (o n) p", p=128), r_vrow)
        load(identf[:], c_ident, r_identf)
        load(sinkt[:, 0:1], attn_sinks.rearrange("o h -> h o"), r_sink, slow=True)
        S.op("pe", lambda e: e.transpose(ps[:, 0, 0:64], vrow[:, :], identf[0:64, 0:64]), reads=[r_vrow, r_identf], writes=[r_ps[0]])
        S.op("act", lambda e: e.activation(out=vT[:].rearrange("p a b -> p (a b)"), in_=ps[:, 0, 0:64], func=AF.Copy),
             reads=[r_ps[0]], writes=[r_vT])

        S.op("pool", lambda e: e.memset(mhalf[:], -0.5), writes=[r_mhalf])
        S.op("dve", lambda e: e.tensor_scalar(out=vd[:, 0:2, :], in0=vT[:, 5:7, :], scalar1=0.5, scalar2=None, op0=ALU.mult),
             reads=[r_vT], writes=[r_vd])
        S.op("act", lambda e: e.activation(out=vd[:, 2, :], in_=vT[:, 7, :], func=AF.Exp, scale=-1.0), reads=[r_vT], writes=[r_vd])
        S.op("act", lambda e: e.activation(out=vd[:, 3, :], in_=vd[:, 2, :], func=AF.Ln, bias=1.0), reads=[r_vd], writes=[r_vd])
        S.op("dve", lambda e: e.tensor_scalar(out=vd[:, 2, :], in0=vd[:, 3, :], scalar1=-4.0, scalar2=None, op0=ALU.mult),
             reads=[r_vd], writes=[r_vd])
        S.op("dve", lambda e: e.tensor_scalar(out=vd[:, 3, :], in0=vd[:, 3, :], scalar1=-8.0, scalar2=None, op0=ALU.mult),
             reads=[r_vd], writes=[r_vd])
        S.op("act", lambda e: e.activation(out=sinkt[:, 1:2], in_=sinkt[:, 0:1], func=AF.Exp), reads=[r_sink], writes=[r_sink])

        def slot_units(s):
            res = []
            if s < 4:
                res.append((0, 256, win_v[:, :, OFF_U + 2 * s * 128: OFF_U + (2 * s + 2) * 128]))
                res.append((256, 512, win_v[:, :, OFF_G + 2 * s * 128: OFF_G + (2 * s + 2) * 128]))
            elif s == 4:
                for g in range(4):
                    for hf in range(2):
                        res.append((g * 128 + hf * 64, g * 128 + hf * 64 + 64, win_v[:, :, OFF_K + g * 64: OFF_K + (g + 1) * 64]))
            elif s == 5:
                res.append((0, 256, win_v[:, :, OFF_V:OFF_V + 256]))
                res.append((256, 512, win_v[:, :, OFF_Q:OFF_Q + 256]))
            elif s == 6:
                res.append((0, 512, win_v[:, :, OFF_Q + 256:OFF_Q + 768]))
            elif s == 7:
                res.append((0, 256, win_v[:, :, OFF_Q + 768:OFF_Q + 1024]))
                res.append((256, 512, win_v[:, :, OFF_GA:OFF_GA + 256]))
            elif s == 8:
                res.append((0, 512, win_v[:, :, OFF_GA + 256:OFF_GA + 768]))
            elif s == 9:
                res.append((0, 256, win_v[:, :, OFF_GA + 768:OFF_GA + 1024]))
            elif s < 18:
                i, y = (s - 10) // 2, (s - 10) % 2
                if y == 0:
                    res.append((0, 256, wro_v[:, :, 2 * i * 128:(2 * i + 2) * 128]))
                    res.append((256, 512, wao_v[:, :, 2 * i * 128:(2 * i + 2) * 128]))
                else:
                    res.append((0, 256, win_v[:, :, OFF_MR + 2 * i * 128: OFF_MR + (2 * i + 2) * 128]))
                    res.append((256, 512, win_v[:, :, OFF_MA + 2 * i * 128: OFF_MA + (2 * i + 2) * 128]))
            else:
                half = s - 18
                res.append((0, 512, wo_v[:, :, half * 512:(half + 1) * 512]))
            return res

        def kview(buf):
            return buf[:].rearrange("p (k c) -> p k c", k=8)

        wslot_i = [0]
        stg_i = [0]
        cast_i = [0]
        r_wstream = [Res("wstream%d" % i) for i in range(NSLOT)]

        pend_store = [None]

        def flush_store():
            if pend_store[0] is not None:
                pb_, pres, ps_ = pend_store[0]
                pend_store[0] = None
                S.dma(lambda e, b=pb_, s=ps_: e.dma_start(out=wstream[s], in_=b[:]), pres, reads=[pres], writes=[r_wstream[ps_]])

        def fetch_slot(t, s):
            i = wslot_i[0] % 4
            wslot_i[0] += 1
            buf, res = wring[i], r_wring[i]
            if t == 0:
                pieces = []
                for (c0, c1, src) in slot_units(s):
                    ww = c1 - c0
                    nk = min(8, 1024 // ww)
                    for kq in range(0, 8, nk):
                        pieces.append((c0, ww, kq, nk, src[:, kq:kq + nk, :]))
                last_src = [None, None]
                for (c0, ww, kq, nk, src) in pieces:
                    key = str(src)
                    if last_src[0] == key:
                        stg, rstg = last_src[1]
                    else:
                        i_st = stg_i[0] % 4
                        stg_i[0] += 1
                        stg, rstg = xset[1][i_st], r_xset[1][i_st]
                        sv = stg[:, 0:nk * ww].rearrange("p (k c) -> p k c", k=nk)
                        S.dma(lambda e, sv=sv, src=src: e.dma_start(out=sv, in_=src), rstg, writes=[rstg])
                        last_src[0], last_src[1] = key, (stg, rstg)
                    sv = stg[:, 0:nk * ww].rearrange("p (k c) -> p k c", k=nk)
                    dv = kview(buf)[:, kq:kq + nk, c0:c0 + ww]
                    if cast_i[0] % 2 == 0:
                        S.op("act", lambda e, dv=dv, sv=sv: e.activation(out=dv, in_=sv, func=AF.Copy), reads=[rstg], writes=[res])
                    else:
                        S.op("dve", lambda e, dv=dv, sv=sv: e.tensor_copy(out=dv, in_=sv), reads=[rstg], writes=[res])
                    cast_i[0] += 1
                prev = pend_store[0]
                pend_store[0] = (buf, res, s)
                if prev is not None:
                    pb_, pres, ps_ = prev
                    S.dma(lambda e, b=pb_, s=ps_: e.dma_start(out=wstream[s], in_=b[:]), pres, reads=[pres], writes=[r_wstream[ps_]])
            else:
                flush_store()
                S.dma(lambda e, b=buf, s=s: e.dma_start(out=b[:], in_=wstream[s]), res, reads=[r_wstream[s]], writes=[res])
            return buf, res

        def unit(buf, j):
            return kview(buf)[:, :, j * 128:(j + 1) * 128]

        def mm_group(out_ap, lhs_fn, rhs_fn, nk, reads, writes):
            def fn(e, out_ap=out_ap, lhs_fn=lhs_fn, rhs_fn=rhs_fn, nk=nk):
                ins = None
                for k in range(nk):
                    ins = e.matmul(out_ap, lhsT=lhs_fn(k), rhs=rhs_fn(k), start=(k == 0), stop=(k == nk - 1))
                return ins
            S.op("pe", fn, reads=reads, writes=writes)

        def load_x(t):
            xs, rx = xset[t % 2], r_xset[t % 2]
            for b in range(4):
                r0 = t * T + b * 128
                S.dma(lambda e, d=xs[b], r0=r0: e.dma_start(out=d[:], in_=x[r0:r0 + 128, :]), rx[b], writes=[rx[b]])
            pos0 = (t % TPS) * T
            i = t % 2
            S.dma(lambda e, i=i, p=pos0: e.dma_start(out=ropeC[i][:], in_=c_ropeC[:, p:p + T]), r_rope[i], writes=[r_rope[i]])
            S.dma(lambda e, i=i, p=pos0: e.dma_start(out=ropeS[i][:], in_=c_ropeS[:, p:p + T]), r_rope[i], writes=[r_rope[i]])

        def rstd_ops(sidx, rs):
            c0 = 4 * sidx
            S.op("pool", lambda e, c0=c0: e.tensor_scalar(out=stat[:, c0 + 1:c0 + 2], in0=stat[:, c0:c0 + 1], scalar1=1.0 / D,
                                                          scalar2=EPS, op0=ALU.mult, op1=ALU.add), reads=[rs], writes=[rs])
            S.op("pool", lambda e, c0=c0: e.tensor_tensor(out=stat[:, c0 + 1:c0 + 2], in0=stat[:, c0 + 1:c0 + 2], in1=mhalf[:],
                                                          op=ALU.pow), reads=[rs, r_mhalf], writes=[rs])

        stat_i = [0]

        def stage_A(t, blocks=(0, 1, 2, 3), part="both"):
            hTt, rh = hT[t % 2], r_hT[t % 2]
            xs, rx = xset[t % 2], r_xset[t % 2]
            for b in blocks:
                xb_, rxb = xnb[b % 2], r_xnb[b % 2]
                if part in ("both", "tr"):
                    def tr(e, xb_=xb_):
                        ins = None
                        for c in range(8):
                            ins = e.transpose(psT[:, c * 128:(c + 1) * 128], xb_[:, c * 128:(c + 1) * 128], ident[:])
                        return ins
                if part == "tr":
                    S.op("pe", tr, reads=[rxb, r_ident], writes=[r_psT])
                    S.op("act", lambda e, hTt=hTt, b=b: e.activation(
                        out=hTt[:, :, b * 128:(b + 1) * 128], in_=psT[:].rearrange("p (c m) -> p c m", c=8), func=AF.Copy),
                        reads=[r_psT], writes=[rh[b]])
                    continue
                si = stat_i[0] % 4
                stat_i[0] += 1
                rs = r_stat[si]
                bg, rbg = big[b % 2], r_big[b % 2]
                S.op("act", lambda e, bg=bg, xb=xs[b]: e.activation(out=bg[:], in_=xb[:], func=AF.Square), reads=[rx[b]], writes=[rbg])
                S.op("dve", lambda e, bg=bg, si=si: e.tensor_reduce(out=stat[:, 4 * si:4 * si + 1], in_=bg[:], axis=AX.X, op=ALU.add),
                     reads=[rbg], writes=[rs])
                rstd_ops(si, rs)
                S.op("dve", lambda e, xb_=xb_, xb=xs[b], si=si: e.scalar_tensor_tensor(
                    out=xb_[:], in0=xb[:], scalar=stat[:, 4 * si + 1:4 * si + 2], in1=grep[:], op0=ALU.mult, op1=ALU.mult),
                    reads=[rx[b], rs, r_grep], writes=[rxb])
                if part == "elem":
                    continue
                S.op("pe", tr, reads=[rxb, r_ident], writes=[r_psT])
                S.op("act", lambda e, hTt=hTt, b=b: e.activation(
                    out=hTt[:, :, b * 128:(b + 1) * 128], in_=psT[:].rearrange("p (c m) -> p c m", c=8), func=AF.Copy),
                    reads=[r_psT], writes=[rh[b]])

        def make_B(t):
            hTt, rh = hT[t % 2], r_hT[t % 2]
            first = (t % TPS == 0)
            bstate = {}

            def front(n, buf, rbuf, j0):
                wu, wg = unit(buf, j0), unit(buf, 2 + j0)
                bu, rbu = psr.get()
                mm_group(pbank(bu), lambda k, wu=wu: wu[:, k, :], lambda k: hTt[:, k, :], 8, reads=[rbuf] + rh, writes=[rbu])
                bgp, rbg_ = psr.get()
                mm_group(pbank(bgp), lambda k, wg=wg: wg[:, k, :], lambda k: hTt[:, k, :], 8, reads=[rbuf] + rh, writes=[rbg_])
                ue, rue = wk_ue.get()
                if first:
                    S.op("dve", lambda e, ue=ue: e.memset(ue[:, 0:3], 0.0), writes=[rue])
                else:
                    S.op("pool", lambda e, ue=ue, n=n: e.tensor_copy(out=ue[:, 0:3], in_=halo[:, n, 0:3]),
                         reads=[r_halo[n]], writes=[rue])
                S.op("act", lambda e, ue=ue, bu=bu: e.activation(out=ue[:, 3:3 + T], in_=pbank(bu), func=AF.Copy),
                     reads=[rbu, rue], writes=[rue])
                S.op("pool", lambda e, ue=ue, n=n: e.tensor_copy(out=halo[:, n, 0:3], in_=ue[:, T:T + 3]),
                     reads=[rue], writes=[r_halo[n]])
                sg, rsg = wk_sg.get()
                S.op("act", lambda e, sg=sg, bgp=bgp: e.activation(out=sg[:, 0:T], in_=pbank(bgp), func=AF.Tanh, scale=0.5),
                     reads=[rbg_], writes=[rsg])
                S.op("dve", lambda e, sg=sg, bgp=bgp: e.scalar_tensor_tensor(
                    out=sg[:, 0:T], in0=sg[:, 0:T], scalar=1.0, in1=pbank(bgp), op0=ALU.add, op1=ALU.mult),
                    reads=[rsg, rbg_], writes=[rsg])
                uc, ruc = wk_uc.get()
                S.op("pool", lambda e, ue=ue, uc=uc, n=n: e.tensor_scalar(
                    out=uc[:, 0:T], in0=ue[:, 3:3 + T], scalar1=vT[:, 3, n:n + 1], scalar2=vT[:, 4, n:n + 1], op0=ALU.mult, op1=ALU.add),
                    reads=[rue, r_vT], writes=[ruc])
                for j in (2,):
                    cq, rcq = wk.get()
                    S.op("pool", lambda e, ue=ue, cq=cq, n=n, j=j: e.tensor_scalar(
                        out=cq[:, 0:T], in0=ue[:, j:j + T], scalar1=vT[:, j, n:n + 1], scalar2=0.0, op0=ALU.mult, op1=ALU.add),
                        reads=[rue, r_vT], writes=[rcq])
                    S.op("pool", lambda e, cq=cq, uc=uc: e.tensor_tensor(out=uc[:, 0:T], in0=uc[:, 0:T], in1=cq[:, 0:T], op=ALU.add),
                         reads=[rcq, ruc], writes=[ruc])
                for j in (1, 0):
                    S.op("dve", lambda e, ue=ue, uc=uc, n=n, j=j: e.scalar_tensor_tensor(
                        out=uc[:, 0:T], in0=ue[:, j:j + T], scalar=vT[:, j, n:n + 1], in1=uc[:, 0:T], op0=ALU.mult, op1=ALU.add),
                        reads=[rue, r_vT, ruc], writes=[ruc])
                ucb, rucb = wb_ucb.get()
                return (n, uc, ruc, ucb, rucb, sg, rsg)

            def back(n, uc, ruc, ucb, rucb, sg, rsg):
                br, rbr = psr.get()
                mm_group(pbank(br), lambda k, n=n: lruA[:, n, :], lambda k, ucb=ucb: ucb[:], 1, reads=[r_lruA, rucb], writes=[rbr])
                bi, rbi = psr.get()
                mm_group(pbank(bi), lambda k, n=n: lruX[:, n, :], lambda k, ucb=ucb: ucb[:], 1, reads=[r_lruX, rucb], writes=[rbi])
                tr_, rtr = wk.get()
                S.op("act", lambda e, tr_=tr_, br=br, n=n: e.activation(out=tr_[:, 0:T], in_=pbank(br), func=AF.Tanh, scale=0.5,
                                                                       bias=vd[:, 0, n:n + 1]), reads=[rbr, r_vd], writes=[rtr])
                iu, riu = wk.get()
                S.op("act", lambda e, iu=iu, bi=bi, n=n: e.activation(out=iu[:, 0:T], in_=pbank(bi), func=AF.Tanh, scale=0.5,
                                                                     bias=vd[:, 1, n:n + 1]), reads=[rbi, r_vd], writes=[riu])
                a_, ra = wk.get()
                S.op("act", lambda e, a_=a_, tr_=tr_, n=n: e.activation(out=a_[:, 0:T], in_=tr_[:, 0:T], func=AF.Exp,
                                                                       scale=vd[:, 2, n:n + 1], bias=vd[:, 2, n:n + 1]),
                     reads=[rtr, r_vd], writes=[ra])
                s_, rs_ = wk.get()
                S.op("act", lambda e, s_=s_, tr_=tr_, n=n: e.activation(out=s_[:, 0:T], in_=tr_[:, 0:T], func=AF.Exp,
                                                                       scale=vd[:, 3, n:n + 1], bias=vd[:, 3, n:n + 1]),
                     reads=[rtr, r_vd], writes=[rs_])
                S.op("act", lambda e, s_=s_: e.activation(out=s_[:, 0:T], in_=s_[:, 0:T], func=AF.Sqrt, scale=-1.0, bias=1.0),
                     reads=[rs_], writes=[rs_])
                S.op("dve", lambda e, iu=iu, uc=uc: e.scalar_tensor_tensor(out=iu[:, 0:T], in0=iu[:, 0:T], scalar=1.0, in1=uc[:, 0:T],
                                                                          op0=ALU.add, op1=ALU.mult), reads=[riu, ruc], writes=[riu])
                S.op("dve", lambda e, iu=iu, s_=s_: e.scalar_tensor_tensor(out=iu[:, 0:T], in0=s_[:, 0:T], scalar=0.5, in1=iu[:, 0:T],
                                                                          op0=ALU.mult, op1=ALU.mult), reads=[riu, rs_], writes=[riu])
                h_, rh_ = wk.get()
                if first:
                    S.op("dve", lambda e, h_=h_, a_=a_, iu=iu: e.tensor_tensor_scan(
                        out=h_[:, 0:T], data0=a_[:, 0:T], data1=iu[:, 0:T], initial=0.0, op0=ALU.mult, op1=ALU.add),
                        reads=[ra, riu], writes=[rh_])
                else:
                    S.op("dve", lambda e, h_=h_, a_=a_, iu=iu, n=n: e.tensor_tensor_scan(
                        out=h_[:, 0:T], data0=a_[:, 0:T], data1=iu[:, 0:T], initial=hst[:, n:n + 1], op0=ALU.mult, op1=ALU.add),
                        reads=[ra, riu, r_hst[n]], writes=[rh_])
                S.op("pool", lambda e, h_=h_, n=n: e.tensor_copy(out=hst[:, n:n + 1], in_=h_[:, T - 1:T]),
                     reads=[rh_], writes=[r_hst[n]])
                S.op("dve", lambda e, h_=h_, sg=sg, n=n: e.scalar_tensor_tensor(
                    out=yr[:, n, :], in0=h_[:, 0:T], scalar=0.5, in1=sg[:, 0:T], op0=ALU.mult, op1=ALU.mult),
                    reads=[rh_, rsg], writes=[r_yr[n]])

            def front_n(n):
                if n % 2 == 0:
                    bstate["buf"] = fetch_slot(t, n // 2)
                buf, rbuf = bstate["buf"]
                return front(n, buf, rbuf, n % 2)

            def front_b(n, uc, ruc, ucb, rucb, sg, rsg):
                S.op("act", lambda e, uc=uc, ucb=ucb: e.activation(out=ucb[:], in_=uc[:, 0:T], func=AF.Copy), reads=[ruc], writes=[rucb])
            return front_n, front_b, back

        def rope_chain(pb, rpb, dst_ap, rdst, t):
            i = t % 2
            raw, rraw = wb.get()
            S.op("act", lambda e, raw=raw, pb=pb: e.activation(out=raw[:], in_=pbank(pb), func=AF.Copy), reads=[rpb], writes=[rraw])
            return (raw, rraw, dst_ap, rdst, i)

        def rope_finish(raw, rraw, dst_ap, rdst, i):
            p2, rp2 = psr.get()
            mm_group(pbank(p2), lambda k: permb[:], lambda k, raw=raw: raw[:], 1, reads=[r_perm, rraw], writes=[rp2])
            t2, rt2 = wk.get()
            S.op("pool", lambda e, t2=t2, raw=raw, i=i: e.tensor_tensor(out=t2[:, 0:T], in0=raw[:], in1=ropeC[i][:], op=ALU.mult),
                 reads=[rraw, r_rope[i]], writes=[rt2])
            t1, rt1 = wk.get()
            S.op("dve", lambda e, t1=t1, p2=p2, i=i: e.tensor_tensor(out=t1[:, 0:T], in0=pbank(p2), in1=ropeS[i][:], op=ALU.mult),
                 reads=[rp2, r_rope[i]], writes=[rt1])
            S.op("dve", lambda e, t1=t1, t2=t2, dst_ap=dst_ap: e.tensor_tensor(out=dst_ap, in0=t1[:, 0:T], in1=t2[:, 0:T], op=ALU.add),
                 reads=[rt1, rt2], writes=[rdst])

        def make_C(t):
            hTt, rh = hT[t % 2], r_hT[t % 2]
            units = [("kd", g) for g in range(4)] + [("v", 0), ("v", 1)] + [("q", c) for c in range(8)] + \
                    [("ga", c) for c in range(8)] + [("pad", 0), ("pad", 1)]
            pend = []
            cstate = {}

            def unit_fn(ui):
                kind, i = units[ui]
                j = ui % 4
                if j == 0:
                    cstate["buf"] = fetch_slot(t, 4 + ui // 4)
                buf, rbuf = cstate["buf"]
                if kind in ("kd", "q"):
                    w = unit(buf, j)
                    pb, rpb = psr.get()
                    mm_group(pbank(pb), lambda k, w=w: w[:, k, :], lambda k: hTt[:, k, :], 8, reads=[rbuf] + rh, writes=[rpb])
                    if kind == "kd":
                        dst, rd = kT[:, i, 128:128 + T], r_kT[i]
                    else:
                        dst, rd = qmg[:, i, :], r_qmg[i]
                    pend.append(rope_chain(pb, rpb, dst, rd, t))
                    if len(pend) > 2:
                        rope_finish(*pend.pop(0))
                elif kind == "v" and i == 0:
                    wv = kview(buf)[:, :, j * 128:(j + 2) * 128]
                    for pr in range(2):
                        pb, rpb = psr.get()

                        def fn(e, pb=pb, pr=pr, wv=wv):
                            ins = None
                            for bb in range(2):
                                b = pr * 2 + bb
                                for k in range(8):
                                    ins = e.matmul(ps[:, pb, bb * 256:(bb + 1) * 256], lhsT=hTt[:, k, b * 128:(b + 1) * 128],
                                                   rhs=wv[:, k, :], start=(k == 0), stop=(k == 7))
                            return ins
                        S.op("pe", fn, reads=[rbuf] + rh, writes=[rpb])
                        S.op("act", lambda e, pb=pb, pr=pr: e.activation(
                            out=vtok[:, 1 + 2 * pr:3 + 2 * pr, :], in_=pbank(pb).rearrange("p (b m) -> p b m", b=2), func=AF.Copy),
                            reads=[rpb], writes=[r_vtok])
                elif kind == "ga":
                    w = unit(buf, j)
                    pb, rpb = psr.get()
                    mm_group(pbank(pb), lambda k, w=w: w[:, k, :], lambda k: hTt[:, k, :], 8, reads=[rbuf] + rh, writes=[rpb])
                    tg, rtg = wk.get()
                    S.op("act", lambda e, tg=tg, pb=pb: e.activation(out=tg[:, 0:T], in_=pbank(pb), func=AF.Tanh, scale=0.5),
                         reads=[rpb], writes=[rtg])
                    S.op("dve", lambda e, tg=tg, pb=pb, i=i: e.scalar_tensor_tensor(
                        out=sga[:, i, :], in0=tg[:, 0:T], scalar=1.0, in1=pbank(pb), op0=ALU.add, op1=ALU.mult),
                        reads=[rtg, rpb], writes=[r_sga[i]])

            def flush():
                while pend:
                    rope_finish(*pend.pop(0))
            return unit_fn, flush, len(units)

        def stage_BC(t, dsteps=None):
            frontB, frontB2, backB = make_B(t)
            unitC, flushC, NU = make_C(t)
            pendB = []
            ui = 0
            NP = 10
            for p_ in range(NP):
                if p_ < 8:
                    pendB.append(frontB(p_))
                if p_ >= 2:
                    backB(*pendB.pop(0))
                if p_ < 8:
                    frontB2(*pendB[-1])
                for _ in range(3):
                    if ui < NU:
                        unitC(ui)
                        ui += 1
                        if ui == NU:
                            flushC()
                if ui == NU and dsteps is not None and p_ >= 8:
                    psr.items = [(i, r_ps[i]) for i in (0, 1, 5, 6)]
                    for _ in range(5):
                        if dsteps:
                            dsteps.pop(0)()
            psr.items = [(i, r_ps[i]) for i in range(7)]

        def stage_D(t):
            ts = t % TPS
            S_BANKS = [(0, 0), (5, 0)]
            sring = Ring(S_BANKS)
            DEN, VALS = 2, 3
            rbv = psT[:].bitcast(F32)
            post2_pending = [None]
            steps = []
            for jq in range(4):
                first = (ts == 0 and jq == 0)
                lo = 128 if first else 0
                pend = []

                def s1(i, jq=jq, first=first, lo=lo):
                    c, g = i, i // 2
                    bk, _hf = sring.get()
                    rbk = [r_ps[bk], r_ps[bk + 1]]

                    def fn(e, bk=bk, g=g, c=c):
                        ins = None
                        for hh in range(2):
                            hb = hh * 64
                            if not first:
                                ins = e.matmul(ps[:, bk + hh, 0:128], lhsT=kT[hb:hb + 64, g, jq * 128:(jq + 1) * 128],
                                               rhs=qmg[hb:hb + 64, c, jq * 128:(jq + 1) * 128], start=True, stop=True)
                            ins = e.matmul(ps[:, bk + hh, 128:256], lhsT=kT[hb:hb + 64, g, 128 + jq * 128:128 + (jq + 1) * 128],
                                           rhs=qmg[hb:hb + 64, c, jq * 128:(jq + 1) * 128], start=True, stop=True)
                        return ins
                    S.op("pe", fn, reads=[r_kT[g], r_qmg[c]], writes=[rbk])
                    pt, rpt = pTr.get()
                    S.op("act", lambda e, pt=pt, bk=bk: e.activation(
                        out=pt[:, :, lo:256], in_=ps[:, bk:bk + 2, lo:256], func=AF.Exp, scale=0.125),
                        reads=[rbk], writes=[rpt])
                    S.op("dve", lambda e, pt=pt: e.tensor_tensor(out=pt[:, :, lo:256], in0=pt[:, :, lo:256], in1=maskb[:, :, lo:256], op=ALU.mult),
                         reads=[rpt, r_mask], writes=[rpt])
                    return (i, pt, rpt)

                def s4(i, pt, rpt, jq=jq, first=first):
                    c, g = i, i // 2

                    def fn(e, pt=pt, c=c, g=g):
                        ins = None
                        for hh in range(2):
                            h = 2 * c + hh
                            hb = hh * 64
                            vo = ps[hb:hb + 64, VALS + c // 4, (c % 4) * 128:(c % 4 + 1) * 128]
                            if not first:
                                e.matmul(vo, lhsT=vtok[:, jq, g * 64:(g + 1) * 64], rhs=pt[:, hh, 0:128], start=True, stop=False)
                            e.matmul(vo, lhsT=vtok[:, jq + 1, g * 64:(g + 1) * 64], rhs=pt[:, hh, 128:256], start=first, stop=True)
                            if not first:
                                ins = e.matmul(ps[0:16, DEN, 0:256], lhsT=ehb[:, h * 16:(h + 1) * 16], rhs=pt[:, hh, 0:256], start=(h == 0),
                                               stop=(h == 15))
                            else:
                                ins = e.matmul(ps[0:16, DEN, 128:256], lhsT=ehb[:, h * 16:(h + 1) * 16], rhs=pt[:, hh, 128:256],
                                               start=(h == 0), stop=(h == 15))
                        return ins
                    S.op("pe", fn, reads=[rpt, r_vtok, r_eh], writes=[r_ps[VALS], r_ps[VALS + 1], r_ps[DEN]])

                def post1(jq=jq, first=first):
                    di = jq % 2
                    bg, rbg = big[0], r_big[0]
                    S.op("act", lambda e, bg=bg: e.activation(out=bg[:].rearrange("p (a m) -> p a m", a=2), in_=ps[:, VALS:VALS + 2, :],
                                                              func=AF.Copy), reads=[r_ps[VALS], r_ps[VALS + 1]], writes=[rbg])
                    S.op("dve", lambda e, di=di: e.tensor_scalar(out=dsm[:, di, :], in0=ps[0:16, DEN, 128:256], scalar1=sinkt[:, 1:2],
                                                                  scalar2=None, op0=ALU.add), reads=[r_ps[DEN], r_sink], writes=[r_dsm[di]])
                    if not first:
                        S.op("dve", lambda e, di=di: e.tensor_tensor(out=dsm[:, di, :], in0=dsm[:, di, :], in1=ps[0:16, DEN, 0:128],
                                                                      op=ALU.add), reads=[r_ps[DEN], r_dsm[di]], writes=[r_dsm[di]])
                    S.op("dve", lambda e, di=di: e.reciprocal(out=dsm[:, di, :], in_=dsm[:, di, :]), reads=[r_dsm[di]], writes=[r_dsm[di]])

                def post2(jq=jq):
                    di = jq % 2
                    bg, rbg = big[0], r_big[0]
                    b1, rb1 = big[1], r_big[1]
                    for hf in range(2):
                        def fnb(e, di=di, hf=hf):
                            ins = None
                            for cc in range(4):
                                c = hf * 4 + cc
                                ins = e.matmul(rbv[:, cc * 128:(cc + 1) * 128], lhsT=bcs[:, c * 128:(c + 1) * 128],
                                               rhs=dsm[:, di, :], start=True, stop=True)
                            return ins
                        S.op("pe", fnb, reads=[r_bc, r_dsm[di]], writes=[r_psT])
                        S.op("dve", lambda e, bg=bg, b1=b1, hf=hf: e.tensor_tensor(
                            out=b1[:, hf * 512:(hf + 1) * 512], in0=rbv, in1=bg[:, hf * 512:(hf + 1) * 512], op=ALU.mult),
                            reads=[r_psT, rbg], writes=[rb1])
                    S.op("dve", lambda e, b1=b1, jq=jq: e.scalar_tensor_tensor(
                        out=ya[:, :, jq * 128:(jq + 1) * 128], in0=b1[:].rearrange("p (c m) -> p c m", c=8), scalar=0.5,
                        in1=sga[:, :, jq * 128:(jq + 1) * 128], op0=ALU.mult, op1=ALU.mult),
                        reads=[rb1] + r_sga, writes=[r_ya[jq]])

                SK = 1

                def hstep(i, s1=s1, s4=s4, pend=pend):
                    if i < 8:
                        pend.append(s1(i))
                    if i >= SK:
                        s4(*pend.pop(0))
                    if i == 3 and post2_pending[0] is not None:
                        post2_pending[0]()
                        post2_pending[0] = None
                for i in range(8 + SK):
                    steps.append(lambda i=i, hstep=hstep: hstep(i))

                def pstep(post1=post1, post2=post2):
                    post1()
                    post2_pending[0] = post2
                steps.append(pstep)

            def fin():
                post2_pending[0]()
                S.op("pool", lambda e: e.tensor_copy(out=kT[:, :, 0:128], in_=kT[:, :, T:T + 128]), reads=r_kT, writes=r_kT)
                S.op("pool", lambda e: e.tensor_copy(out=vtok[:, 0, :], in_=vtok[:, 4, :]), reads=[r_vtok], writes=[r_vtok])
            steps.append(fin)
            return steps

        def stage_E(t, a_next=None):
            hTt, rh = hT[t % 2], r_hT[t % 2]
            ebuf = {}
            for f in range(8):
                if a_next is not None:
                    stage_A(a_next, blocks=(f // 2,), part=("elem" if f % 2 == 0 else "tr"))
                if f % 2 == 0:
                    ebuf["x"] = fetch_slot(t, 10 + f)
                    ebuf["y"] = fetch_slot(t, 11 + f)
                (bx, rbx), (by, rby) = ebuf["x"], ebuf["y"]
                rbuf = [rbx, rby]
                wro, wao, wmr, wma = unit(bx, f % 2), unit(bx, 2 + f % 2), unit(by, f % 2), unit(by, 2 + f % 2)
                pc, rpc = psr.get()
                mm_group(pbank(pc), lambda k, w=wmr: w[:, k, :], lambda k: hTt[:, k, :], 8, reads=[rbuf] + rh, writes=[rpc])
                pd, rpd = psr.get()
                mm_group(pbank(pd), lambda k, w=wma: w[:, k, :], lambda k: hTt[:, k, :], 8, reads=[rbuf] + rh, writes=[rpd])
                pa, rpa = psr.get()
                mm_group(pbank(pa), lambda k, w=wro: w[:, k, :], lambda k: yr[:, k, :], 8, reads=[rbuf] + r_yr, writes=[rpa])
                pb, rpb = psr.get()
                mm_group(pbank(pb), lambda k, w=wao: w[:, k, :], lambda k: ya[:, k, :], 8, reads=[rbuf] + r_ya, writes=[rpb])
                tc_, rtc = wk.get()
                S.op("act", lambda e, tc_=tc_, pc=pc: e.activation(out=tc_[:, 0:T], in_=pbank(pc), func=AF.Tanh, scale=0.5),
                     reads=[rpc], writes=[rtc])
                td_, rtd = wk.get()
                S.op("act", lambda e, td_=td_, pd=pd: e.activation(out=td_[:, 0:T], in_=pbank(pd), func=AF.Tanh, scale=0.5),
                     reads=[rpd], writes=[rtd])
                S.op("dve", lambda e, tc_=tc_, pa=pa: e.scalar_tensor_tensor(out=tc_[:, 0:T], in0=tc_[:, 0:T], scalar=1.0, in1=pbank(pa),
                                                                            op0=ALU.add, op1=ALU.mult), reads=[rtc, rpa], writes=[rtc])
                S.op("dve", lambda e, td_=td_, pb=pb: e.scalar_tensor_tensor(out=td_[:, 0:T], in0=td_[:, 0:T], scalar=1.0, in1=pbank(pb),
                                                                            op0=ALU.add, op1=ALU.mult), reads=[rtd, rpb], writes=[rtd])
                S.op("pool", lambda e, tc_=tc_, td_=td_, f=f: e.tensor_tensor(out=qmg[:, f, :], in0=tc_[:, 0:T], in1=td_[:, 0:T], op=ALU.add),
                     reads=[rtc, rtd], writes=[r_qmg[f]])

        def stage_F(t):
            xs, rx = xset[t % 2], r_xset[t % 2]
            bufs = [fetch_slot(t, 18), fetch_slot(t, 19)]
            pairs = [(0, 1), (2, 3), (4, 5)]
            for b in range(4):
                p0, p1 = pairs[b % 3]

                def fn(e, b=b, p0=p0):
                    ins = None
                    for half in range(2):
                        w = bufs[half][0][:].rearrange("p (k m) -> p k m", k=8)
                        for k in range(8):
                            ins = e.matmul(ps[:, p0 + half, :], lhsT=qmg[:, k, b * 128:(b + 1) * 128], rhs=w[:, k, :],
                                           start=(k == 0), stop=(k == 7))
                    return ins
                S.op("pe", fn, reads=[bufs[0][1], bufs[1][1]] + r_qmg, writes=[r_ps[p0], r_ps[p1]])
                xb = xs[b]
                xv = xb[:].rearrange("p (a m) -> p a m", a=2)
                S.op("dve", lambda e, xv=xv, p0=p0: e.scalar_tensor_tensor(out=xv, in0=ps[:, p0:p0 + 2, :], scalar=0.5, in1=xv,
                                                                          op0=ALU.mult, op1=ALU.add),
                     reads=[r_ps[p0], r_ps[p1], rx[b]], writes=[rx[b]])
                si = stat_i[0] % 4
                stat_i[0] += 1
                rs = r_stat[si]
                bg, rbg = big[b % 2], r_big[b % 2]
                S.op("act", lambda e, bg=bg, xb=xb: e.activation(out=bg[:], in_=xb[:], func=AF.Square), reads=[rx[b]], writes=[rbg])
                S.op("dve", lambda e, bg=bg, si=si: e.tensor_reduce(out=stat[:, 4 * si:4 * si + 1], in_=bg[:], axis=AX.X, op=ALU.add),
                     reads=[rbg], writes=[rs])
                rstd_ops(si, rs)
                S.op("dve", lambda e, xb=xb, si=si: e.scalar_tensor_tensor(
                    out=xb[:], in0=xb[:], scalar=stat[:, 4 * si + 1:4 * si + 2], in1=fgrep[:], op0=ALU.mult, op1=ALU.mult),
                    reads=[rx[b], rs, r_fgrep], writes=[rx[b]])
                r0 = t * T + b * 128
                S.dma(lambda e, xb=xb, r0=r0: e.dma_start(out=out[r0:r0 + 128, :], in_=xb[:]), rx[b], reads=[rx[b]], qeng="pool", final=True)

        load_x(0)
        stage_A(0)
        stop = tuple(stop_after) if stop_after is not None else None
        for t in range(ntiles):
            more = (t + 1 < ntiles)
            dsteps = stage_D(t)
            if stop == ("B", t) or stop == ("C", t):
                stage_BC(t)
                break
            stage_BC(t, dsteps)
            if more and t > 0:
                load_x(t + 1)
            while dsteps:
                dsteps.pop(0)()
            if stop == ("D", t):
                break
            stage_E(t, a_next=(t + 1 if (more and t > 0) else None))
            if stop == ("E", t):
                break
            stage_F(t)
            if stop == ("F", t):
                break
            if more and t == 0:
                load_x(1)
                stage_A(1)
        dumpable = {
            "hT0": (hT[0], r_hT[0], [128, 8, T], BF16), "yr": (yr, r_yr, [128, 8, T], BF16), "ya": (ya, r_ya, [128, 8, T], BF16),
            "qmg": (qmg, r_qmg, [128, 8, T], BF16), "sga": (sga, r_sga, [128, 8, T], BF16), "kT": (kT, r_kT, [128, 4, 128 + T], BF16),
            "vtok": (vtok, [r_vtok], [128, 5, 256], BF16), "vT": (vT, [r_vT], [128, 8, 8], F32), "vd": (vd, [r_vd], [128, 4, 8], F32),
            "x00": (xset[0][0], [r_xset[0][0]], [128, D], F32),
        }
        for name in dump:
            tens, rl, shape, dt = dumpable[name]
            dd = nc.dram_tensor("dbg_" + name, shape, dt, kind="ExternalOutput").ap()
            S.dma(lambda e, dd=dd, tens=tens: e.dma_start(out=dd, in_=tens[:]), rl[0], reads=rl, qeng="pool", final=True)
        flush_store()
        S.finish("pool")
        S.run_block()
    return nc


def make_in_maps(inputs):
    consts = _consts()
    maps = []
    xs = np.ascontiguousarray(inputs["x"], dtype=np.float32)
    for c in range(NCORES):
        m = {"x": xs[2 * c:2 * c + 2].reshape(2 * SEQ, D)}
        for k in ("norm_g", "w_in", "conv_w", "conv_b", "lru_w_a", "lru_b_a", "lru_w_x", "lru_b_x", "lru_lambda",
                  "attn_sinks", "w_rnn_out", "w_attn_out", "w_o"):
            m[k] = np.ascontiguousarray(inputs[k], dtype=np.float32)
        m["final_norm_g"] = np.ascontiguousarray(inputs["final_norm_g"], dtype=np.float32).reshape(1, D)
        m.update(consts)
        maps.append(m)
    return maps


def kernel(**inputs):
    nc = build_program()
    in_maps = make_in_maps(inputs)
    res = run_bass_kernel_spmd(nc, in_maps, core_ids=list(range(NCORES)))
    outs = [np.asarray(r["out"], dtype=np.float32).reshape(2, SEQ, D) for r in res.results]
    return np.concatenate(outs, axis=0)
```
